# Optimizing a Trainium2 kernel written in Bass

```python
import jax, jax.numpy as jnp
from jax import lax
import numpy as np

D_MODEL = 1024
BATCH = 2
SEQ = 8192
DEPTH = 1

GRID_W = 64
CTX_LEN = 256
CHUNK = 128
GMLP_HEADS = 4
GMLP_DH = 128
GMLP_W = GMLP_HEADS * GMLP_DH
MLSTM_HEADS = 4
MLSTM_DH = 128
MLSTM_W = MLSTM_HEADS * MLSTM_DH
MIX_W = GMLP_W + MLSTM_W
N_DIR = 2
N_GATE = 2 * N_DIR * MLSTM_HEADS
N_IN = 2 * GMLP_W + 4 * MLSTM_W + N_GATE
NORM_HEAD = 128
D_FF = 2688
CONV_W = 3
ALPHA = (2 * DEPTH) ** 0.25
BETA = (8 * DEPTH) ** -0.25
EPS = 1e-5

kernel_name = "hymba_gmlp_mlstm_convffn_deepnorm_prefix"


def _normalize(xf):
    mu = jnp.mean(xf, -1, keepdims=True)
    var = jnp.mean(jnp.square(xf - mu), -1, keepdims=True)
    return (xf - mu) * lax.rsqrt(var + EPS)


def layer_norm(x, g, b):
    return (_normalize(x.astype(jnp.float32)) * g + b).astype(x.dtype)


def head_norm(y, g):
    B, L, W = y.shape
    yf = _normalize(y.astype(jnp.float32).reshape(B, L, W // NORM_HEAD, NORM_HEAD))
    return (yf.reshape(B, L, W) * g).astype(y.dtype)


def dwconv3(x, w, b, axis):
    L = x.shape[axis]
    pad = [(0, 0)] * x.ndim
    pad[axis] = (1, 1)
    xp = jnp.pad(x, pad)
    sl = lambda s: lax.slice_in_dim(xp, s, s + L, axis=axis)
    return sl(0) * w[0] + sl(1) * w[1] + sl(2) * w[2] + b


def gmlp_mix(u, v, ln_g, w_s, b_s):
    B, L, _ = u.shape
    u = jax.nn.gelu(u)
    vh = jax.nn.gelu(v).reshape(B, L // CHUNK, CHUNK, GMLP_HEADS, GMLP_DH)
    vh = (_normalize(vh.astype(jnp.float32)) * ln_g).astype(u.dtype)
    mixed = jnp.einsum('gts,bnsgc->bntgc', w_s, vh) + b_s.T[None, None, :, :, None]
    return u * mixed.reshape(B, L, GMLP_W)


def mlstm_chunkwise(q, k, v, log_i, log_f, state):
    B, H, L, Dh = q.shape
    nc = L // CHUNK
    to_chunks = lambda a: jnp.moveaxis(a.reshape(B, H, nc, CHUNK, *a.shape[3:]), 2, 0)
    tril = jnp.tril(jnp.ones((CHUNK, CHUNK), bool))

    def step(carry, inp):
        C, n, m = carry
        qc, kc, vc, li, lf = inp
        b = jnp.cumsum(lf, axis=-1)
        dmat = b[..., :, None] - b[..., None, :] + li[..., None, :]
        dmat = jnp.where(tril, dmat, -jnp.inf)
        inter = b + m[..., None]
        m_t = jnp.maximum(inter, dmat.max(-1))
        w_intra = jnp.exp(dmat - m_t[..., None])
        w_state = jnp.exp(inter - m_t)
        s = jnp.einsum('bhtd,bhsd->bhts', qc, kc) * w_intra
        num = jnp.einsum('bhts,bhse->bhte', s, vc) + w_state[..., None] * jnp.einsum('bhtd,bhde->bhte', qc, C)
        den = s.sum(-1) + w_state * jnp.einsum('bhtd,bhd->bht', qc, n)
        h = num / jnp.maximum(jnp.abs(den), jnp.exp(-m_t))[..., None]
        b_last = b[..., -1]
        g = b_last[..., None] - b + li
        m_new = jnp.maximum(b_last + m, g.max(-1))
        w_old = jnp.exp(b_last + m - m_new)
        w_k = jnp.exp(g - m_new[..., None])
        C = w_old[..., None, None] * C + jnp.einsum('bhs,bhsd,bhse->bhde', w_k, kc, vc)
        n = w_old[..., None] * n + jnp.einsum('bhs,bhsd->bhd', w_k, kc)
        return (C, n, m_new), h

    state, h = lax.scan(step, state, (to_chunks(q), to_chunks(k), to_chunks(v), to_chunks(log_i), to_chunks(log_f)))
    return jnp.moveaxis(h, 0, 2).reshape(B, H, L, Dh), state


def zero_state(batch):
    H2 = N_DIR * MLSTM_HEADS
    return (jnp.zeros((batch, H2, MLSTM_DH, MLSTM_DH), jnp.float32),
            jnp.zeros((batch, H2, MLSTM_DH), jnp.float32),
            jnp.zeros((batch, H2), jnp.float32))


def mlstm_branch(pm, conv_w, conv_b, b_i, b_f, state):
    B, L, _ = pm.shape
    qk = jax.nn.silu(dwconv3(pm[..., :2 * MLSTM_W], conv_w, conv_b, axis=1))
    q, k = qk[..., :MLSTM_W], qk[..., MLSTM_W:]
    vm = pm[..., 2 * MLSTM_W:3 * MLSTM_W]
    o_pre = pm[..., 3 * MLSTM_W:4 * MLSTM_W]
    gates = pm[..., 4 * MLSTM_W:].astype(jnp.float32).reshape(B, L, 2, N_DIR, MLSTM_HEADS)
    log_i = gates[:, :, 0] + b_i
    log_f = jax.nn.log_sigmoid(gates[:, :, 1] + b_f)

    def heads(a):
        return a.astype(jnp.float32).reshape(B, L, MLSTM_HEADS, MLSTM_DH).transpose(0, 2, 1, 3)

    def both_dirs(a):
        return jnp.concatenate([a, jnp.flip(a, 2)], 1)

    def dir_gates(g):
        g = jnp.moveaxis(g, 1, -1)
        return jnp.concatenate([g[:, 0], jnp.flip(g[:, 1], -1)], 1)

    h, state = mlstm_chunkwise(both_dirs(heads(q)), both_dirs(heads(k)) * (MLSTM_DH ** -0.5),
                               both_dirs(heads(vm)), dir_gates(log_i), dir_gates(log_f), state)
    h = h[:, :MLSTM_HEADS] + jnp.flip(h[:, MLSTM_HEADS:], 2)
    h = h.transpose(0, 2, 1, 3).reshape(B, L, MLSTM_W).astype(pm.dtype)
    return h, o_pre, state


def mixer_out(y_a, h_b, o_pre, norm_g, w_out):
    y = head_norm(jnp.concatenate([y_a, h_b], -1), norm_g)
    y = jnp.concatenate([y[..., :GMLP_W], y[..., GMLP_W:] * jax.nn.sigmoid(o_pre)], -1)
    return jnp.einsum('blw,wd->bld', y, w_out)


def conv_ffn(h, w_up, conv_w, conv_b, w_down, rows):
    a = jnp.einsum('bld,df->blf', h, w_up)
    B, L, F = a.shape
    if rows is None:
        a = dwconv3(a, conv_w, conv_b, axis=1)
    else:
        a = dwconv3(a.reshape(B, rows, GRID_W, F), conv_w, conv_b, axis=2).reshape(B, L, F)
    val, gate = a[..., :D_FF], a[..., D_FF:]
    return jnp.einsum('blf,fd->bld', jax.nn.silu(gate) * val, w_down)


def setup_inputs(seed: int = 0) -> dict:
    key = jax.random.key(seed)
    ks = iter(jax.random.split(key, 32))
    nrm = lambda shape, s: jax.random.normal(next(ks), shape, jnp.float32) * s
    ones_n = lambda shape: 1.0 + nrm(shape, 0.01)
    L = DEPTH
    return {
        "x": nrm((BATCH, SEQ, D_MODEL), 1.0),
        "c": nrm((BATCH, D_MODEL), 1.0),
        "ctx": nrm((BATCH, CTX_LEN, D_MODEL), 1.0),
        "c_ctx": nrm((D_MODEL,), 1.0),
        "w_ada": nrm((L, D_MODEL, 6 * D_MODEL), D_MODEL ** -0.5),
        "b_ada": nrm((L, 6 * D_MODEL), 0.01),
        "w_in": nrm((L, D_MODEL, N_IN), D_MODEL ** -0.5),
        "gmlp_ln_g": ones_n((L, GMLP_HEADS, GMLP_DH)),
        "gmlp_ws": nrm((L, GMLP_HEADS, CHUNK, CHUNK), CHUNK ** -0.5),
        "gmlp_bs": ones_n((L, GMLP_HEADS, CHUNK)),
        "qk_conv_w": nrm((L, CONV_W, 2 * MLSTM_W), CONV_W ** -0.5),
        "qk_conv_b": nrm((L, 2 * MLSTM_W), 0.01),
        "b_igate": nrm((L, N_DIR, MLSTM_HEADS), 0.1),
        "b_fgate": 3.0 + 3.0 * jax.random.uniform(next(ks), (L, N_DIR, MLSTM_HEADS), jnp.float32),
        "mix_norm_g": ones_n((L, MIX_W)),
        "w_out": nrm((L, MIX_W, D_MODEL), BETA * MIX_W ** -0.5),
        "ln1_g": ones_n((L, D_MODEL)),
        "ln1_b": nrm((L, D_MODEL), 0.01),
        "w_up": nrm((L, D_MODEL, 2 * D_FF), D_MODEL ** -0.5),
        "ffn_conv_w": nrm((L, CONV_W, 2 * D_FF), CONV_W ** -0.5),
        "ffn_conv_b": nrm((L, 2 * D_FF), 0.01),
        "w_down": nrm((L, D_FF, D_MODEL), BETA * D_FF ** -0.5),
        "ln2_g": ones_n((L, D_MODEL)),
        "ln2_b": nrm((L, D_MODEL), 0.01),
    }


def reference(x, c, ctx, c_ctx, w_ada, b_ada, w_in, gmlp_ln_g, gmlp_ws, gmlp_bs, qk_conv_w, qk_conv_b,
              b_igate, b_fgate, mix_norm_g, w_out, ln1_g, ln1_b, w_up, ffn_conv_w, ffn_conv_b, w_down,
              ln2_g, ln2_b):
    B, S, _ = x.shape
    rows = S // GRID_W
    for l in range(DEPTH):
        last = l == DEPTH - 1
        mod_x = jnp.einsum('bd,de->be', jax.nn.silu(c), w_ada[l]) + b_ada[l]
        mod_c = jnp.einsum('d,de->e', jax.nn.silu(c_ctx), w_ada[l]) + b_ada[l]
        sh1, sc1, g1, sh2, sc2, g2 = jnp.split(mod_x[:, None, :], 6, axis=-1)
        csh1, csc1, cg1, csh2, csc2, cg2 = jnp.split(mod_c, 6, axis=-1)
        mparams = (qk_conv_w[l], qk_conv_b[l], b_igate[l], b_fgate[l])

        hc = ctx * (1 + csc1) + csh1
        if last:
            pc_m = jnp.einsum('bld,dn->bln', hc, w_in[l][:, 2 * GMLP_W:])
            _, _, ctx_state = mlstm_branch(pc_m, *mparams, zero_state(B))
        else:
            pc = jnp.einsum('bld,dn->bln', hc, w_in[l])
            ya_c = gmlp_mix(pc[..., :GMLP_W], pc[..., GMLP_W:2 * GMLP_W], gmlp_ln_g[l], gmlp_ws[l], gmlp_bs[l])
            hb_c, o_c, ctx_state = mlstm_branch(pc[..., 2 * GMLP_W:], *mparams, zero_state(B))
            ctx = layer_norm(ALPHA * ctx + cg1 * mixer_out(ya_c, hb_c, o_c, mix_norm_g[l], w_out[l]), ln1_g[l], ln1_b[l])
            f_c = conv_ffn(ctx * (1 + csc2) + csh2, w_up[l], ffn_conv_w[l], ffn_conv_b[l], w_down[l], None)
            ctx = layer_norm(ALPHA * ctx + cg2 * f_c, ln2_g[l], ln2_b[l])

        hx = x * (1 + sc1) + sh1
        px = jnp.einsum('bld,dn->bln', hx, w_in[l])
        ya = gmlp_mix(px[..., :GMLP_W], px[..., GMLP_W:2 * GMLP_W], gmlp_ln_g[l], gmlp_ws[l], gmlp_bs[l])
        hb, o_pre, _ = mlstm_branch(px[..., 2 * GMLP_W:], *mparams, ctx_state)
        x = layer_norm(ALPHA * x + g1 * mixer_out(ya, hb, o_pre, mix_norm_g[l], w_out[l]), ln1_g[l], ln1_b[l])
        f_x = conv_ffn(x * (1 + sc2) + sh2, w_up[l], ffn_conv_w[l], ffn_conv_b[l], w_down[l], rows)
        x = layer_norm(ALPHA * x + g2 * f_x, ln2_g[l], ln2_b[l])
    return x
```

```python
import numpy as np
import concourse.bass as bass
import concourse.mybir as mybir
from concourse.bass_utils import run_bass_kernel_spmd

F32 = mybir.dt.float32
BF16 = mybir.dt.bfloat16
AF = mybir.ActivationFunctionType
ALU = mybir.AluOpType

D = 1024
SEQ = 8192
NCORE = 8
TOK = 2048
NT = 16
CTX = 256
N_IN = 3088
DFF = 2688
NFC = 21
ALPHA = 2.0 ** 0.25
EPS = 1e-5
LN_QSCALE = float(np.log(128.0 ** -0.5))
GC = 1.5957691216057308

ENGS = ("pe", "act", "dve", "pool", "sp")


class T:
    __slots__ = ("name", "w", "rs", "dsem", "dcnt")

    def __init__(self, name):
        self.name = name
        self.w = None
        self.rs = []
        self.dsem = None
        self.dcnt = 0


class Op:
    __slots__ = ("eng", "fn", "deps", "is_dma", "sem", "val", "need_inc", "idx")

    def __init__(self, eng, fn, is_dma):
        self.eng = eng
        self.fn = fn
        self.deps = []
        self.is_dma = is_dma
        self.sem = None
        self.val = None
        self.need_inc = False


class Prog:
    def __init__(self, nc):
        self.nc = nc
        self.q = {e: [] for e in ENGS}
        self.esem = {}
        self.final = []
        self.nsem = 0
        self.dma_tiles = {}

    def _newsem(self, name):
        self.nsem += 1
        return self.nc.alloc_semaphore(name)

    def op(self, eng, fn, reads=(), writes=(), dma=None):
        o = Op(eng, fn, dma is not None)
        deps = []
        for t in reads:
            if t.w is not None:
                deps.append(t.w)
        for t in writes:
            if t.w is not None:
                deps.append(t.w)
            deps.extend(t.rs)
        best = {}
        for d in deps:
            if d.is_dma:
                key = ("dma", d.sem.num)
                if key not in best or d.val > best[key].val:
                    best[key] = d
            else:
                if eng == "pe" and d.eng == "pe":
                    continue
                key = ("eng", d.eng)
                if key not in best or d.idx > best[key].idx:
                    best[key] = d
        for d in best.values():
            o.deps.append(d)
            d.need_inc = True
        if dma is not None:
            if dma.dsem is None:
                dma.dsem = self._newsem("d_" + dma.name)
            dma.dcnt += 16
            self.dma_tiles[id(dma)] = dma
            o.sem = dma.dsem
            o.val = dma.dcnt
            o.need_inc = True
        for t in reads:
            t.rs.append(o)
        for t in writes:
            t.w = o
            t.rs = []
        o.idx = len(self.q[eng])
        self.q[eng].append(o)
        return o

    def emit(self):
        nc = self.nc
        for e in ENGS:
            cnt = 0
            for o in self.q[e]:
                if o.is_dma or not o.need_inc:
                    continue
                if e not in self.esem:
                    self.esem[e] = self._newsem("e_" + e)
                cnt += 1
                o.sem = self.esem[e]
                o.val = cnt
        me = self

        def run(e, eng):
            waited = {}
            for o in me.q[e]:
                for d in o.deps:
                    k = d.sem.num
                    if waited.get(k, 0) >= d.val:
                        continue
                    eng.wait_ge(d.sem, d.val)
                    waited[k] = d.val
                ins = o.fn(eng)
                if o.need_inc:
                    ins.then_inc(o.sem, 16 if o.is_dma else 1)
            if e == "sp":
                for t in me.dma_tiles.values():
                    eng.wait_ge(t.dsem, t.dcnt)

        with nc.Block() as block:
            @block.sync
            def _(eng):
                run("sp", eng)

            @block.tensor
            def _(eng):
                run("pe", eng)

            @block.scalar
            def _(eng):
                run("act", eng)

            @block.vector
            def _(eng):
                run("dve", eng)

            @block.gpsimd
            def _(eng):
                run("pool", eng)


class Ring:
    def __init__(self, items):
        self.items = items
        self.i = 0

    def next(self):
        it = self.items[self.i % len(self.items)]
        self.i += 1
        return it


def build_nc():
    nc = bass.Bass("TRN2", target_bir_lowering=False)
    p = Prog(nc)

    def din(name, shape):
        return nc.dram_tensor(name, list(shape), F32, kind="ExternalInput").ap()

    x_d = din("x", [TOK, D])
    xh_d = din("xh", [2, D])
    xs_d = [din("ctxf", [CTX + 2, D]), din("ctxr", [CTX + 2, D]),
            din("xo0", [TOK + 2, D]), din("xo1", [TOK + 2, D]), din("xo2", [TOK + 2, D])]
    SL_NT = [2, 2, 16, 16, 16]
    SL_VEC = [1, 1, 0, 0, 0]
    cT_d = din("cT", [128, 8, 2])
    cst_d = din("cst", [128, 4, 128])
    flags_d = din("flags", [128, 16])
    w_ada_d = din("w_ada", [D, 6 * D])
    b_ada_d = din("b_adaT", [128, 48])
    w_in_d = din("w_in", [D, N_IN])
    wg_d = din("wg", [5, D, 8])
    gb_d = din("gb", [5, 8])
    cw_d = din("cwk", [5, 3, 512])
    cbk_d = din("cbk", [512])
    lng_d = din("gmlp_ln_g", [512])
    wsT_d = din("wsT", [128, 4, 128])
    bsT_d = din("bsT", [128, 4])
    qkw_d = din("qkcw", [128, 8, 3])
    qkb_d = din("qkcb", [128, 8])
    gbo_d = din("gbo", [16])
    mng_d = din("mix_norm_g", [D])
    w_out_d = din("w_out", [D, D])
    ln1g_d = din("ln1_g", [D]); ln1b_d = din("ln1_b", [D])
    w_up_d = din("w_up", [D, 2 * DFF])
    fcw_d = din("fcw", [128, 42, 3])
    fcb_d = din("fcb", [128, 42])
    w_dn_d = din("w_down", [DFF, D])
    ln2g_d = din("ln2_g", [D]); ln2b_d = din("ln2_b", [D])
    out_d = nc.dram_tensor("out", [TOK, D], F32, kind="ExternalOutput").ap()
    x1s_d = nc.dram_tensor("x1s", [TOK, D], F32).ap()

    def sb(name, shape, dt=F32):
        return nc.alloc_sbuf_tensor("s_" + name, list(shape), dt)

    PBALL = nc.alloc_psum_tensor("pball", [128, 4096], F32)
    PB = [PBALL[:, i * 512:(i + 1) * 512] for i in range(8)]
    PT = [T("pb%d" % i) for i in range(8)]

    cst = sb("cst", [128, 4, 128]); t_cst = T("cst")
    ident = cst[:, 0, :]; triU = cst[:, 1, :]; triL = cst[:, 2, :]; ones = cst[:, 3, :]
    cstb = sb("cstb", [128, 4, 128], BF16); t_cstb = T("cstb")
    identb = cstb[:, 0, :]; maskU = cstb[:, 1, :]; maskL = cstb[:, 2, :]; onesb = cstb[:, 3, :]
    flags = sb("flags", [128, 16]); t_flags = T("flags")
    p.op("sp", lambda e: e.dma_start(out=cst[:], in_=cst_d), writes=[t_cst], dma=t_cst)
    p.op("sp", lambda e: e.dma_start(out=flags[:], in_=flags_d), writes=[t_flags], dma=t_flags)
    p.op("dve", lambda e: e.tensor_copy(out=cstb[:], in_=cst[:]), reads=[t_cst], writes=[t_cstb])

    def load_const(name, shape, src):
        tl = sb(name, shape); tt = T(name)
        p.op("sp", lambda e: e.dma_start(out=tl[:], in_=src), writes=[tt], dma=tt)
        return tl, tt

    b_adaT, t_bada = load_const("b_adaT", [128, 48], b_ada_d)
    cT, t_cT = load_const("cT", [128, 8, 2], cT_d)
    bsT, t_bsT = load_const("bsT", [128, 4], bsT_d)
    qkw, t_qkw = load_const("qkw", [128, 8, 3], qkw_d)
    qkb, t_qkb = load_const("qkb", [128, 8], qkb_d)
    fcw, t_fcw = load_const("fcw", [128, 42, 3], fcw_d)
    fcb, t_fcb = load_const("fcb", [128, 42], fcb_d)
    gbo, t_gbo = load_const("gbo", [128, 16], gbo_d.partition_broadcast(128))
    gbs, t_gbs = load_const("gbs", [128, 40], gb_d.rearrange("a b -> (a b)").partition_broadcast(128))
    lng, t_lng = load_const("lng", [128, 512], lng_d.partition_broadcast(128))
    cbk32, t_cbk32 = load_const("cbk32", [1, 512], cbk_d.rearrange("(a b) -> a b", a=1))
    cbkb = sb("cbkb", [1, 512], BF16); t_cbkb = T("cbkb")
    p.op("dve", lambda e: e.tensor_copy(out=cbkb[:], in_=cbk32[:]), reads=[t_cbk32], writes=[t_cbkb])

    A_WIN, A_HT, A_KTOK, A_VEXT, A_X, A_END = 0, 24704, 41104, 49296, 57552, 73936
    ARENA = sb("arena", [128, A_END], BF16)
    ARENA32 = ARENA.bitcast(F32)
    t_win = T("w_in"); t_wout = T("w_out"); t_wup = T("w_up")
    w_in = ARENA[:, A_WIN:A_WIN + 8 * N_IN].rearrange("p (k n) -> p k n", k=8)
    wsTb = sb("wsTb", [128, 4, 128], BF16); t_wsTb = T("wsTb")
    wgb = sb("wgb", [128, 5, 8, 8], BF16); t_wgb = T("wgb")
    for k in range(8):
        p.op("pool", lambda e, k=k: e.dma_start(out=w_in[:, k, :], in_=w_in_d[k * 128:(k + 1) * 128, :]),
             writes=[t_win], dma=t_win)
    p.op("pool", lambda e: e.dma_start(out=wsTb[:], in_=wsT_d), writes=[t_wsTb], dma=t_wsTb)
    for s_ in range(5):
        p.op("pool", lambda e, s_=s_: e.dma_start(out=wgb[:, s_, :, :], in_=wg_d[s_].rearrange("(k p) n -> p k n", p=128)), writes=[t_wgb], dma=t_wgb)

    siluc = sb("siluc", [128, 8, 2]); t_siluc = T("siluc")
    p.op("act", lambda e: e.activation(out=siluc[:], in_=cT[:], func=AF.Sigmoid), reads=[t_cT], writes=[t_siluc])
    p.op("dve", lambda e: e.tensor_tensor(out=siluc[:], in0=siluc[:], in1=cT[:], op=ALU.mult), reads=[t_siluc, t_cT], writes=[t_siluc])
    mod = sb("mod", [128, 48, 2]); t_mod = T("mod")
    hT = ARENA[:, A_HT:A_HT + 8 * (TOK + 2)].rearrange("p (k n) -> p k n", k=8)
    t_hTt = [T("hT%d" % i) for i in range(NT)]; t_hTh = T("hTh"); t_hT_all = t_hTt + [t_hTh]
    wa = [ARENA32[:, A_HT // 2 + i * 4096:A_HT // 2 + (i + 1) * 4096].rearrange("p (k n) -> p k n", k=8) for i in range(2)]
    t_wa = [T("wa0"), T("wa1")]
    modrow = sb("modrow", [2, 512]); t_modrow = T("modrow")
    for blk in range(4):
        bi = blk % 2
        p.op("sp", lambda e, blk=blk, bi=bi: e.dma_start(
            out=wa[bi], in_=w_ada_d[:, blk * 512:(blk + 1) * 512].rearrange("(k p) n -> p k n", p=128)),
            writes=[t_wa[bi]], dma=t_wa[bi])
        pbk = blk % 2
        for k in range(8):
            p.op("pe", lambda e, bi=bi, k=k, pbk=pbk: e.matmul(PB[pbk][0:2, :], lhsT=siluc[:, k, :], rhs=wa[bi][:, k, :],
                                                              start=(k == 0), stop=(k == 7)), reads=[t_wa[bi], t_siluc], writes=[PT[pbk]])
        p.op("act", lambda e, pbk=pbk: e.activation(out=modrow[:], in_=PB[pbk][0:2, :], func=AF.Copy), reads=[PT[pbk]], writes=[t_modrow])
        for j in range(4):
            p.op("pe", lambda e, j=j, pbk=pbk: e.transpose(PB[2 + pbk][:, j * 2:j * 2 + 2], modrow[0:2, j * 128:(j + 1) * 128], ident[0:2, 0:2]),
                 reads=[t_modrow, t_cst], writes=[PT[2 + pbk]])
        p.op("dve", lambda e, blk=blk, pbk=pbk: e.tensor_tensor(
            out=mod[:, blk * 4:(blk + 1) * 4, :], in0=PB[2 + pbk][:, 0:8].rearrange("p (j v) -> p j v", v=2),
            in1=b_adaT[:, blk * 4:(blk + 1) * 4].unsqueeze(2).to_broadcast([128, 4, 2]), op=ALU.add),
            reads=[PT[2 + pbk], t_bada], writes=[t_mod])
    p.op("dve", lambda e: e.tensor_scalar_add(out=mod[:, 8:16, :], in0=mod[:, 8:16, :], scalar1=1.0), reads=[t_mod], writes=[t_mod])
    fence0 = sb("fence0", [128, 1])
    p.op("dve", lambda e: e.memset(fence0[:], 0.0), writes=t_wa + t_hT_all)
    xt = [sb("xt%d" % i, [128, D]) for i in range(2)]
    t_xt = [T("xt%d" % i) for i in range(2)]
    xt_ring = Ring(list(range(2)))
    ktok = ARENA[:, A_KTOK:A_KTOK + NT * 512].rearrange("p (a b) -> p a b", a=NT); t_ktok = [T("ktok%d" % i) for i in range(NT)]
    vext = ARENA[:, A_VEXT:A_VEXT + NT * 4 * 129].rearrange("p (a b c) -> p a b c", a=NT, b=4); t_vext = [T("vext%d" % i) for i in range(NT)]
    p.op("pool", lambda e: e.memset(ARENA[:, A_VEXT:A_VEXT + NT * 4 * 129], 1.0), writes=t_vext)
    PA = sb("pa", [128, 5136]); PAb = PA.bitcast(BF16)
    G = sb("G", [128, NT, 16]); t_G = T("G")
    GW = PA[:, 2064:3088].rearrange("p (a b c) -> p a b c", a=8, b=NT); t_GW = T("GW")
    hmod = sb("hmod", [128, 2, 8]); t_hmod = T("hmod")
    vs = [sb("vs%d" % i, [128, 4, 129], BF16) for i in range(2)]
    t_vs = [T("vs%d" % i) for i in range(2)]
    vs_ring = Ring(list(range(2)))
    S = PA[:, 0:1032].rearrange("p (a b) -> p a b", a=8); t_S = [T("S%d" % i) for i in range(8)]
    Sb = sb("Sb", [128, 8, 129], BF16); t_Sb = [T("Sb%d" % i) for i in range(8)]
    ctxS = PA[:, 1032:2064].rearrange("p (a b) -> p a b", a=8); t_ctxS = T("ctxS")
    initF = sb("initF", [128, 4, 129]); t_initF = T("initF")
    tmpS = sb("tmpS", [128, 4, 129]); t_tmpS = T("tmpS")
    Wk = ARENA[:, A_X:A_X + 12288].rearrange("p (a b c) -> p a b c", a=3, b=8); t_Wk = T("Wk")
    cwbc = ARENA32[:, (A_X + 12288) // 2:(A_X + 12288) // 2 + 1536].rearrange("p (a b) -> p a b", a=3); t_cwbc = T("cwbc")
    sgk = ARENA32[:, (A_X + 15360) // 2:(A_X + 15360) // 2 + 512]; t_sgk = T("sgk")

    kcol = 1536
    vcol = 2048

    def transposes(src_d, row0, nrows, col0, vec, gate=None, ti=0):
        xi = xt_ring.next()
        hi = hx_ring.next() if gate is not None else None
        p.op("sp", lambda e: e.dma_start(out=xt[xi][0:nrows, :], in_=src_d[row0:row0 + nrows, :]), writes=[t_xt[xi]], dma=t_xt[xi])
        for half in range(2):
            pbk = half
            for k4 in range(4):
                k = half * 4 + k4
                p.op("pe", lambda e, k=k, k4=k4, pbk=pbk: e.transpose(PB[pbk][:, k4 * 128:k4 * 128 + nrows], xt[xi][0:nrows, k * 128:(k + 1) * 128],
                                                                      ident[0:nrows, 0:nrows]),
                     reads=[t_xt[xi], t_cst], writes=[PT[pbk]])
            for k4 in range(4):
                k = half * 4 + k4
                p.op("act", lambda e, k=k, k4=k4, pbk=pbk: e.activation(
                    out=hT[:, k, col0:col0 + nrows], in_=PB[pbk][:, k4 * 128:k4 * 128 + nrows], func=AF.Identity,
                    bias=mod[:, k, vec:vec + 1], scale=mod[:, 8 + k, vec:vec + 1]),
                    reads=[PT[pbk], t_mod], writes=[t_hTt[ti]])
                if gate is not None:
                    p.op("act", lambda e, k=k, k4=k4, pbk=pbk: e.activation(
                        out=hx32[hi][:, k, 0:nrows], in_=PB[pbk][:, k4 * 128:k4 * 128 + nrows], func=AF.Identity,
                        bias=mod[:, k, vec:vec + 1], scale=mod[:, 8 + k, vec:vec + 1]),
                        reads=[PT[pbk], t_mod], writes=[t_hx32[hi]])
        if gate is not None:
            w32, t_w32, ng, dsts = gate
            for k in range(8):
                p.op("pe", lambda e, k=k: e.matmul(PB[4][:, 0:ng], lhsT=hx32[hi][:, k, 0:nrows], rhs=w32[:, k, :],
                                                  start=(k == 0), stop=(k == 7)), reads=[t_hx32[hi], t_w32], writes=[PT[4]])
            for (dst, c0, c1) in dsts:
                p.op("dve", lambda e, dst=dst, c0=c0, c1=c1: e.tensor_copy(out=dst, in_=PB[4][:, c0:c1]), reads=[PT[4]], writes=[t_G])

    def halos(src_rows, vec, fcol, ncols):
        xi = xt_ring.next()
        for r in range(2):
            p.op("sp", lambda e, r=r: e.dma_start(out=xt[xi][r:r + 1, :], in_=src_rows[r]), writes=[t_xt[xi]], dma=t_xt[xi])
        for r in range(2):
            p.op("dve", lambda e, r=r: e.tensor_scalar_mul(out=hmod[:, 0, :], in0=mod[:, 0:8, vec], scalar1=flags[:, fcol + r:fcol + r + 1]),
                 reads=[t_mod, t_flags], writes=[t_hmod])
            p.op("dve", lambda e, r=r: e.tensor_scalar_mul(out=hmod[:, 1, :], in0=mod[:, 8:16, vec], scalar1=flags[:, fcol + r:fcol + r + 1]),
                 reads=[t_mod, t_flags], writes=[t_hmod])
            for k in range(8):
                p.op("pe", lambda e, k=k: e.transpose(PB[0][:, k * 2:k * 2 + 2], xt[xi][0:2, k * 128:(k + 1) * 128], ident[0:2, 0:2]),
                     reads=[t_xt[xi], t_cst], writes=[PT[0]])
            c = 0 if r == 0 else ncols + 1
            for k in range(8):
                p.op("act", lambda e, k=k, r=r, c=c: e.activation(
                    out=hT[:, k, c:c + 1], in_=PB[0][:, k * 2 + r:k * 2 + r + 1], func=AF.Identity,
                    bias=hmod[:, 0, k:k + 1], scale=hmod[:, 1, k:k + 1]),
                    reads=[PT[0], t_hmod], writes=[t_hTh])

    def gate_math(nt, bi_ap, bf_ap, ncol):
        li = GW[:, 4, 0:nt, 0:ncol]; gf = GW[:, 5, 0:nt, 0:ncol]; bb = GW[:, 6, 0:nt, 0:ncol]; bl = GW[:, 7, 0:nt, 0:ncol]
        bc = lambda ap: ap.unsqueeze(1).to_broadcast([128, nt, ncol])
        p.op("dve", lambda e: e.tensor_tensor(out=li, in0=G[:, 0:nt, 0:ncol], in1=bc(bi_ap), op=ALU.add), reads=[t_G, t_gbo, t_gbs], writes=[t_GW])
        p.op("dve", lambda e: e.tensor_tensor(out=gf, in0=G[:, 0:nt, 8:8 + ncol], in1=bc(bf_ap), op=ALU.add), reads=[t_G, t_gbo, t_gbs], writes=[t_GW])
        p.op("act", lambda e: e.activation(out=gf, in_=gf, func=AF.Exp, scale=-1.0), reads=[t_GW], writes=[t_GW])
        p.op("act", lambda e: e.activation(out=gf, in_=gf, func=AF.Ln, bias=1.0), reads=[t_GW], writes=[t_GW])
        for h in range(ncol // 4):
            tri = triU if h == 0 else triL
            p.op("pe", lambda e, h=h, tri=tri: e.matmul(PB[6][:, h * 64:h * 64 + nt * 4], lhsT=tri, rhs=GW[:, 5, 0:nt, h * 4:h * 4 + 4],
                                                         start=True, stop=True), reads=[t_GW, t_cst], writes=[PT[6]])
        p.op("pe", lambda e: e.matmul(PB[6][:, 256:256 + nt * ncol], lhsT=ones, rhs=gf, start=True, stop=True), reads=[t_GW, t_cst], writes=[PT[6]])
        for h in range(ncol // 4):
            p.op("dve", lambda e, h=h: e.tensor_scalar_mul(out=GW[:, 6, 0:nt, h * 4:h * 4 + 4],
                                                           in0=PB[6][:, h * 64:h * 64 + nt * 4].rearrange("p (a b) -> p a b", b=4), scalar1=-1.0),
                 reads=[PT[6]], writes=[t_GW])
        p.op("dve", lambda e: e.tensor_scalar_mul(out=bl, in0=PB[6][:, 256:256 + nt * ncol].rearrange("p (a b) -> p a b", b=ncol), scalar1=-1.0),
             reads=[PT[6]], writes=[t_GW])
        p.op("dve", lambda e: e.tensor_tensor(out=li, in0=li, in1=bb, op=ALU.subtract), reads=[t_GW], writes=[t_GW])
        p.op("act", lambda e: e.activation(out=GW[:, 0, 0:nt, 0:ncol], in_=li, func=AF.Exp), reads=[t_GW], writes=[t_GW])
        p.op("act", lambda e: e.activation(out=GW[:, 1, 0:nt, 0:ncol], in_=bb, func=AF.Exp, bias=LN_QSCALE), reads=[t_GW], writes=[t_GW])
        p.op("dve", lambda e: e.tensor_tensor(out=li, in0=li, in1=bl, op=ALU.add), reads=[t_GW], writes=[t_GW])
        p.op("act", lambda e: e.activation(out=GW[:, 2, 0:nt, 0:ncol], in_=li, func=AF.Exp), reads=[t_GW], writes=[t_GW])
        p.op("act", lambda e: e.activation(out=GW[:, 3, 0:nt, 0:ncol], in_=bl, func=AF.Exp), reads=[t_GW], writes=[t_GW])

    def state_update(i, h, gcol, sidx, pbk, first=False):
        vi = vs_ring.next()
        p.op("act", lambda e: e.activation(out=vs[vi][:, h, :], in_=vext[:, i, h, :], func=AF.Copy, scale=GW[:, 2, i, gcol:gcol + 1]),
             reads=[t_vext[i], t_GW], writes=[t_vs[vi]])
        p.op("pe", lambda e: e.matmul(PB[pbk][:, (h % 2) * 256:(h % 2) * 256 + 129], lhsT=ktok[:, i, h * 128:(h + 1) * 128], rhs=vs[vi][:, h, :],
                                      start=True, stop=True), reads=[t_ktok[i], t_vs[vi]], writes=[PT[pbk]])
        p.op("dve", lambda e: e.scalar_tensor_tensor(out=S[:, sidx, :], in0=S[:, sidx, :], scalar=GW[:, 3, i, gcol:gcol + 1],
                                                     in1=PB[pbk][:, (h % 2) * 256:(h % 2) * 256 + 129], op0=ALU.mult, op1=ALU.add),
             reads=[t_S[sidx], t_GW, PT[pbk]], writes=[t_S[sidx]])

    p.op("pool", lambda e: e.memset(PA[:, 0:1032], 0.0), writes=t_S)
    p.op("pool", lambda e: e.memset(initF[:].rearrange("p a b -> p (a b)"), 0.0), writes=[t_initF])

    def chain_switch(kk):
        swc = flags[:, 12 + kk:13 + kk]
        s4 = S[:, 0:4, :]
        rd = [t_flags, t_initF, t_ctxS, t_tmpS] + t_S[0:4]
        p.op("dve", lambda e: e.scalar_tensor_tensor(out=initF[:], in0=s4, scalar=swc, in1=initF[:], op0=ALU.mult, op1=ALU.add),
             reads=rd, writes=[t_initF])
        p.op("dve", lambda e: e.tensor_tensor(out=tmpS[:], in0=ctxS[:, 4:8, :], in1=s4, op=ALU.subtract), reads=rd, writes=[t_tmpS])
        p.op("dve", lambda e: e.scalar_tensor_tensor(out=s4, in0=tmpS[:], scalar=swc, in1=s4, op0=ALU.mult, op1=ALU.add),
             reads=rd, writes=t_S[0:4])

    for sl in range(5):
        nt = SL_NT[sl]; ncols = nt * 128; vec = SL_VEC[sl]; src = xs_d[sl]
        p.op("sp", lambda e, sl=sl: e.dma_start(out=ARENA32[:, (A_X + 12288) // 2:(A_X + 12288) // 2 + 1536],
                                                  in_=cw_d[sl].rearrange("a b -> (a b)").partition_broadcast(128)),
             writes=[t_cwbc], dma=t_cwbc)
        n_b = 0
        for j in ((0, 1, 2) if sl == 0 else (0, 2)):
            for k in range(8):
                eng = "pool" if (n_b % 4 == 3) else "dve"
                n_b += 1
                p.op(eng, lambda e, j=j, k=k: e.tensor_tensor(out=Wk[:, j, k, :], in0=w_in[:, k, kcol:kcol + 512], in1=cwbc[:, j, :], op=ALU.mult),
                     reads=[t_win, t_cwbc], writes=[t_Wk])
        halos([src[0:1, :], src[ncols + 1:ncols + 2, :]], vec, 2 + 2 * sl, ncols)

        def slot_T(i, sl=sl, src=src, vec=vec):
            transposes(src, 1 + i * 128, 128, 1 + i * 128, vec, ti=i)

        def win(i, nt=nt):
            return [t_hTt[i], t_hTt[i - 1] if i > 0 else t_hTh, t_hTt[i + 1] if i < nt - 1 else t_hTh]
        if sl == 2:
            chain_switch(0)
        slot_T(0)
        if nt > 1:
            slot_T(1)
        for i in range(nt):
            if i + 2 < nt:
                slot_T(i + 2)
            n = 0
            for j in range(3):
                for k in range(8):
                    p.op("pe", lambda e, i=i, j=j, k=k, n=n: e.matmul(PB[2][:], lhsT=hT[:, k, i * 128 + j:i * 128 + j + 128], rhs=Wk[:, j, k, :],
                                                                       start=(n == 0), stop=False), reads=win(i) + [t_Wk], writes=[PT[2]])
                    n += 1
            p.op("pe", lambda e: e.matmul(PB[2][:], lhsT=onesb[0:1, :], rhs=cbkb[:], start=False, stop=True), reads=[t_cstb, t_cbkb], writes=[PT[2]])
            p.op("act", lambda e: e.activation(out=sgk, in_=PB[2][:], func=AF.Sigmoid), reads=[PT[2]], writes=[t_sgk])
            p.op("dve", lambda e, i=i: e.tensor_tensor(out=ktok[:, i, :], in0=PB[2][:], in1=sgk, op=ALU.mult), reads=[PT[2], t_sgk], writes=[t_ktok[i]])
            for k in range(8):
                p.op("pe", lambda e, i=i, k=k: e.matmul(PB[3][:], lhsT=hT[:, k, 1 + i * 128:1 + (i + 1) * 128], rhs=w_in[:, k, vcol:vcol + 512],
                                                         start=(k == 0), stop=(k == 7)), reads=[t_hTt[i], t_win], writes=[PT[3]])
            p.op("act", lambda e, i=i: e.activation(out=vext[:, i, :, 0:128], in_=PB[3][:].rearrange("p (a b) -> p a b", b=128), func=AF.Copy),
                 reads=[PT[3]], writes=[t_vext[i]])
            for k in range(8):
                p.op("pe", lambda e, i=i, k=k, sl=sl: e.matmul(PB[4][:, 0:8], lhsT=hT[:, k, 1 + i * 128:1 + (i + 1) * 128], rhs=wgb[:, sl, k, :],
                                                                start=(k == 0), stop=(k == 7)), reads=[t_hTt[i], t_wgb], writes=[PT[4]])
            p.op("dve", lambda e, i=i: e.tensor_copy(out=G[:, i, 0:4], in_=PB[4][:, 0:4]), reads=[PT[4]], writes=[t_G])
            p.op("dve", lambda e, i=i: e.tensor_copy(out=G[:, i, 8:12], in_=PB[4][:, 4:8]), reads=[PT[4]], writes=[t_G])
        gate_math(nt, gbs[:, sl * 8:sl * 8 + 4], gbs[:, sl * 8 + 4:sl * 8 + 8], 4)
        for i in range(nt):
            p.op("pool", lambda e, i=i: e.tensor_tensor(out=vext[:, i, :, 0:128], in0=vext[:, i, :, 0:128],
                                                        in1=GW[:, 2, i, 0:4].unsqueeze(2).to_broadcast([128, 4, 128]), op=ALU.mult),
                 reads=[t_vext[i], t_GW], writes=[t_vext[i]])
            p.op("pool", lambda e, i=i: e.tensor_copy(out=vext[:, i, :, 128], in_=GW[:, 2, i, 0:4]), reads=[t_vext[i], t_GW], writes=[t_vext[i]])
        for i in range(nt):
            for h in range(4):
                pbk = 5 + (h // 2)
                p.op("pe", lambda e, i=i, h=h, pbk=pbk: e.matmul(PB[pbk][:, (h % 2) * 256:(h % 2) * 256 + 129], lhsT=ktok[:, i, h * 128:(h + 1) * 128], rhs=vext[:, i, h, :],
                                                                 start=True, stop=True), reads=[t_ktok[i], t_vext[i]], writes=[PT[pbk]])
                p.op("dve", lambda e, i=i, h=h, pbk=pbk: e.scalar_tensor_tensor(out=S[:, h, :], in0=S[:, h, :], scalar=GW[:, 3, i, h:h + 1],
                                                                                in1=PB[pbk][:, (h % 2) * 256:(h % 2) * 256 + 129], op0=ALU.mult, op1=ALU.add),
                     reads=[t_S[h], t_GW, PT[pbk]], writes=[t_S[h]])
        if sl == 0:
            p.op("dve", lambda e: e.tensor_copy(out=ctxS[:, 0:4, :], in_=S[:, 0:4, :]), reads=t_S[0:4], writes=[t_ctxS])
            p.op("pool", lambda e: e.memset(S[:, 0:4, :], 0.0), reads=[t_ctxS], writes=t_S[0:4])
        elif sl == 1:
            p.op("dve", lambda e: e.tensor_copy(out=ctxS[:, 4:8, :], in_=S[:, 0:4, :]), reads=t_S[0:4], writes=[t_ctxS])
            p.op("dve", lambda e: e.tensor_copy(out=S[:, 0:4, :], in_=ctxS[:, 0:4, :]), reads=[t_ctxS], writes=t_S[0:4])
        else:
            chain_switch(sl - 1)
    p.op("pool", lambda e: e.memset(vext[:, :, :, 128], 1.0), writes=t_vext)
    p.op("dve", lambda e: e.tensor_copy(out=S[:, 4:8, :], in_=S[:, 0:4, :]), reads=t_S[0:4], writes=t_S[4:8])
    p.op("dve", lambda e: e.tensor_copy(out=S[:, 0:4, :], in_=initF[:]), reads=[t_initF] + t_S[4:8], writes=t_S[0:4])
    p.op("act", lambda e: e.activation(out=Sb[:], in_=S[:], func=AF.Copy), reads=t_S, writes=t_Sb)
    DBG = globals().get("DBG_STAGE", "NONE")
    ya_d = nc.dram_tensor("ya_s", [NT, 128, 512], F32).ap()
    og_d = nc.dram_tensor("og_s", [NT, 128, 512], F32).ap()
    qk_d = nc.dram_tensor("qk_s", [NT, 128, 8 * 128], BF16).ap()
    t_yad = [T("yad%d" % i) for i in range(NT)]; t_ogd = [T("ogd%d" % i) for i in range(NT)]; t_qkd = [T("qkd%d" % i) for i in range(NT)]
    X32 = A_X // 2

    def xf(off, n):
        return ARENA32[:, X32 + off:X32 + off + n]

    gu = xf(0, 512); t_gu = T("gu")
    gv = xf(512, 512); t_gv = T("gv")
    t1 = xf(1024, 512); t_t1 = T("t1")
    ya_t = [xf(1536 + i * 512, 512) for i in range(2)]; t_ya = [T("ya%d" % i) for i in range(2)]
    og_t = [xf(2560 + i * 512, 512) for i in range(2)]; t_og = [T("og%d" % i) for i in range(2)]
    cv = [xf(3584 + i * 128, 128) for i in range(2)]; t_cv = [T("cv%d" % i) for i in range(2)]
    sgq = [xf(3840 + i * 128, 128) for i in range(2)]; t_sgq = [T("sgq%d" % i) for i in range(2)]
    XB = A_X + 2 * 4096
    vn = ARENA[:, XB:XB + 512]; t_vn = T("vn")
    qk_t = [ARENA[:, XB + 512 + i * 1024:XB + 512 + (i + 1) * 1024].rearrange("p (a b) -> p a b", a=8) for i in range(2)]
    t_qk = [T("qk%d" % i) for i in range(2)]
    stats = sb("stats", [128, 8, 6]); t_stats = T("stats")
    mv = sb("mv", [128, 8, 2]); t_mv = T("mv")
    rstd = sb("rstd", [128, 8]); t_rstd = T("rstd")
    PB7b = PB[7].bitcast(BF16)

    def rstd_of(n, mv_=None, t_mv_=None, rstd_=None, t_rstd_=None):
        mv_ = mv if mv_ is None else mv_; t_mv_ = t_mv if t_mv_ is None else t_mv_
        rstd_ = rstd if rstd_ is None else rstd_; t_rstd_ = t_rstd if t_rstd_ is None else t_rstd_
        p.op("act", lambda e: e.activation(out=rstd_[:, 0:n], in_=mv_[:, 0:n, 1], func=AF.Ln, bias=EPS), reads=[t_mv_], writes=[t_rstd_])
        p.op("act", lambda e: e.activation(out=rstd_[:, 0:n], in_=rstd_[:, 0:n], func=AF.Exp, scale=-0.5), reads=[t_rstd_], writes=[t_rstd_])

    def gelu_from_psum(pb, tpb, dst, tdst):
        p.op("act", lambda e: e.activation(out=t1, in_=pb[:], func=AF.Square), reads=[tpb], writes=[t_t1])
        p.op("dve", lambda e: e.tensor_scalar(out=t1, in0=t1, scalar1=0.044715, scalar2=1.0, op0=ALU.mult, op1=ALU.add), reads=[t_t1], writes=[t_t1])
        p.op("dve", lambda e: e.tensor_tensor(out=t1, in0=t1, in1=pb[:], op=ALU.mult), reads=[t_t1, tpb], writes=[t_t1])
        p.op("act", lambda e: e.activation(out=t1, in_=t1, func=AF.Sigmoid, scale=GC), reads=[t_t1], writes=[t_t1])
        p.op("dve", lambda e: e.tensor_tensor(out=dst, in0=t1, in1=pb[:], op=ALU.mult), reads=[t_t1, tpb], writes=[tdst])

    halos([xh_d[0:1, :], xh_d[1:2, :]], 0, 0, TOK)

    def own_T(i):
        transposes(x_d, i * 128, 128, 1 + i * 128, 0, ti=i)

    def owin(i):
        return [t_hTt[i], t_hTt[i - 1] if i > 0 else t_hTh, t_hTt[i + 1] if i < NT - 1 else t_hTh]
    own_T(0)
    own_T(1)

    def proj_tok(i, pbk, c0, n):
        for k in range(8):
            p.op("pe", lambda e, k=k: e.matmul(PB[pbk][:, 0:n], lhsT=hT[:, k, 1 + i * 128:1 + (i + 1) * 128], rhs=w_in[:, k, c0:c0 + n],
                                              start=(k == 0), stop=(k == 7)), reads=[t_hTt[i], t_win], writes=[PT[pbk]])

    def gen_G(i):
        sl2 = i % 2
        proj_tok(i, 2, 0, 512)
        gelu_from_psum(PB[2], PT[2], gu, t_gu)
        yield
        proj_tok(i, 3, 512, 512)
        gelu_from_psum(PB[3], PT[3], gv, t_gv)
        yield
        for g in range(4):
            p.op("dve", lambda e, g=g: e.bn_stats(out=stats[:, g, :], in_=gv[:, g * 128:(g + 1) * 128]), reads=[t_gv], writes=[t_stats])
        for g in range(4):
            p.op("dve", lambda e, g=g: e.bn_aggr(out=mv[:, g, :], in_=stats[:, g, :]), reads=[t_stats], writes=[t_mv])
        rstd_of(4)
        yield
        for g in range(4):
            p.op("dve", lambda e, g=g: e.tensor_scalar(out=gv[:, g * 128:(g + 1) * 128], in0=gv[:, g * 128:(g + 1) * 128],
                                                       scalar1=mv[:, g, 0:1], scalar2=rstd[:, g:g + 1], op0=ALU.subtract, op1=ALU.mult),
                 reads=[t_gv, t_mv, t_rstd], writes=[t_gv])
        p.op("pool", lambda e: e.tensor_tensor(out=vn, in0=gv, in1=lng[:], op=ALU.mult), reads=[t_gv, t_lng], writes=[t_vn])
        yield
        for g in range(4):
            p.op("pe", lambda e, g=g: e.matmul(PB[4][:, g * 128:(g + 1) * 128], lhsT=wsTb[:, g, :], rhs=vn[:, g * 128:(g + 1) * 128],
                                              start=True, stop=True), reads=[t_wsTb, t_vn], writes=[PT[4]])
        for g in range(4):
            p.op("dve", lambda e, g=g: e.scalar_tensor_tensor(out=ya_t[sl2][:, g * 128:(g + 1) * 128], in0=PB[4][:, g * 128:(g + 1) * 128],
                                                              scalar=bsT[:, g:g + 1], in1=gu[:, g * 128:(g + 1) * 128], op0=ALU.add, op1=ALU.mult),
                 reads=[PT[4], t_bsT, t_gu], writes=[t_ya[sl2]])
        p.op("sp", lambda e, i=i: e.dma_start(out=ya_d[i], in_=ya_t[sl2]), reads=[t_ya[sl2]], writes=[t_yad[i]], dma=t_ya[sl2])
        yield

    def gen_Q(i):
        sl2 = i % 2
        proj_tok(i, 6, vcol, 512)
        p.op("act", lambda e, i=i: e.activation(out=vext[:, i, :, 0:128], in_=PB[6][:].rearrange("p (a b) -> p a b", b=128), func=AF.Copy),
             reads=[PT[6]], writes=[t_vext[i]])
        proj_tok(i, 7, 3072, 16)
        p.op("dve", lambda e, i=i: e.tensor_copy(out=G[:, i, :], in_=PB[7][:, 0:16]), reads=[PT[7]], writes=[t_G])
        yield
        proj_tok(i, 5, 2560, 512)
        p.op("act", lambda e: e.activation(out=og_t[sl2], in_=PB[5][:], func=AF.Sigmoid), reads=[PT[5]], writes=[t_og[sl2]])
        p.op("sp", lambda e, i=i: e.dma_start(out=og_d[i], in_=og_t[sl2]), reads=[t_og[sl2]], writes=[t_ogd[i]], dma=t_og[sl2])
        yield
        def qk_mm(hc):
            pbk = 5 + hc % 2
            c2 = hc % 2
            for k in range(8):
                p.op("pe", lambda e, k=k: e.matmul(PB[pbk][:, 0:130], lhsT=w_in[:, k, 1024 + hc * 128:1024 + (hc + 1) * 128],
                                                  rhs=hT[:, k, i * 128:i * 128 + 130], start=(k == 0), stop=(k == 7)),
                     reads=owin(i) + [t_win], writes=[PT[pbk]])
            p.op("act", lambda e: e.activation(out=cv[c2], in_=PB[pbk][:, 0:128], func=AF.Identity,
                                               bias=qkb[:, hc:hc + 1], scale=qkw[:, hc, 0:1]), reads=[PT[pbk], t_qkw, t_qkb], writes=[t_cv[c2]])

        def qk_ew(hc):
            pbk = 5 + hc % 2
            c2 = hc % 2
            for j in (1, 2):
                p.op("dve", lambda e, j=j: e.scalar_tensor_tensor(out=cv[c2], in0=PB[pbk][:, j:j + 128], scalar=qkw[:, hc, j:j + 1], in1=cv[c2],
                                                                  op0=ALU.mult, op1=ALU.add), reads=[PT[pbk], t_qkw, t_cv[c2]], writes=[t_cv[c2]])
            p.op("act", lambda e: e.activation(out=sgq[c2], in_=cv[c2], func=AF.Sigmoid), reads=[t_cv[c2]], writes=[t_sgq[c2]])
            p.op("dve", lambda e: e.tensor_tensor(out=qk_t[sl2][:, hc, :], in0=cv[c2], in1=sgq[c2], op=ALU.mult),
                 reads=[t_cv[c2], t_sgq[c2]], writes=[t_qk[sl2]])

        qk_mm(0)
        for hc in range(8):
            if hc + 1 < 8:
                qk_mm(hc + 1)
            qk_ew(hc)
            yield
        for h in range(4):
            p.op("pe", lambda e, h=h: e.transpose(PB7b[:, 512 + h * 128:512 + (h + 1) * 128], qk_t[sl2][:, 4 + h, :], identb),
                 reads=[t_qk[sl2], t_cstb], writes=[PT[7]])
        p.op("act", lambda e, i=i: e.activation(out=ktok[:, i, :], in_=PB7b[:, 512:1024], func=AF.Copy), reads=[PT[7]], writes=[t_ktok[i]])
        p.op("sp", lambda e, i=i: e.dma_start(out=qk_d[i], in_=qk_t[sl2].rearrange("p a b -> p (a b)")), reads=[t_qk[sl2]], writes=[t_qkd[i]], dma=t_qk[sl2])
        yield

    def drain(*gens):
        gens = [g for g in gens if g is not None]
        while gens:
            for g in list(gens):
                try:
                    next(g)
                except StopIteration:
                    gens.remove(g)

    def qk_reload(i):
        p.op("sp", lambda e: e.dma_start(out=hT[:, :, 1 + i * 128:1 + (i + 1) * 128], in_=qk_d[i].rearrange("p (a b) -> p a b", a=8)),
             reads=[t_qkd[i]], writes=[t_hTt[i]], dma=t_hTt[i])

    wa3 = [PA[:, 3088 + i * 1024:3088 + (i + 1) * 1024].rearrange("p (k n) -> p k n", k=8) for i in range(2)]; t_wa3 = [T("wa3_0"), T("wa3_1")]

    def mod3_dma(qb):
        sl_ = qb % 2
        c0 = 2048 + qb * 128
        p.op("sp", lambda e: e.dma_start(out=wa3[sl_], in_=w_ada_d[:, c0:c0 + 128].rearrange("(k p) n -> p k n", p=128)), writes=[t_wa3[sl_]], dma=t_wa3[sl_])

    def mod3_tile(i):
        for h2 in range(2):
            qb = 2 * i + h2
            for k in range(8):
                p.op("pe", lambda e, k=k, h2=h2: e.matmul(PB[4][0:2, h2 * 128:(h2 + 1) * 128], lhsT=siluc[:, k, :], rhs=wa3[h2][:, k, :],
                                                          start=(k == 0), stop=(k == 7)), reads=[t_wa3[h2], t_siluc], writes=[PT[4]])
        p.op("act", lambda e: e.activation(out=modrow[:, 0:256], in_=PB[4][0:2, 0:256], func=AF.Copy), reads=[PT[4]], writes=[t_modrow])
        for h2 in range(2):
            p.op("pe", lambda e, h2=h2: e.transpose(PB[4][:, 256 + h2 * 2:256 + h2 * 2 + 2], modrow[0:2, h2 * 128:(h2 + 1) * 128], ident[0:2, 0:2]),
                 reads=[t_modrow, t_cst], writes=[PT[4]])
        p.op("dve", lambda e: e.tensor_tensor(out=mod[:, 16 + 2 * i:16 + 2 * i + 2, :], in0=PB[4][:, 256:260].rearrange("p (j v) -> p j v", v=2),
                                              in1=b_adaT[:, 16 + 2 * i:16 + 2 * i + 2].unsqueeze(2).to_broadcast([128, 2, 2]), op=ALU.add),
             reads=[PT[4], t_bada], writes=[t_mod])
        if i + 1 < NT:
            mod3_dma(2 * (i + 1)); mod3_dma(2 * (i + 1) + 1)

    mod3_dma(0); mod3_dma(1)
    drain(gen_G(0))
    for i in range(NT):
        mod3_tile(i)
        if i + 2 < NT:
            own_T(i + 2)
        drain(gen_Q(i), gen_G(i + 1) if i + 1 < NT else None)
        if i >= 1:
            qk_reload(i - 1)
    qk_reload(NT - 1)
    gate_math(NT, gbo[:, 0:8], gbo[:, 8:16], 8)

    if DBG == "A1":
        dbg = xt[0]; t_dbg = t_xt[0]

        def dump(r0, c0, n, src_ap, rd, eng="dve"):
            p.op(eng, lambda e: e.tensor_copy(out=dbg[:, 0:n], in_=src_ap), reads=rd + [t_dbg], writes=[t_dbg])
            o = p.op("sp", lambda e: e.dma_start(out=out_d[r0:r0 + 128, c0:c0 + n], in_=dbg[:, 0:n]), reads=[t_dbg], dma=t_dbg)
            p.final.append(("sp", o.sem, o.val))
        for (row, ti) in ((0, 0), (5, 15)):
            base = row * 128
            dump(base, 0, 1024, qk_t[ti % 2].rearrange("p a b -> p (a b)"), [t_qk[ti % 2]]) if ti == 15 else None
        dump(128, 0, 512, ktok[:, 0, :], [t_ktok[0]])
        dump(128, 512, 512, ktok[:, 15, :], [t_ktok[15]])
        dump(256, 0, 516, vext[:, 0, :, :].rearrange("p a b -> p (a b)"), [t_vext[0]])
        dump(384, 0, 256, G[:].rearrange("p a b -> p (a b)"), [t_G])
        dump(512, 0, 512, GW[:, 0, :, :].rearrange("p a b -> p (a b)")[:, 0:128], [t_GW]) if False else None
        for pl in range(4):
            dump(512, pl * 128, 128, GW[:, pl, :, :].rearrange("p a b -> p (a b)"), [t_GW])
        dump(768, 0, 512, ya_t[1], [t_ya[1]])
        dump(768, 512, 512, og_t[1], [t_og[1]])
        return nc, p

    H = ARENA32[:, 0:NT * 512].rearrange("p (a b) -> p a b", a=NT); t_H = [T("H%d" % i) for i in range(NT)]
    w_out = ARENA[:, 16384:16384 + 8 * D].rearrange("p (k n) -> p k n", k=8)
    t_qkT = T("qkT_unused")
    for k in range(8):
        p.op("pool", lambda e, k=k: e.dma_start(out=w_out[:, k, :], in_=w_out_d[k * 128:(k + 1) * 128, :]),
             writes=[t_wout, t_win], dma=t_wout)
    swb = [ARENA[:, XB + 4096 + i * 512:XB + 4096 + (i + 1) * 512].rearrange("p (a b) -> p a b", a=4) for i in range(2)]
    t_sw = [T("sw%d" % i) for i in range(2)]
    dn = [sb("dn%d" % i, [128, 4]) for i in range(2)]; t_dn = [T("dn%d" % i) for i in range(2)]
    r4 = [sb("r4%d" % i, [128, 4]) for i in range(2)]; t_r4 = [T("r4%d" % i) for i in range(2)]
    PP = [PBALL[:, (2 + 2 * d) * 512:(4 + 2 * d) * 512] for d in range(2)]
    PPT = [T("pp0"), T("pp1")]
    h_first = [True] * NT

    vsd = [sb("vsd%d" % i, [128, 4, 129], BF16) for i in range(2)]; t_vsd = [T("vsd%d" % i) for i in range(2)]

    def scan_pair(j):
        last = (j == NT - 1)
        steps = [(j, 0), (NT - 1 - j, 1)]
        css = {c: slice(1 + c * 128, 1 + (c + 1) * 128) for c, _ in steps}
        if not last:
            for c, d in steps:
                for h in range(4):
                    p.op("act", lambda e, c=c, d=d, h=h: e.activation(out=vsd[d][:, h, :], in_=vext[:, c, h, :], func=AF.Copy,
                                                                      scale=GW[:, 2, c, d * 4 + h:d * 4 + h + 1]),
                         reads=[t_vext[c], t_GW], writes=[t_vsd[d]])
        for c, d in steps:
            for h in range(4):
                p.op("pe", lambda e, c=c, d=d, h=h: e.matmul(PB[d][:, h * 128:(h + 1) * 128], lhsT=hT[:, 4 + h, css[c]], rhs=hT[:, h, css[c]], start=True, stop=True),
                     reads=[t_hTt[c]], writes=[PT[d]])
        for c, d in steps:
            mask = maskU if d == 0 else maskL
            for h in range(4):
                p.op("dve", lambda e, c=c, d=d, h=h, mask=mask: e.scalar_tensor_tensor(out=swb[d][:, h, :], in0=PB[d][:, h * 128:(h + 1) * 128],
                                                                                     scalar=GW[:, 0, c, d * 4 + h:d * 4 + h + 1], in1=mask, op0=ALU.mult, op1=ALU.mult),
                     reads=[PT[d], t_GW, t_cstb], writes=[t_sw[d]])
        for c, d in steps:
            for h in range(4):
                p.op("pe", lambda e, c=c, d=d, h=h: e.matmul(PP[d][:, h * 256:h * 256 + 129], lhsT=swb[d][:, h, :], rhs=vext[:, c, h, :], start=True, stop=False),
                     reads=[t_sw[d], t_vext[c]], writes=[PPT[d]])
                p.op("pe", lambda e, c=c, d=d, h=h: e.matmul(PP[d][:, h * 256:h * 256 + 129], lhsT=hT[:, h, css[c]], rhs=Sb[:, d * 4 + h, :], start=False, stop=True),
                     reads=[t_hTt[c], t_Sb[d * 4 + h]], writes=[PPT[d]])
        if not last:
            for c, d in steps:
                for h in range(4):
                    pbk = 6 + (h // 2)
                    p.op("pe", lambda e, c=c, d=d, h=h, pbk=pbk: e.matmul(PB[pbk][:, (h % 2) * 256:(h % 2) * 256 + 129], lhsT=ktok[:, c, h * 128:(h + 1) * 128],
                                                                         rhs=vsd[d][:, h, :], start=True, stop=True), reads=[t_ktok[c], t_vsd[d]], writes=[PT[pbk]])
                    sidx = d * 4 + h
                    p.op("dve", lambda e, c=c, h=h, pbk=pbk, sidx=sidx: e.scalar_tensor_tensor(out=S[:, sidx, :], in0=S[:, sidx, :], scalar=GW[:, 3, c, sidx:sidx + 1],
                                                                                           in1=PB[pbk][:, (h % 2) * 256:(h % 2) * 256 + 129], op0=ALU.mult, op1=ALU.add),
                         reads=[t_S[sidx], t_GW, PT[pbk]], writes=[t_S[sidx]])
        for c, d in steps:
            den = PP[d].rearrange("p (a b) -> p a b", b=256)[:, :, 128]
            ebs = GW[:, 1, c, d * 4:(d + 1) * 4]
            p.op("dve", lambda e, d=d, den=den, ebs=ebs: e.tensor_tensor(out=dn[d][:], in0=den, in1=ebs, op=ALU.mult), reads=[PPT[d], t_GW], writes=[t_dn[d]])
            p.op("dve", lambda e, d=d: e.scalar_tensor_tensor(out=r4[d][:], in0=dn[d][:], scalar=-1.0, in1=dn[d][:], op0=ALU.mult, op1=ALU.max),
                 reads=[t_dn[d]], writes=[t_r4[d]])
            p.op("dve", lambda e, d=d: e.tensor_scalar_max(out=r4[d][:], in0=r4[d][:], scalar1=1.0), reads=[t_r4[d]], writes=[t_r4[d]])
            p.op("dve", lambda e, d=d: e.reciprocal(out=r4[d][:], in_=r4[d][:]), reads=[t_r4[d]], writes=[t_r4[d]])
            p.op("dve", lambda e, d=d, ebs=ebs: e.tensor_tensor(out=r4[d][:], in0=r4[d][:], in1=ebs, op=ALU.mult), reads=[t_r4[d], t_GW], writes=[t_r4[d]])
        if not last:
            for c, d in steps:
                for h in range(4):
                    sidx = d * 4 + h
                    p.op("act", lambda e, sidx=sidx: e.activation(out=Sb[:, sidx, :], in_=S[:, sidx, :], func=AF.Copy), reads=[t_S[sidx]], writes=[t_Sb[sidx]])
        for c, d in steps:
            first = h_first[c]
            h_first[c] = False
            for h in range(4):
                hs = slice(h * 128, (h + 1) * 128)
                if first:
                    p.op("act", lambda e, c=c, d=d, h=h, hs=hs: e.activation(out=H[:, c, hs], in_=PP[d][:, h * 256:h * 256 + 128], func=AF.Copy, scale=r4[d][:, h:h + 1]),
                         reads=[PPT[d], t_r4[d]], writes=[t_H[c], t_win])
                else:
                    p.op("dve", lambda e, c=c, d=d, h=h, hs=hs: e.scalar_tensor_tensor(out=H[:, c, hs], in0=PP[d][:, h * 256:h * 256 + 128], scalar=r4[d][:, h:h + 1],
                                                                                     in1=H[:, c, hs], op0=ALU.mult, op1=ALU.add),
                         reads=[PPT[d], t_r4[d], t_H[c]], writes=[t_H[c]])

    for j in range(NT):
        scan_pair(j)
    p.op("dve", lambda e: e.tensor_scalar_add(out=mod[:, 32:40, :], in0=mod[:, 32:40, :], scalar1=1.0), reads=[t_mod], writes=[t_mod])

    A_WDN = 8 * 2 * DFF
    A_SP = A_WDN + NFC * D
    w_up = ARENA[:, 0:A_WDN].rearrange("p (k n) -> p k n", k=8)
    w_dn = ARENA[:, A_WDN:A_SP].rearrange("p (k n) -> p k n", k=NFC)
    t_wdn = T("w_dn")
    fence1 = sb("fence1", [128, 1])
    p.op("dve", lambda e: e.memset(fence1[:], 0.0), writes=t_hT_all + t_ktok + t_vext + [t_wup, t_wdn])
    for k in (5, 6, 7):
        p.op("pool", lambda e, k=k: e.dma_start(out=w_up[:, k, :], in_=w_up_d[k * 128:(k + 1) * 128, :]), writes=[t_wup], dma=t_wup)
    for k in range(14):
        p.op("pool", lambda e, k=k: e.dma_start(out=w_dn[:, k, :], in_=w_dn_d[k * 128:(k + 1) * 128, :]), writes=[t_wdn], dma=t_wdn)

    if DBG == "H":
        for c in range(NT):
            o = p.op("sp", lambda e, c=c: e.dma_start(out=out_d[c * 128:(c + 1) * 128, 0:512], in_=H[:, c, :]), reads=[t_H[c]], dma=t_H[c])
            p.final.append(("sp", o.sem, o.val))
        return nc, p

    x_old = [t_gu, t_gv, t_t1, t_vn, t_Wk, t_cwbc, t_sgk] + t_ya + t_og + t_cv + t_sgq + t_qk + t_sw
    mng = xf(0, 1024); ln1g = xf(1024, 1024); ln1b = xf(2048, 1024); g1bc = xf(3072, 1024)
    yab = xf(4096, 1024); zt = xf(5120, 1024); ogt = xf(6144, 512)
    diag = yab
    YB = A_X + 2 * 6656
    ybf_q = [ARENA[:, YB:YB + 1024], ARENA[:, A_X + 2 * 6144:A_X + 2 * 6144 + 1024]]
    yT = ARENA[:, YB + 1024:YB + 2048].rearrange("p (a b) -> p a b", a=8)
    t_mng = T("mng"); t_ln1g = T("ln1g"); t_ln1b = T("ln1b"); t_g1bc = T("g1bc"); t_yab = T("yab"); t_zt = T("zt")
    t_ogt = T("ogt"); t_diag = t_yab; t_ybf_q = [T("ybf0"), T("ybf1")]; t_yT = T("yT")
    x_new = [t_mng, t_ln1g, t_ln1b, t_g1bc, t_yab, t_zt, t_ogt, t_yT] + t_ybf_q
    fence = sb("fence", [128, 1])
    p.op("dve", lambda e: e.memset(fence[:], 0.0), writes=x_old + x_new)
    p.op("sp", lambda e: e.dma_start(out=mng, in_=mng_d.partition_broadcast(128)), writes=[t_mng], dma=t_mng)
    p.op("sp", lambda e: e.dma_start(out=ln1g, in_=ln1g_d.partition_broadcast(128)), writes=[t_ln1g], dma=t_ln1g)
    p.op("sp", lambda e: e.dma_start(out=ln1b, in_=ln1b_d.partition_broadcast(128)), writes=[t_ln1b], dma=t_ln1b)

    def bcast_row(base, dst, tdst, tdiag, dg):
        dg3 = dg.rearrange("p (a b) -> p a b", a=8)
        for k in range(8):
            p.op("dve", lambda e, k=k: e.tensor_scalar_mul(out=dg3[:, k, :], in0=ident, scalar1=mod[:, base + k, 0:1]),
                 reads=[t_cst, t_mod], writes=[tdiag])
        for hlf in range(2):
            for k4 in range(4):
                k = hlf * 4 + k4
                p.op("pe", lambda e, k=k, k4=k4, hlf=hlf: e.matmul(PB[2 + hlf][:, k4 * 128:(k4 + 1) * 128], lhsT=ones, rhs=dg3[:, k, :], start=True, stop=True),
                     reads=[t_cst, tdiag], writes=[PT[2 + hlf]])
            p.op("act", lambda e, hlf=hlf: e.activation(out=dst[:, hlf * 512:(hlf + 1) * 512], in_=PB[2 + hlf][:], func=AF.Copy),
                 reads=[PT[2 + hlf]], writes=[tdst])

    bcast_row(16, g1bc, t_g1bc, t_diag, diag)
    PB6b = PB[6].bitcast(BF16)

    def layer_norm_store(z, tz, g_ap, tg, b_ap, tb, dst_d, extra_w=(), aff="pool", priv=None):
        st_, t_st_, mv_, t_mv_, rs_, t_rs_ = priv if priv is not None else (stats, t_stats, mv, t_mv, rstd, t_rstd)
        for c in range(2):
            p.op("dve", lambda e, c=c: e.bn_stats(out=st_[:, c, :], in_=z[:, c * 512:(c + 1) * 512]), reads=[tz], writes=[t_st_])
        p.op("dve", lambda e: e.bn_aggr(out=mv_[:, 0, :], in_=st_[:, 0:2, :]), reads=[t_st_], writes=[t_mv_])
        rstd_of(1, mv_, t_mv_, rs_, t_rs_)
        p.op("dve", lambda e: e.tensor_scalar(out=z, in0=z, scalar1=mv_[:, 0, 0:1], scalar2=rs_[:, 0:1], op0=ALU.subtract, op1=ALU.mult),
             reads=[tz, t_mv_, t_rs_], writes=[tz])
        p.op(aff, lambda e: e.tensor_tensor(out=z, in0=z, in1=g_ap, op=ALU.mult), reads=[tz, tg], writes=[tz])
        p.op(aff, lambda e: e.tensor_tensor(out=z, in0=z, in1=b_ap, op=ALU.add), reads=[tz, tb], writes=[tz])
        return p.op("sp", lambda e: e.dma_start(out=dst_d, in_=z), reads=[tz], writes=list(extra_w), dma=tz)

    t_x1 = [T("x1_%d" % i) for i in range(NT)]
    ya_in = [PA[:, 3088 + i * 512:3088 + (i + 1) * 512] for i in range(2)]; t_ya_in = [T("ya_in%d" % i) for i in range(2)]
    og_in = [PA[:, 4112 + i * 512:4112 + (i + 1) * 512] for i in range(2)]; t_og_in = [T("og_in%d" % i) for i in range(2)]

    def loads_yaog(i):
        s2 = i % 2
        p.op("sp", lambda e: e.dma_start(out=ya_in[s2][:], in_=ya_d[i]), reads=[t_yad[i]], writes=[t_ya_in[s2]] + t_wa3, dma=t_ya_in[s2])
        p.op("sp", lambda e: e.dma_start(out=og_in[s2][:], in_=og_d[i]), reads=[t_ogd[i]], writes=[t_og_in[s2]] + t_wa3, dma=t_og_in[s2])

    def loads_x(i):
        s2 = i % 2
        p.op("sp", lambda e: e.dma_start(out=xt[s2][:], in_=x_d[i * 128:(i + 1) * 128, :]), writes=[t_xt[s2]], dma=t_xt[s2])

    def headnorm(i):
        s2 = i % 2
        src = [(ya_in[s2][:, g * 128:(g + 1) * 128], t_ya_in[s2]) for g in range(4)] + [(H[:, i, g * 128:(g + 1) * 128], t_H[i]) for g in range(4)]
        for g in range(8):
            p.op("dve", lambda e, g=g: e.bn_stats(out=stats[:, g, :], in_=src[g][0]), reads=[src[g][1]], writes=[t_stats])
        for g in range(8):
            p.op("dve", lambda e, g=g: e.bn_aggr(out=mv[:, g, :], in_=stats[:, g, :]), reads=[t_stats], writes=[t_mv])
        rstd_of(8)
        for g in range(8):
            p.op("dve", lambda e, g=g: e.tensor_scalar(out=yab[:, g * 128:(g + 1) * 128], in0=src[g][0],
                                                       scalar1=mv[:, g, 0:1], scalar2=rstd[:, g:g + 1], op0=ALU.subtract, op1=ALU.mult),
                 reads=[src[g][1], t_mv, t_rstd], writes=[t_yab])
        p.op("dve", lambda e: e.tensor_tensor(out=ybf_q[s2][:, 0:512], in0=yab[:, 0:512], in1=mng[:, 0:512], op=ALU.mult), reads=[t_yab, t_mng], writes=[t_ybf_q[s2]])
        p.op("dve", lambda e: e.tensor_tensor(out=yab[:, 512:1024], in0=yab[:, 512:1024], in1=mng[:, 512:1024], op=ALU.mult), reads=[t_yab, t_mng], writes=[t_yab])
        p.op("dve", lambda e: e.tensor_tensor(out=ybf_q[s2][:, 512:1024], in0=yab[:, 512:1024], in1=og_in[s2][:], op=ALU.mult), reads=[t_yab, t_og_in[s2]], writes=[t_ybf_q[s2]])

    st1 = sb("st1", [128, 2, 6]); mv1 = sb("mv1", [128, 1, 2]); rs1 = sb("rs1", [128, 1])
    ln1priv = (st1, T("st1"), mv1, T("mv1"), rs1, T("rs1"))

    def out_mm(i):
        s2 = i % 2
        for k in range(8):
            p.op("pe", lambda e, k=k: e.transpose(PB6b[:, k * 128:(k + 1) * 128], ybf_q[s2][:, k * 128:(k + 1) * 128], identb), reads=[t_ybf_q[s2], t_cstb], writes=[PT[6]])
        p.op("act", lambda e: e.activation(out=yT.rearrange("p a b -> p (a b)"), in_=PB6b[:, 0:1024], func=AF.Copy), reads=[PT[6]], writes=[t_yT])
        for hlf in range(2):
            for k in range(8):
                p.op("pe", lambda e, k=k, hlf=hlf: e.matmul(PB[2 + 2 * s2 + hlf][:], lhsT=yT[:, k, :], rhs=w_out[:, k, hlf * 512:(hlf + 1) * 512],
                                                            start=(k == 0), stop=(k == 7)), reads=[t_yT, t_wout], writes=[PT[2 + 2 * s2 + hlf]])

    def out_ln(i):
        s2 = i % 2
        for hlf in range(2):
            p.op("dve", lambda e, hlf=hlf: e.tensor_tensor(out=zt[:, hlf * 512:(hlf + 1) * 512], in0=PB[2 + 2 * s2 + hlf][:], in1=g1bc[:, hlf * 512:(hlf + 1) * 512], op=ALU.mult),
                 reads=[PT[2 + 2 * s2 + hlf], t_g1bc], writes=[t_zt])
        p.op("dve", lambda e: e.scalar_tensor_tensor(out=zt, in0=xt[s2][:], scalar=ALPHA, in1=zt, op0=ALU.mult, op1=ALU.add),
             reads=[t_xt[s2], t_zt], writes=[t_zt])
        dst = out_d[i * 128:(i + 1) * 128, :] if DBG == "X1" else x1s_d[i * 128:(i + 1) * 128, :]
        return layer_norm_store(zt, t_zt, ln1g, t_ln1g, ln1b, t_ln1b, dst, extra_w=[t_x1[i]], aff="pool", priv=ln1priv)

    loads_yaog(0); loads_x(0)
    if NT > 1:
        loads_yaog(1); loads_x(1)
    headnorm(0)
    if NT > 2:
        loads_yaog(2)
    out_mm(0)
    for i in range(NT):
        if i + 1 < NT:
            headnorm(i + 1)
            if i + 3 < NT:
                loads_yaog(i + 3)
            out_mm(i + 1)
        o = out_ln(i)
        if i + 2 < NT:
            loads_x(i + 2)
    if DBG == "X1":
        return nc, p

    everything = ([t_win, t_wout] + t_hT_all + [ t_Wk, t_cwbc, t_sgk] + t_ktok + t_vext + t_H + x_old + x_new)
    ln2g = ARENA32[:, A_SP // 2:A_SP // 2 + 1024]; ln2b = ARENA32[:, A_SP // 2 + 1024:A_SP // 2 + 2048]
    g2bc = ARENA32[:, A_SP // 2 + 2048:A_SP // 2 + 3072]
    gT = ARENA[:, A_SP + 6144:A_SP + 6144 + NFC * 128].rearrange("p (a b) -> p a b", a=NFC)
    t_ln2g = T("ln2g"); t_ln2b = T("ln2b"); t_g2bc = T("g2bc"); t_gT = T("gT")
    t_fenceB = T("fenceB")
    newB = [t_ln2g, t_ln2b, t_g2bc, t_gT, t_fenceB]
    p.op("dve", lambda e: e.memset(fence[:], 0.0), writes=everything + newB)
    for k in range(5):
        p.op("pool", lambda e, k=k: e.dma_start(out=w_up[:, k, :], in_=w_up_d[k * 128:(k + 1) * 128, :]), reads=[t_fenceB], writes=[t_wup], dma=t_wup)
    for k in range(14, NFC):
        p.op("pool", lambda e, k=k: e.dma_start(out=w_dn[:, k, :], in_=w_dn_d[k * 128:(k + 1) * 128, :]), reads=[t_fenceB], writes=[t_wdn], dma=t_wdn)
    p.op("sp", lambda e: e.dma_start(out=ln2g, in_=ln2g_d.partition_broadcast(128)), writes=[t_ln2g], dma=t_ln2g)
    p.op("sp", lambda e: e.dma_start(out=ln2b, in_=ln2b_d.partition_broadcast(128)), writes=[t_ln2b], dma=t_ln2b)
    bcast_row(40, g2bc, t_g2bc, t_xt[0], xt[0][:])
    h2T_q = [ARENA[:, A_SP + 6144:A_SP + 6144 + 2048].rearrange("p (a b) -> p a b", a=8), PAb[:, 6144:8192].rearrange("p (a b) -> p a b", a=8)]
    t_h2T_q = [T("h2T0"), T("h2T1")]
    gTf = [ARENA[:, A_SP + 8192 + i * 256:A_SP + 8192 + (i + 1) * 256] for i in range(2)]; t_gTf = [T("gTf%d" % i) for i in range(2)]
    tcv = [sb("tcv%d" % i, [128, 256]) for i in range(4)]; t_tcv = [T("tcv%d" % i) for i in range(4)]
    sgl = [sb("sgl%d" % i, [128, 256]) for i in range(2)]; t_sgl = [T("sgl%d" % i) for i in range(2)]
    zt2 = sb("zt2", [128, D])
    zt_tt = [zt2[:], PA[:, 2048:3072]]; t_zt_tt = [T("zt2a"), T("zt2b")]
    xt_q = [[xt[0][:], xt[1][:]], [PA[:, 0:1024], PA[:, 1024:2048]]]
    t_xt_q = [[t_xt[0], t_xt[1]], [T("xt2"), T("xt3")]]
    deadA = t_S + [t_ctxS, t_GW] + t_ya_in + t_og_in
    p.op("dve", lambda e: e.memset(fence[:], 0.0), writes=deadA + [t_gT] + t_h2T_q + t_gTf + t_zt_tt + t_xt_q[1])
    ACC = [[0, 1], [6, 7]]

    def ffn_loads(ip):
        q = ip % 2
        for tt, i in enumerate((2 * ip, 2 * ip + 1)):
            p.op("sp", lambda e, tt=tt, i=i: e.dma_start(out=xt_q[q][tt], in_=x1s_d[i * 128:(i + 1) * 128, :]), reads=[t_x1[i]], writes=[t_xt_q[q][tt]], dma=t_xt_q[q][tt])

    def ffn_transposes(ip):
        q = ip % 2
        for tt in range(2):
            for half in range(2):
                pbk = 2 + 2 * tt + half
                for k4 in range(4):
                    k = half * 4 + k4
                    p.op("pe", lambda e, k=k, k4=k4, pbk=pbk, tt=tt: e.transpose(PB[pbk][:, k4 * 128:(k4 + 1) * 128], xt_q[q][tt][:, k * 128:(k + 1) * 128], ident),
                         reads=[t_xt_q[q][tt], t_cst], writes=[PT[pbk]])
                for k4 in range(4):
                    k = half * 4 + k4
                    p.op("act", lambda e, k=k, k4=k4, pbk=pbk, tt=tt: e.activation(out=h2T_q[q][:, k, tt * 128:(tt + 1) * 128], in_=PB[pbk][:, k4 * 128:(k4 + 1) * 128],
                                                                                func=AF.Identity, bias=mod[:, 24 + k, 0:1], scale=mod[:, 32 + k, 0:1]),
                         reads=[PT[pbk], t_mod], writes=[t_h2T_q[q]])

    def ffn_fcloop(ip):
        q = ip % 2
        h2T = h2T_q[q]; t_h2T = t_h2T_q[q]

        def up(fc):
            par = fc % 2
            for which in range(2):
                ch = fc + which * NFC
                pbk = 2 + 2 * par + which
                for k in range(8):
                    p.op("pe", lambda e, k=k, ch=ch, pbk=pbk: e.matmul(PB[pbk][:, 0:256], lhsT=w_up[:, k, ch * 128:(ch + 1) * 128], rhs=h2T[:, k, :],
                                                                      start=(k == 0), stop=(k == 7)), reads=[t_wup, t_h2T], writes=[PT[pbk]])

        def ew(fc):
            par = fc % 2
            for which in range(2):
                ch = fc + which * NFC
                pbk = 2 + 2 * par + which
                tb = 2 * par + which
                Pv = PB[pbk][:, 0:256]
                P3 = Pv.rearrange("p (r w) -> p r w", w=64)
                t3 = tcv[tb][:].rearrange("p (r w) -> p r w", w=64)
                p.op("act", lambda e, ch=ch, Pv=Pv, tb=tb: e.activation(out=tcv[tb][:], in_=Pv, func=AF.Identity, bias=fcb[:, ch:ch + 1], scale=fcw[:, ch, 1:2]),
                     reads=[PT[pbk], t_fcw, t_fcb], writes=[t_tcv[tb]])
                p.op("dve", lambda e, ch=ch, P3=P3, t3=t3, pbk=pbk, tb=tb: e.scalar_tensor_tensor(out=t3[:, :, 1:64], in0=P3[:, :, 0:63], scalar=fcw[:, ch, 0:1], in1=t3[:, :, 1:64],
                                                                                  op0=ALU.mult, op1=ALU.add), reads=[PT[pbk], t_fcw, t_tcv[tb]], writes=[t_tcv[tb]])
                p.op("dve", lambda e, ch=ch, P3=P3, t3=t3, pbk=pbk, tb=tb: e.scalar_tensor_tensor(out=t3[:, :, 0:63], in0=P3[:, :, 1:64], scalar=fcw[:, ch, 2:3], in1=t3[:, :, 0:63],
                                                                                  op0=ALU.mult, op1=ALU.add), reads=[PT[pbk], t_fcw, t_tcv[tb]], writes=[t_tcv[tb]])
            tv, tg = 2 * par, 2 * par + 1
            p.op("act", lambda e: e.activation(out=sgl[par][:], in_=tcv[tg][:], func=AF.Silu), reads=[t_tcv[tg]], writes=[t_sgl[par]])
            p.op("dve", lambda e: e.tensor_tensor(out=gTf[par], in0=tcv[tv][:], in1=sgl[par][:], op=ALU.mult),
                 reads=[t_tcv[tv], t_sgl[par]], writes=[t_gTf[par]])

        def wd(fc):
            par = fc % 2
            for tt in range(2):
                for hlf in range(2):
                    bk = ACC[tt][hlf]
                    p.op("pe", lambda e, tt=tt, hlf=hlf, bk=bk: e.matmul(PB[bk][:], lhsT=gTf[par][:, tt * 128:(tt + 1) * 128],
                                                                         rhs=w_dn[:, fc, hlf * 512:(hlf + 1) * 512], start=(fc == 0), stop=(fc == NFC - 1)),
                         reads=[t_gTf[par], t_wdn], writes=[PT[bk]])

        up(0)
        for fc in range(NFC):
            if fc + 1 < NFC:
                up(fc + 1)
            ew(fc)
            wd(fc)
            if ip + 1 < NT // 2:
                if fc == 2:
                    ffn_loads(ip + 1)

    def ffn_epilogue(ip):
        q = ip % 2
        for tt, i in enumerate((2 * ip, 2 * ip + 1)):
            z = zt_tt[tt]; tz = t_zt_tt[tt]
            for hlf in range(2):
                bk = ACC[tt][hlf]
                p.op("dve", lambda e, hlf=hlf, bk=bk, z=z: e.tensor_tensor(out=z[:, hlf * 512:(hlf + 1) * 512], in0=PB[bk][:], in1=g2bc[:, hlf * 512:(hlf + 1) * 512], op=ALU.mult),
                     reads=[PT[bk], t_g2bc], writes=[tz])
            p.op("dve", lambda e, tt=tt, z=z: e.scalar_tensor_tensor(out=z, in0=xt_q[q][tt], scalar=ALPHA, in1=z, op0=ALU.mult, op1=ALU.add),
                 reads=[t_xt_q[q][tt], tz], writes=[tz])
        for tt, i in enumerate((2 * ip, 2 * ip + 1)):
            layer_norm_store(zt_tt[tt], t_zt_tt[tt], ln2g, t_ln2g, ln2b, t_ln2b, out_d[i * 128:(i + 1) * 128, :])

    ffn_loads(0)
    ffn_transposes(0)
    for ip in range(NT // 2):
        ffn_fcloop(ip)
        if ip + 1 < NT // 2:
            ffn_transposes(ip + 1)
        ffn_epilogue(ip)
    return nc, p


def _consts():
    i = np.arange(128)
    ident = np.eye(128, dtype=np.float32)
    triU = (i[:, None] <= i[None, :]).astype(np.float32)
    triL = (i[:, None] >= i[None, :]).astype(np.float32)
    ones = np.ones((128, 128), np.float32)
    return np.ascontiguousarray(np.stack([ident, triU, triL, ones], axis=1))


def _pcol(v, nch):
    return np.ascontiguousarray(np.asarray(v, np.float32).reshape(nch, 128).T)


def make_in_maps(x, c, ctx, c_ctx, w_ada, b_ada, w_in, gmlp_ln_g, gmlp_ws, gmlp_bs, qk_conv_w, qk_conv_b,
                 b_igate, b_fgate, mix_norm_g, w_out, ln1_g, ln1_b, w_up, ffn_conv_w, ffn_conv_b, w_down, ln2_g, ln2_b):
    f = lambda a: np.ascontiguousarray(np.asarray(a, np.float32))
    x = f(x); ctx = f(ctx); w_in0 = f(w_in)[0]
    zero = np.zeros((1, D), np.float32)
    GOFF = 3072
    shared = {
        "cst": _consts(),
        "w_ada": f(w_ada)[0], "b_adaT": _pcol(f(b_ada)[0], 48), "w_in": w_in0,
        "cbk": f(qk_conv_b)[0, 512:1024],
        "gmlp_ln_g": f(gmlp_ln_g)[0].reshape(512),
        "wsT": np.ascontiguousarray(f(gmlp_ws)[0].transpose(2, 0, 1)),
        "bsT": np.ascontiguousarray(f(gmlp_bs)[0].T),
        "qkcw": np.ascontiguousarray(f(qk_conv_w)[0].reshape(3, 8, 128).transpose(2, 1, 0)),
        "qkcb": _pcol(f(qk_conv_b)[0], 8),
        "gbo": np.concatenate([f(b_igate)[0].reshape(8), f(b_fgate)[0].reshape(8)]),
        "mix_norm_g": f(mix_norm_g)[0], "w_out": f(w_out)[0], "ln1_g": f(ln1_g)[0], "ln1_b": f(ln1_b)[0],
        "w_up": f(w_up)[0],
        "fcw": np.ascontiguousarray(f(ffn_conv_w)[0].reshape(3, 42, 128).transpose(2, 1, 0)),
        "fcb": _pcol(f(ffn_conv_b)[0], 42),
        "w_down": f(w_down)[0], "ln2_g": f(ln2_g)[0], "ln2_b": f(ln2_b)[0],
    }
    taps = f(qk_conv_w)[0][:, 512:1024]
    bi = f(b_igate)[0]; bf_ = f(b_fgate)[0]

    def dir_params(d):
        wg = np.concatenate([w_in0[:, GOFF + d * 4:GOFF + d * 4 + 4], w_in0[:, GOFF + 8 + d * 4:GOFF + 8 + d * 4 + 4]], axis=1)
        gb = np.concatenate([bi[d], bf_[d]])
        cw = taps if d == 0 else taps[::-1]
        return wg, gb, cw

    maps = []
    for core in range(NCORE):
        b, r = divmod(core, 4)
        T0 = r * TOK
        xb = x[b]
        m = dict(shared)
        m["x"] = np.ascontiguousarray(xb[T0:T0 + TOK])
        lo = xb[T0 - 1:T0] if T0 > 0 else zero
        hi = xb[T0 + TOK:T0 + TOK + 1] if T0 + TOK < SEQ else zero
        m["xh"] = np.ascontiguousarray(np.concatenate([lo, hi], 0))
        flags = np.zeros((128, 16), np.float32)
        flags[:, 0] = 1.0 if T0 > 0 else 0.0
        flags[:, 1] = 1.0 if T0 + TOK < SEQ else 0.0
        m["ctxf"] = np.ascontiguousarray(np.concatenate([zero, ctx[b], zero], 0))
        m["ctxr"] = np.ascontiguousarray(np.concatenate([zero, ctx[b][::-1], zero], 0))
        dirs = [0, 1]
        segs = [(j, 0) for j in range(r)] + [(j, 1) for j in range(3, r, -1)]
        for k, (j, d) in enumerate(segs):
            s0 = j * TOK
            plo = xb[s0 - 1:s0] if s0 > 0 else zero
            phi = xb[s0 + TOK:s0 + TOK + 1] if s0 + TOK < SEQ else zero
            flo = 1.0 if s0 > 0 else 0.0
            fhi = 1.0 if s0 + TOK < SEQ else 0.0
            if d == 0:
                rows = np.concatenate([plo, xb[s0:s0 + TOK], phi], 0)
                flags[:, 6 + 2 * k] = flo; flags[:, 7 + 2 * k] = fhi
            else:
                rows = np.concatenate([phi, xb[s0:s0 + TOK][::-1], plo], 0)
                flags[:, 6 + 2 * k] = fhi; flags[:, 7 + 2 * k] = flo
            m["xo%d" % k] = np.ascontiguousarray(rows)
            dirs.append(d)
        flags[:, 12 + r] = 1.0
        m["flags"] = flags
        ps = [dir_params(d) for d in dirs]
        m["wg"] = np.ascontiguousarray(np.stack([q[0] for q in ps]))
        m["gb"] = np.ascontiguousarray(np.stack([q[1] for q in ps]))
        m["cwk"] = np.ascontiguousarray(np.stack([q[2] for q in ps]))
        cT = np.stack([f(c)[b], f(c_ctx)], axis=1)
        m["cT"] = np.ascontiguousarray(cT.reshape(8, 128, 2).transpose(1, 0, 2))
        maps.append(m)
    return maps


_NC_CACHE = {}


def kernel(**inputs):
    if "nc" not in _NC_CACHE:
        nc, p = build_nc()
        p.emit()
        _NC_CACHE["nc"] = nc
    nc = _NC_CACHE["nc"]
    maps = make_in_maps(**inputs)
    res = run_bass_kernel_spmd(nc, maps, core_ids=list(range(NCORE)))
    out = np.zeros((2, SEQ, D), np.float32)
    for core in range(NCORE):
        b, r = divmod(core, 4)
        out[b, r * TOK:(r + 1) * TOK] = res.results[core]["out"]
    return out
```

```python
import numpy as np
import concourse.bass as bass
import concourse.mybir as mybir
from concourse.bass_utils import run_bass_kernel_spmd

F32 = mybir.dt.float32
BF16 = mybir.dt.bfloat16
AF = mybir.ActivationFunctionType
ALU = mybir.AluOpType

D = 1024
SEQ = 8192
NCORE = 8
TOK = 2048
NT = 16
CTX = 256
N_IN = 3088
DFF = 2688
NFC = 21
ALPHA = 2.0 ** 0.25
EPS = 1e-5
LN_QSCALE = float(np.log(128.0 ** -0.5))
GC = 1.5957691216057308

ENGS = ("pe", "act", "dve", "pool", "sp")


class T:
    __slots__ = ("name", "w", "rs", "dsem", "dcnt")

    def __init__(self, name):
        self.name = name
        self.w = None
        self.rs = []
        self.dsem = None
        self.dcnt = 0


class Op:
    __slots__ = ("eng", "fn", "deps", "is_dma", "sem", "val", "need_inc", "idx")

    def __init__(self, eng, fn, is_dma):
        self.eng = eng
        self.fn = fn
        self.deps = []
        self.is_dma = is_dma
        self.sem = None
        self.val = None
        self.need_inc = False


class Prog:
    def __init__(self, nc):
        self.nc = nc
        self.q = {e: [] for e in ENGS}
        self.esem = {}
        self.final = []
        self.nsem = 0
        self.dma_tiles = {}

    def _newsem(self, name):
        self.nsem += 1
        return self.nc.alloc_semaphore(name)

    def op(self, eng, fn, reads=(), writes=(), dma=None):
        o = Op(eng, fn, dma is not None)
        deps = []
        for t in reads:
            if t.w is not None:
                deps.append(t.w)
        for t in writes:
            if t.w is not None:
                deps.append(t.w)
            deps.extend(t.rs)
        best = {}
        for d in deps:
            if d.is_dma:
                key = ("dma", d.sem.num)
                if key not in best or d.val > best[key].val:
                    best[key] = d
            else:
                if eng == "pe" and d.eng == "pe":
                    continue
                key = ("eng", d.eng)
                if key not in best or d.idx > best[key].idx:
                    best[key] = d
        for d in best.values():
            o.deps.append(d)
            d.need_inc = True
        if dma is not None:
            if dma.dsem is None:
                dma.dsem = self._newsem("d_" + dma.name)
            dma.dcnt += 16
            self.dma_tiles[id(dma)] = dma
            o.sem = dma.dsem
            o.val = dma.dcnt
            o.need_inc = True
        for t in reads:
            t.rs.append(o)
        for t in writes:
            t.w = o
            t.rs = []
        o.idx = len(self.q[eng])
        self.q[eng].append(o)
        return o

    def emit(self):
        nc = self.nc
        for e in ENGS:
            cnt = 0
            for o in self.q[e]:
                if o.is_dma or not o.need_inc:
                    continue
                if e not in self.esem:
                    self.esem[e] = self._newsem("e_" + e)
                cnt += 1
                o.sem = self.esem[e]
                o.val = cnt
        me = self

        def run(e, eng):
            waited = {}
            for o in me.q[e]:
                for d in o.deps:
                    k = d.sem.num
                    if waited.get(k, 0) >= d.val:
                        continue
                    eng.wait_ge(d.sem, d.val)
                    waited[k] = d.val
                ins = o.fn(eng)
                if o.need_inc:
                    ins.then_inc(o.sem, 16 if o.is_dma else 1)
            if e == "sp":
                for t in me.dma_tiles.values():
                    eng.wait_ge(t.dsem, t.dcnt)

        with nc.Block() as block:
            @block.sync
            def _(eng):
                run("sp", eng)

            @block.tensor
            def _(eng):
                run("pe", eng)

            @block.scalar
            def _(eng):
                run("act", eng)

            @block.vector
            def _(eng):
                run("dve", eng)

            @block.gpsimd
            def _(eng):
                run("pool", eng)


class Ring:
    def __init__(self, items):
        self.items = items
        self.i = 0

    def next(self):
        it = self.items[self.i % len(self.items)]
        self.i += 1
        return it


def build_nc():
    nc = bass.Bass("TRN2", target_bir_lowering=False)
    p = Prog(nc)

    def din(name, shape):
        return nc.dram_tensor(name, list(shape), F32, kind="ExternalInput").ap()

    x_d = din("x", [TOK, D])
    xh_d = din("xh", [2, D])
    xs_d = [din("ctxf", [CTX + 2, D]), din("ctxr", [CTX + 2, D]),
            din("xo0", [TOK + 2, D]), din("xo1", [TOK + 2, D]), din("xo2", [TOK + 2, D])]
    SL_NT = [2, 2, 16, 16, 16]
    SL_VEC = [1, 1, 0, 0, 0]
    cT_d = din("cT", [128, 8, 2])
    cst_d = din("cst", [128, 4, 128])
    flags_d = din("flags", [128, 16])
    w_ada_d = din("w_ada", [D, 6 * D])
    b_ada_d = din("b_adaT", [128, 48])
    w_in_d = din("w_in", [D, N_IN])
    wg_d = din("wg", [5, D, 8])
    gb_d = din("gb", [5, 8])
    cw_d = din("cwk", [5, 3, 512])
    cbk_d = din("cbk", [512])
    lng_d = din("gmlp_ln_g", [512])
    wsT_d = din("wsT", [128, 4, 128])
    bsT_d = din("bsT", [128, 4])
    qkw_d = din("qkcw", [128, 8, 3])
    qkb_d = din("qkcb", [128, 8])
    gbo_d = din("gbo", [16])
    mng_d = din("mix_norm_g", [D])
    w_out_d = din("w_out", [D, D])
    ln1g_d = din("ln1_g", [D]); ln1b_d = din("ln1_b", [D])
    w_up_d = din("w_up", [D, 2 * DFF])
    fcw_d = din("fcw", [128, 42, 3])
    fcb_d = din("fcb", [128, 42])
    w_dn_d = din("w_down", [DFF, D])
    ln2g_d = din("ln2_g", [D]); ln2b_d = din("ln2_b", [D])
    out_d = nc.dram_tensor("out", [TOK, D], F32, kind="ExternalOutput").ap()
    x1s_d = nc.dram_tensor("x1s", [TOK, D], F32).ap()

    def sb(name, shape, dt=F32):
        return nc.alloc_sbuf_tensor("s_" + name, list(shape), dt)

    PBALL = nc.alloc_psum_tensor("pball", [128, 4096], F32)
    PB = [PBALL[:, i * 512:(i + 1) * 512] for i in range(8)]
    PT = [T("pb%d" % i) for i in range(8)]

    cst = sb("cst", [128, 4, 128]); t_cst = T("cst")
    ident = cst[:, 0, :]; triU = cst[:, 1, :]; triL = cst[:, 2, :]; ones = cst[:, 3, :]
    cstb = sb("cstb", [128, 4, 128], BF16); t_cstb = T("cstb")
    identb = cstb[:, 0, :]; maskU = cstb[:, 1, :]; maskL = cstb[:, 2, :]; onesb = cstb[:, 3, :]
    flags = sb("flags", [128, 16]); t_flags = T("flags")
    p.op("sp", lambda e: e.dma_start(out=cst[:], in_=cst_d), writes=[t_cst], dma=t_cst)
    p.op("sp", lambda e: e.dma_start(out=flags[:], in_=flags_d), writes=[t_flags], dma=t_flags)
    p.op("dve", lambda e: e.tensor_copy(out=cstb[:], in_=cst[:]), reads=[t_cst], writes=[t_cstb])

    def load_const(name, shape, src):
        tl = sb(name, shape); tt = T(name)
        p.op("sp", lambda e: e.dma_start(out=tl[:], in_=src), writes=[tt], dma=tt)
        return tl, tt

    b_adaT, t_bada = load_const("b_adaT", [128, 48], b_ada_d)
    cT, t_cT = load_const("cT", [128, 8, 2], cT_d)
    bsT, t_bsT = load_const("bsT", [128, 4], bsT_d)
    qkw, t_qkw = load_const("qkw", [128, 8, 3], qkw_d)
    qkb, t_qkb = load_const("qkb", [128, 8], qkb_d)
    fcw, t_fcw = load_const("fcw", [128, 42, 3], fcw_d)
    fcb, t_fcb = load_const("fcb", [128, 42], fcb_d)
    gbo, t_gbo = load_const("gbo", [128, 16], gbo_d.partition_broadcast(128))
    gbs, t_gbs = load_const("gbs", [128, 40], gb_d.rearrange("a b -> (a b)").partition_broadcast(128))
    lng, t_lng = load_const("lng", [128, 512], lng_d.partition_broadcast(128))
    cbk32, t_cbk32 = load_const("cbk32", [1, 512], cbk_d.rearrange("(a b) -> a b", a=1))
    cbkb = sb("cbkb", [1, 512], BF16); t_cbkb = T("cbkb")
    p.op("dve", lambda e: e.tensor_copy(out=cbkb[:], in_=cbk32[:]), reads=[t_cbk32], writes=[t_cbkb])

    A_WIN, A_HT, A_KTOK, A_VEXT, A_X, A_END = 0, 24704, 41104, 49296, 57552, 73936
    ARENA = sb("arena", [128, A_END], BF16)
    ARENA32 = ARENA.bitcast(F32)
    t_win = T("w_in"); t_wout = T("w_out"); t_wup = T("w_up")
    w_in = ARENA[:, A_WIN:A_WIN + 8 * N_IN].rearrange("p (k n) -> p k n", k=8)
    wsTb = sb("wsTb", [128, 4, 128], BF16); t_wsTb = T("wsTb")
    wgb = sb("wgb", [128, 5, 8, 8], BF16); t_wgb = T("wgb")
    for k in range(8):
        p.op("pool", lambda e, k=k: e.dma_start(out=w_in[:, k, :], in_=w_in_d[k * 128:(k + 1) * 128, :]),
             writes=[t_win], dma=t_win)
    p.op("pool", lambda e: e.dma_start(out=wsTb[:], in_=wsT_d), writes=[t_wsTb], dma=t_wsTb)
    for s_ in range(5):
        p.op("pool", lambda e, s_=s_: e.dma_start(out=wgb[:, s_, :, :], in_=wg_d[s_].rearrange("(k p) n -> p k n", p=128)), writes=[t_wgb], dma=t_wgb)

    siluc = sb("siluc", [128, 8, 2]); t_siluc = T("siluc")
    p.op("act", lambda e: e.activation(out=siluc[:], in_=cT[:], func=AF.Sigmoid), reads=[t_cT], writes=[t_siluc])
    p.op("dve", lambda e: e.tensor_tensor(out=siluc[:], in0=siluc[:], in1=cT[:], op=ALU.mult), reads=[t_siluc, t_cT], writes=[t_siluc])
    mod = sb("mod", [128, 48, 2]); t_mod = T("mod")
    hT = ARENA[:, A_HT:A_HT + 8 * (TOK + 2)].rearrange("p (k n) -> p k n", k=8)
    t_hTt = [T("hT%d" % i) for i in range(NT)]; t_hTh = T("hTh"); t_hT_all = t_hTt + [t_hTh]
    wa = [ARENA32[:, A_HT // 2 + i * 4096:A_HT // 2 + (i + 1) * 4096].rearrange("p (k n) -> p k n", k=8) for i in range(2)]
    t_wa = [T("wa0"), T("wa1")]
    modrow = sb("modrow", [2, 512]); t_modrow = T("modrow")
    for blk in range(12):
        bi = blk % 2
        p.op("sp", lambda e, blk=blk, bi=bi: e.dma_start(
            out=wa[bi], in_=w_ada_d[:, blk * 512:(blk + 1) * 512].rearrange("(k p) n -> p k n", p=128)),
            writes=[t_wa[bi]], dma=t_wa[bi])
        pbk = blk % 2
        for k in range(8):
            p.op("pe", lambda e, bi=bi, k=k, pbk=pbk: e.matmul(PB[pbk][0:2, :], lhsT=siluc[:, k, :], rhs=wa[bi][:, k, :],
                                                              start=(k == 0), stop=(k == 7)), reads=[t_wa[bi], t_siluc], writes=[PT[pbk]])
        p.op("act", lambda e, pbk=pbk: e.activation(out=modrow[:], in_=PB[pbk][0:2, :], func=AF.Copy), reads=[PT[pbk]], writes=[t_modrow])
        for j in range(4):
            p.op("pe", lambda e, j=j, pbk=pbk: e.transpose(PB[2 + pbk][:, j * 2:j * 2 + 2], modrow[0:2, j * 128:(j + 1) * 128], ident[0:2, 0:2]),
                 reads=[t_modrow, t_cst], writes=[PT[2 + pbk]])
        p.op("dve", lambda e, blk=blk, pbk=pbk: e.tensor_tensor(
            out=mod[:, blk * 4:(blk + 1) * 4, :], in0=PB[2 + pbk][:, 0:8].rearrange("p (j v) -> p j v", v=2),
            in1=b_adaT[:, blk * 4:(blk + 1) * 4].unsqueeze(2).to_broadcast([128, 4, 2]), op=ALU.add),
            reads=[PT[2 + pbk], t_bada], writes=[t_mod])
    p.op("dve", lambda e: e.tensor_scalar_add(out=mod[:, 8:16, :], in0=mod[:, 8:16, :], scalar1=1.0), reads=[t_mod], writes=[t_mod])
    p.op("dve", lambda e: e.tensor_scalar_add(out=mod[:, 32:40, :], in0=mod[:, 32:40, :], scalar1=1.0), reads=[t_mod], writes=[t_mod])
    fence0 = sb("fence0", [128, 1])
    p.op("dve", lambda e: e.memset(fence0[:], 0.0), writes=t_wa + t_hT_all)
    xt = [sb("xt%d" % i, [128, D]) for i in range(2)]
    t_xt = [T("xt%d" % i) for i in range(2)]
    xt_ring = Ring(list(range(2)))
    ktok = ARENA[:, A_KTOK:A_KTOK + NT * 512].rearrange("p (a b) -> p a b", a=NT); t_ktok = [T("ktok%d" % i) for i in range(NT)]
    vext = ARENA[:, A_VEXT:A_VEXT + NT * 4 * 129].rearrange("p (a b c) -> p a b c", a=NT, b=4); t_vext = [T("vext%d" % i) for i in range(NT)]
    p.op("pool", lambda e: e.memset(ARENA[:, A_VEXT:A_VEXT + NT * 4 * 129], 1.0), writes=t_vext)
    PA = sb("pa", [128, 5136]); PAb = PA.bitcast(BF16)
    G = sb("G", [128, NT, 16]); t_G = T("G")
    GW = PA[:, 2064:3088].rearrange("p (a b c) -> p a b c", a=8, b=NT); t_GW = T("GW")
    hmod = sb("hmod", [128, 2, 8]); t_hmod = T("hmod")
    vs = [sb("vs%d" % i, [128, 4, 129], BF16) for i in range(2)]
    t_vs = [T("vs%d" % i) for i in range(2)]
    vs_ring = Ring(list(range(2)))
    S = PA[:, 0:1032].rearrange("p (a b) -> p a b", a=8); t_S = [T("S%d" % i) for i in range(8)]
    Sb = sb("Sb", [128, 8, 129], BF16); t_Sb = [T("Sb%d" % i) for i in range(8)]
    ctxS = PA[:, 1032:2064].rearrange("p (a b) -> p a b", a=8); t_ctxS = T("ctxS")
    initF = sb("initF", [128, 4, 129]); t_initF = T("initF")
    tmpS = sb("tmpS", [128, 4, 129]); t_tmpS = T("tmpS")
    Wk = ARENA[:, A_X:A_X + 12288].rearrange("p (a b c) -> p a b c", a=3, b=8); t_Wk = T("Wk")
    cwbc = ARENA32[:, (A_X + 12288) // 2:(A_X + 12288) // 2 + 1536].rearrange("p (a b) -> p a b", a=3); t_cwbc = T("cwbc")
    sgk = ARENA32[:, (A_X + 15360) // 2:(A_X + 15360) // 2 + 512]; t_sgk = T("sgk")

    kcol = 1536
    vcol = 2048

    def transposes(src_d, row0, nrows, col0, vec, gate=None, ti=0):
        xi = xt_ring.next()
        hi = hx_ring.next() if gate is not None else None
        p.op("sp", lambda e: e.dma_start(out=xt[xi][0:nrows, :], in_=src_d[row0:row0 + nrows, :]), writes=[t_xt[xi]], dma=t_xt[xi])
        for half in range(2):
            pbk = half
            for k4 in range(4):
                k = half * 4 + k4
                p.op("pe", lambda e, k=k, k4=k4, pbk=pbk: e.transpose(PB[pbk][:, k4 * 128:k4 * 128 + nrows], xt[xi][0:nrows, k * 128:(k + 1) * 128],
                                                                      ident[0:nrows, 0:nrows]),
                     reads=[t_xt[xi], t_cst], writes=[PT[pbk]])
            for k4 in range(4):
                k = half * 4 + k4
                p.op("act", lambda e, k=k, k4=k4, pbk=pbk: e.activation(
                    out=hT[:, k, col0:col0 + nrows], in_=PB[pbk][:, k4 * 128:k4 * 128 + nrows], func=AF.Identity,
                    bias=mod[:, k, vec:vec + 1], scale=mod[:, 8 + k, vec:vec + 1]),
                    reads=[PT[pbk], t_mod], writes=[t_hTt[ti]])
                if gate is not None:
                    p.op("act", lambda e, k=k, k4=k4, pbk=pbk: e.activation(
                        out=hx32[hi][:, k, 0:nrows], in_=PB[pbk][:, k4 * 128:k4 * 128 + nrows], func=AF.Identity,
                        bias=mod[:, k, vec:vec + 1], scale=mod[:, 8 + k, vec:vec + 1]),
                        reads=[PT[pbk], t_mod], writes=[t_hx32[hi]])
        if gate is not None:
            w32, t_w32, ng, dsts = gate
            for k in range(8):
                p.op("pe", lambda e, k=k: e.matmul(PB[4][:, 0:ng], lhsT=hx32[hi][:, k, 0:nrows], rhs=w32[:, k, :],
                                                  start=(k == 0), stop=(k == 7)), reads=[t_hx32[hi], t_w32], writes=[PT[4]])
            for (dst, c0, c1) in dsts:
                p.op("dve", lambda e, dst=dst, c0=c0, c1=c1: e.tensor_copy(out=dst, in_=PB[4][:, c0:c1]), reads=[PT[4]], writes=[t_G])

    def halos(src_rows, vec, fcol, ncols):
        xi = xt_ring.next()
        for r in range(2):
            p.op("sp", lambda e, r=r: e.dma_start(out=xt[xi][r:r + 1, :], in_=src_rows[r]), writes=[t_xt[xi]], dma=t_xt[xi])
        for r in range(2):
            p.op("dve", lambda e, r=r: e.tensor_scalar_mul(out=hmod[:, 0, :], in0=mod[:, 0:8, vec], scalar1=flags[:, fcol + r:fcol + r + 1]),
                 reads=[t_mod, t_flags], writes=[t_hmod])
            p.op("dve", lambda e, r=r: e.tensor_scalar_mul(out=hmod[:, 1, :], in0=mod[:, 8:16, vec], scalar1=flags[:, fcol + r:fcol + r + 1]),
                 reads=[t_mod, t_flags], writes=[t_hmod])
            for k in range(8):
                p.op("pe", lambda e, k=k: e.transpose(PB[0][:, k * 2:k * 2 + 2], xt[xi][0:2, k * 128:(k + 1) * 128], ident[0:2, 0:2]),
                     reads=[t_xt[xi], t_cst], writes=[PT[0]])
            c = 0 if r == 0 else ncols + 1
            for k in range(8):
                p.op("act", lambda e, k=k, r=r, c=c: e.activation(
                    out=hT[:, k, c:c + 1], in_=PB[0][:, k * 2 + r:k * 2 + r + 1], func=AF.Identity,
                    bias=hmod[:, 0, k:k + 1], scale=hmod[:, 1, k:k + 1]),
                    reads=[PT[0], t_hmod], writes=[t_hTh])

    def gate_math(nt, bi_ap, bf_ap, ncol):
        li = GW[:, 4, 0:nt, 0:ncol]; gf = GW[:, 5, 0:nt, 0:ncol]; bb = GW[:, 6, 0:nt, 0:ncol]; bl = GW[:, 7, 0:nt, 0:ncol]
        bc = lambda ap: ap.unsqueeze(1).to_broadcast([128, nt, ncol])
        p.op("dve", lambda e: e.tensor_tensor(out=li, in0=G[:, 0:nt, 0:ncol], in1=bc(bi_ap), op=ALU.add), reads=[t_G, t_gbo, t_gbs], writes=[t_GW])
        p.op("dve", lambda e: e.tensor_tensor(out=gf, in0=G[:, 0:nt, 8:8 + ncol], in1=bc(bf_ap), op=ALU.add), reads=[t_G, t_gbo, t_gbs], writes=[t_GW])
        p.op("act", lambda e: e.activation(out=gf, in_=gf, func=AF.Exp, scale=-1.0), reads=[t_GW], writes=[t_GW])
        p.op("act", lambda e: e.activation(out=gf, in_=gf, func=AF.Ln, bias=1.0), reads=[t_GW], writes=[t_GW])
        for h in range(ncol // 4):
            tri = triU if h == 0 else triL
            p.op("pe", lambda e, h=h, tri=tri: e.matmul(PB[6][:, h * 64:h * 64 + nt * 4], lhsT=tri, rhs=GW[:, 5, 0:nt, h * 4:h * 4 + 4],
                                                         start=True, stop=True), reads=[t_GW, t_cst], writes=[PT[6]])
        p.op("pe", lambda e: e.matmul(PB[6][:, 256:256 + nt * ncol], lhsT=ones, rhs=gf, start=True, stop=True), reads=[t_GW, t_cst], writes=[PT[6]])
        for h in range(ncol // 4):
            p.op("dve", lambda e, h=h: e.tensor_scalar_mul(out=GW[:, 6, 0:nt, h * 4:h * 4 + 4],
                                                           in0=PB[6][:, h * 64:h * 64 + nt * 4].rearrange("p (a b) -> p a b", b=4), scalar1=-1.0),
                 reads=[PT[6]], writes=[t_GW])
        p.op("dve", lambda e: e.tensor_scalar_mul(out=bl, in0=PB[6][:, 256:256 + nt * ncol].rearrange("p (a b) -> p a b", b=ncol), scalar1=-1.0),
             reads=[PT[6]], writes=[t_GW])
        p.op("dve", lambda e: e.tensor_tensor(out=li, in0=li, in1=bb, op=ALU.subtract), reads=[t_GW], writes=[t_GW])
        p.op("act", lambda e: e.activation(out=GW[:, 0, 0:nt, 0:ncol], in_=li, func=AF.Exp), reads=[t_GW], writes=[t_GW])
        p.op("act", lambda e: e.activation(out=GW[:, 1, 0:nt, 0:ncol], in_=bb, func=AF.Exp, bias=LN_QSCALE), reads=[t_GW], writes=[t_GW])
        p.op("dve", lambda e: e.tensor_tensor(out=li, in0=li, in1=bl, op=ALU.add), reads=[t_GW], writes=[t_GW])
        p.op("act", lambda e: e.activation(out=GW[:, 2, 0:nt, 0:ncol], in_=li, func=AF.Exp), reads=[t_GW], writes=[t_GW])
        p.op("act", lambda e: e.activation(out=GW[:, 3, 0:nt, 0:ncol], in_=bl, func=AF.Exp), reads=[t_GW], writes=[t_GW])

    def state_update(i, h, gcol, sidx, pbk, first=False):
        vi = vs_ring.next()
        p.op("act", lambda e: e.activation(out=vs[vi][:, h, :], in_=vext[:, i, h, :], func=AF.Copy, scale=GW[:, 2, i, gcol:gcol + 1]),
             reads=[t_vext[i], t_GW], writes=[t_vs[vi]])
        p.op("pe", lambda e: e.matmul(PB[pbk][:, (h % 2) * 256:(h % 2) * 256 + 129], lhsT=ktok[:, i, h * 128:(h + 1) * 128], rhs=vs[vi][:, h, :],
                                      start=True, stop=True), reads=[t_ktok[i], t_vs[vi]], writes=[PT[pbk]])
        p.op("dve", lambda e: e.scalar_tensor_tensor(out=S[:, sidx, :], in0=S[:, sidx, :], scalar=GW[:, 3, i, gcol:gcol + 1],
                                                     in1=PB[pbk][:, (h % 2) * 256:(h % 2) * 256 + 129], op0=ALU.mult, op1=ALU.add),
             reads=[t_S[sidx], t_GW, PT[pbk]], writes=[t_S[sidx]])

    p.op("pool", lambda e: e.memset(PA[:, 0:1032], 0.0), writes=t_S)
    p.op("pool", lambda e: e.memset(initF[:].rearrange("p a b -> p (a b)"), 0.0), writes=[t_initF])

    def chain_switch(kk):
        swc = flags[:, 12 + kk:13 + kk]
        s4 = S[:, 0:4, :]
        rd = [t_flags, t_initF, t_ctxS, t_tmpS] + t_S[0:4]
        p.op("dve", lambda e: e.scalar_tensor_tensor(out=initF[:], in0=s4, scalar=swc, in1=initF[:], op0=ALU.mult, op1=ALU.add),
             reads=rd, writes=[t_initF])
        p.op("dve", lambda e: e.tensor_tensor(out=tmpS[:], in0=ctxS[:, 4:8, :], in1=s4, op=ALU.subtract), reads=rd, writes=[t_tmpS])
        p.op("dve", lambda e: e.scalar_tensor_tensor(out=s4, in0=tmpS[:], scalar=swc, in1=s4, op0=ALU.mult, op1=ALU.add),
             reads=rd, writes=t_S[0:4])

    for sl in range(5):
        nt = SL_NT[sl]; ncols = nt * 128; vec = SL_VEC[sl]; src = xs_d[sl]
        p.op("sp", lambda e, sl=sl: e.dma_start(out=ARENA32[:, (A_X + 12288) // 2:(A_X + 12288) // 2 + 1536],
                                                  in_=cw_d[sl].rearrange("a b -> (a b)").partition_broadcast(128)),
             writes=[t_cwbc], dma=t_cwbc)
        n_b = 0
        for j in ((0, 1, 2) if sl == 0 else (0, 2)):
            for k in range(8):
                eng = "pool" if (n_b % 4 == 3) else "dve"
                n_b += 1
                p.op(eng, lambda e, j=j, k=k: e.tensor_tensor(out=Wk[:, j, k, :], in0=w_in[:, k, kcol:kcol + 512], in1=cwbc[:, j, :], op=ALU.mult),
                     reads=[t_win, t_cwbc], writes=[t_Wk])
        halos([src[0:1, :], src[ncols + 1:ncols + 2, :]], vec, 2 + 2 * sl, ncols)

        def slot_T(i, sl=sl, src=src, vec=vec):
            transposes(src, 1 + i * 128, 128, 1 + i * 128, vec, ti=i)

        def win(i, nt=nt):
            return [t_hTt[i], t_hTt[i - 1] if i > 0 else t_hTh, t_hTt[i + 1] if i < nt - 1 else t_hTh]
        if sl == 2:
            chain_switch(0)
        slot_T(0)
        if nt > 1:
            slot_T(1)
        for i in range(nt):
            if i + 2 < nt:
                slot_T(i + 2)
            n = 0
            for j in range(3):
                for k in range(8):
                    p.op("pe", lambda e, i=i, j=j, k=k, n=n: e.matmul(PB[2][:], lhsT=hT[:, k, i * 128 + j:i * 128 + j + 128], rhs=Wk[:, j, k, :],
                                                                       start=(n == 0), stop=False), reads=win(i) + [t_Wk], writes=[PT[2]])
                    n += 1
            p.op("pe", lambda e: e.matmul(PB[2][:], lhsT=onesb[0:1, :], rhs=cbkb[:], start=False, stop=True), reads=[t_cstb, t_cbkb], writes=[PT[2]])
            p.op("act", lambda e: e.activation(out=sgk, in_=PB[2][:], func=AF.Sigmoid), reads=[PT[2]], writes=[t_sgk])
            p.op("dve", lambda e, i=i: e.tensor_tensor(out=ktok[:, i, :], in0=PB[2][:], in1=sgk, op=ALU.mult), reads=[PT[2], t_sgk], writes=[t_ktok[i]])
            for k in range(8):
                p.op("pe", lambda e, i=i, k=k: e.matmul(PB[3][:], lhsT=hT[:, k, 1 + i * 128:1 + (i + 1) * 128], rhs=w_in[:, k, vcol:vcol + 512],
                                                         start=(k == 0), stop=(k == 7)), reads=[t_hTt[i], t_win], writes=[PT[3]])
            p.op("act", lambda e, i=i: e.activation(out=vext[:, i, :, 0:128], in_=PB[3][:].rearrange("p (a b) -> p a b", b=128), func=AF.Copy),
                 reads=[PT[3]], writes=[t_vext[i]])
            for k in range(8):
                p.op("pe", lambda e, i=i, k=k, sl=sl: e.matmul(PB[4][:, 0:8], lhsT=hT[:, k, 1 + i * 128:1 + (i + 1) * 128], rhs=wgb[:, sl, k, :],
                                                                start=(k == 0), stop=(k == 7)), reads=[t_hTt[i], t_wgb], writes=[PT[4]])
            p.op("dve", lambda e, i=i: e.tensor_copy(out=G[:, i, 0:4], in_=PB[4][:, 0:4]), reads=[PT[4]], writes=[t_G])
            p.op("dve", lambda e, i=i: e.tensor_copy(out=G[:, i, 8:12], in_=PB[4][:, 4:8]), reads=[PT[4]], writes=[t_G])
        gate_math(nt, gbs[:, sl * 8:sl * 8 + 4], gbs[:, sl * 8 + 4:sl * 8 + 8], 4)
        for i in range(nt):
            p.op("pool", lambda e, i=i: e.tensor_tensor(out=vext[:, i, :, 0:128], in0=vext[:, i, :, 0:128],
                                                        in1=GW[:, 2, i, 0:4].unsqueeze(2).to_broadcast([128, 4, 128]), op=ALU.mult),
                 reads=[t_vext[i], t_GW], writes=[t_vext[i]])
            p.op("pool", lambda e, i=i: e.tensor_copy(out=vext[:, i, :, 128], in_=GW[:, 2, i, 0:4]), reads=[t_vext[i], t_GW], writes=[t_vext[i]])
        for i in range(nt):
            for h in range(4):
                pbk = 5 + (h // 2)
                p.op("pe", lambda e, i=i, h=h, pbk=pbk: e.matmul(PB[pbk][:, (h % 2) * 256:(h % 2) * 256 + 129], lhsT=ktok[:, i, h * 128:(h + 1) * 128], rhs=vext[:, i, h, :],
                                                                 start=True, stop=True), reads=[t_ktok[i], t_vext[i]], writes=[PT[pbk]])
                p.op("dve", lambda e, i=i, h=h, pbk=pbk: e.scalar_tensor_tensor(out=S[:, h, :], in0=S[:, h, :], scalar=GW[:, 3, i, h:h + 1],
                                                                                in1=PB[pbk][:, (h % 2) * 256:(h % 2) * 256 + 129], op0=ALU.mult, op1=ALU.add),
                     reads=[t_S[h], t_GW, PT[pbk]], writes=[t_S[h]])
        if sl == 0:
            p.op("dve", lambda e: e.tensor_copy(out=ctxS[:, 0:4, :], in_=S[:, 0:4, :]), reads=t_S[0:4], writes=[t_ctxS])
            p.op("pool", lambda e: e.memset(S[:, 0:4, :], 0.0), reads=[t_ctxS], writes=t_S[0:4])
        elif sl == 1:
            p.op("dve", lambda e: e.tensor_copy(out=ctxS[:, 4:8, :], in_=S[:, 0:4, :]), reads=t_S[0:4], writes=[t_ctxS])
            p.op("dve", lambda e: e.tensor_copy(out=S[:, 0:4, :], in_=ctxS[:, 0:4, :]), reads=[t_ctxS], writes=t_S[0:4])
        else:
            chain_switch(sl - 1)
    p.op("pool", lambda e: e.memset(vext[:, :, :, 128], 1.0), writes=t_vext)
    p.op("dve", lambda e: e.tensor_copy(out=S[:, 4:8, :], in_=S[:, 0:4, :]), reads=t_S[0:4], writes=t_S[4:8])
    p.op("dve", lambda e: e.tensor_copy(out=S[:, 0:4, :], in_=initF[:]), reads=[t_initF] + t_S[4:8], writes=t_S[0:4])
    p.op("act", lambda e: e.activation(out=Sb[:], in_=S[:], func=AF.Copy), reads=t_S, writes=t_Sb)
    DBG = globals().get("DBG_STAGE", "NONE")
    ya_d = nc.dram_tensor("ya_s", [NT, 128, 512], F32).ap()
    og_d = nc.dram_tensor("og_s", [NT, 128, 512], F32).ap()
    qk_d = nc.dram_tensor("qk_s", [NT, 128, 8 * 128], BF16).ap()
    t_yad = [T("yad%d" % i) for i in range(NT)]; t_ogd = [T("ogd%d" % i) for i in range(NT)]; t_qkd = [T("qkd%d" % i) for i in range(NT)]
    X32 = A_X // 2

    def xf(off, n):
        return ARENA32[:, X32 + off:X32 + off + n]

    gu = xf(0, 512); t_gu = T("gu")
    gv = xf(512, 512); t_gv = T("gv")
    t1 = xf(1024, 512); t_t1 = T("t1")
    ya_t = [xf(1536 + i * 512, 512) for i in range(2)]; t_ya = [T("ya%d" % i) for i in range(2)]
    og_t = [xf(2560 + i * 512, 512) for i in range(2)]; t_og = [T("og%d" % i) for i in range(2)]
    cv = [xf(3584 + i * 128, 128) for i in range(2)]; t_cv = [T("cv%d" % i) for i in range(2)]
    sgq = [xf(3840 + i * 128, 128) for i in range(2)]; t_sgq = [T("sgq%d" % i) for i in range(2)]
    XB = A_X + 2 * 4096
    vn = ARENA[:, XB:XB + 512]; t_vn = T("vn")
    qk_t = [ARENA[:, XB + 512 + i * 1024:XB + 512 + (i + 1) * 1024].rearrange("p (a b) -> p a b", a=8) for i in range(2)]
    t_qk = [T("qk%d" % i) for i in range(2)]
    stats = sb("stats", [128, 8, 6]); t_stats = T("stats")
    mv = sb("mv", [128, 8, 2]); t_mv = T("mv")
    rstd = sb("rstd", [128, 8]); t_rstd = T("rstd")
    PB7b = PB[7].bitcast(BF16)

    def rstd_of(n, mv_=None, t_mv_=None, rstd_=None, t_rstd_=None):
        mv_ = mv if mv_ is None else mv_; t_mv_ = t_mv if t_mv_ is None else t_mv_
        rstd_ = rstd if rstd_ is None else rstd_; t_rstd_ = t_rstd if t_rstd_ is None else t_rstd_
        p.op("act", lambda e: e.activation(out=rstd_[:, 0:n], in_=mv_[:, 0:n, 1], func=AF.Ln, bias=EPS), reads=[t_mv_], writes=[t_rstd_])
        p.op("act", lambda e: e.activation(out=rstd_[:, 0:n], in_=rstd_[:, 0:n], func=AF.Exp, scale=-0.5), reads=[t_rstd_], writes=[t_rstd_])

    def gelu_from_psum(pb, tpb, dst, tdst):
        p.op("act", lambda e: e.activation(out=t1, in_=pb[:], func=AF.Square), reads=[tpb], writes=[t_t1])
        p.op("dve", lambda e: e.tensor_scalar(out=t1, in0=t1, scalar1=0.044715, scalar2=1.0, op0=ALU.mult, op1=ALU.add), reads=[t_t1], writes=[t_t1])
        p.op("dve", lambda e: e.tensor_tensor(out=t1, in0=t1, in1=pb[:], op=ALU.mult), reads=[t_t1, tpb], writes=[t_t1])
        p.op("act", lambda e: e.activation(out=t1, in_=t1, func=AF.Sigmoid, scale=GC), reads=[t_t1], writes=[t_t1])
        p.op("dve", lambda e: e.tensor_tensor(out=dst, in0=t1, in1=pb[:], op=ALU.mult), reads=[t_t1, tpb], writes=[tdst])

    halos([xh_d[0:1, :], xh_d[1:2, :]], 0, 0, TOK)

    def own_T(i):
        transposes(x_d, i * 128, 128, 1 + i * 128, 0, ti=i)

    def owin(i):
        return [t_hTt[i], t_hTt[i - 1] if i > 0 else t_hTh, t_hTt[i + 1] if i < NT - 1 else t_hTh]
    own_T(0)
    own_T(1)

    def proj_tok(i, pbk, c0, n):
        for k in range(8):
            p.op("pe", lambda e, k=k: e.matmul(PB[pbk][:, 0:n], lhsT=hT[:, k, 1 + i * 128:1 + (i + 1) * 128], rhs=w_in[:, k, c0:c0 + n],
                                              start=(k == 0), stop=(k == 7)), reads=[t_hTt[i], t_win], writes=[PT[pbk]])

    def gen_G(i):
        sl2 = i % 2
        proj_tok(i, 2, 0, 512)
        gelu_from_psum(PB[2], PT[2], gu, t_gu)
        yield
        proj_tok(i, 3, 512, 512)
        gelu_from_psum(PB[3], PT[3], gv, t_gv)
        yield
        for g in range(4):
            p.op("dve", lambda e, g=g: e.bn_stats(out=stats[:, g, :], in_=gv[:, g * 128:(g + 1) * 128]), reads=[t_gv], writes=[t_stats])
        for g in range(4):
            p.op("dve", lambda e, g=g: e.bn_aggr(out=mv[:, g, :], in_=stats[:, g, :]), reads=[t_stats], writes=[t_mv])
        rstd_of(4)
        yield
        for g in range(4):
            p.op("dve", lambda e, g=g: e.tensor_scalar(out=gv[:, g * 128:(g + 1) * 128], in0=gv[:, g * 128:(g + 1) * 128],
                                                       scalar1=mv[:, g, 0:1], scalar2=rstd[:, g:g + 1], op0=ALU.subtract, op1=ALU.mult),
                 reads=[t_gv, t_mv, t_rstd], writes=[t_gv])
        p.op("pool", lambda e: e.tensor_tensor(out=vn, in0=gv, in1=lng[:], op=ALU.mult), reads=[t_gv, t_lng], writes=[t_vn])
        yield
        for g in range(4):
            p.op("pe", lambda e, g=g: e.matmul(PB[4][:, g * 128:(g + 1) * 128], lhsT=wsTb[:, g, :], rhs=vn[:, g * 128:(g + 1) * 128],
                                              start=True, stop=True), reads=[t_wsTb, t_vn], writes=[PT[4]])
        for g in range(4):
            p.op("dve", lambda e, g=g: e.scalar_tensor_tensor(out=ya_t[sl2][:, g * 128:(g + 1) * 128], in0=PB[4][:, g * 128:(g + 1) * 128],
                                                              scalar=bsT[:, g:g + 1], in1=gu[:, g * 128:(g + 1) * 128], op0=ALU.add, op1=ALU.mult),
                 reads=[PT[4], t_bsT, t_gu], writes=[t_ya[sl2]])
        p.op("sp", lambda e, i=i: e.dma_start(out=ya_d[i], in_=ya_t[sl2]), reads=[t_ya[sl2]], writes=[t_yad[i]], dma=t_ya[sl2])
        yield

    def gen_Q(i):
        sl2 = i % 2
        proj_tok(i, 6, vcol, 512)
        p.op("act", lambda e, i=i: e.activation(out=vext[:, i, :, 0:128], in_=PB[6][:].rearrange("p (a b) -> p a b", b=128), func=AF.Copy),
             reads=[PT[6]], writes=[t_vext[i]])
        proj_tok(i, 7, 3072, 16)
        p.op("dve", lambda e, i=i: e.tensor_copy(out=G[:, i, :], in_=PB[7][:, 0:16]), reads=[PT[7]], writes=[t_G])
        yield
        proj_tok(i, 5, 2560, 512)
        p.op("act", lambda e: e.activation(out=og_t[sl2], in_=PB[5][:], func=AF.Sigmoid), reads=[PT[5]], writes=[t_og[sl2]])
        p.op("sp", lambda e, i=i: e.dma_start(out=og_d[i], in_=og_t[sl2]), reads=[t_og[sl2]], writes=[t_ogd[i]], dma=t_og[sl2])
        yield
        def qk_mm(hc):
            pbk = 5 + hc % 2
            c2 = hc % 2
            for k in range(8):
                p.op("pe", lambda e, k=k: e.matmul(PB[pbk][:, 0:130], lhsT=w_in[:, k, 1024 + hc * 128:1024 + (hc + 1) * 128],
                                                  rhs=hT[:, k, i * 128:i * 128 + 130], start=(k == 0), stop=(k == 7)),
                     reads=owin(i) + [t_win], writes=[PT[pbk]])
            p.op("act", lambda e: e.activation(out=cv[c2], in_=PB[pbk][:, 0:128], func=AF.Identity,
                                               bias=qkb[:, hc:hc + 1], scale=qkw[:, hc, 0:1]), reads=[PT[pbk], t_qkw, t_qkb], writes=[t_cv[c2]])

        def qk_ew(hc):
            pbk = 5 + hc % 2
            c2 = hc % 2
            for j in (1, 2):
                p.op("dve", lambda e, j=j: e.scalar_tensor_tensor(out=cv[c2], in0=PB[pbk][:, j:j + 128], scalar=qkw[:, hc, j:j + 1], in1=cv[c2],
                                                                  op0=ALU.mult, op1=ALU.add), reads=[PT[pbk], t_qkw, t_cv[c2]], writes=[t_cv[c2]])
            p.op("act", lambda e: e.activation(out=sgq[c2], in_=cv[c2], func=AF.Sigmoid), reads=[t_cv[c2]], writes=[t_sgq[c2]])
            p.op("dve", lambda e: e.tensor_tensor(out=qk_t[sl2][:, hc, :], in0=cv[c2], in1=sgq[c2], op=ALU.mult),
                 reads=[t_cv[c2], t_sgq[c2]], writes=[t_qk[sl2]])

        qk_mm(0)
        for hc in range(8):
            if hc + 1 < 8:
                qk_mm(hc + 1)
            qk_ew(hc)
            yield
        for h in range(4):
            p.op("pe", lambda e, h=h: e.transpose(PB7b[:, 512 + h * 128:512 + (h + 1) * 128], qk_t[sl2][:, 4 + h, :], identb),
                 reads=[t_qk[sl2], t_cstb], writes=[PT[7]])
        p.op("act", lambda e, i=i: e.activation(out=ktok[:, i, :], in_=PB7b[:, 512:1024], func=AF.Copy), reads=[PT[7]], writes=[t_ktok[i]])
        p.op("sp", lambda e, i=i: e.dma_start(out=qk_d[i], in_=qk_t[sl2].rearrange("p a b -> p (a b)")), reads=[t_qk[sl2]], writes=[t_qkd[i]], dma=t_qk[sl2])
        yield

    def drain(*gens):
        gens = [g for g in gens if g is not None]
        while gens:
            for g in list(gens):
                try:
                    next(g)
                except StopIteration:
                    gens.remove(g)

    def qk_reload(i):
        p.op("sp", lambda e: e.dma_start(out=hT[:, :, 1 + i * 128:1 + (i + 1) * 128], in_=qk_d[i].rearrange("p (a b) -> p a b", a=8)),
             reads=[t_qkd[i]], writes=[t_hTt[i]], dma=t_hTt[i])

    drain(gen_G(0))
    for i in range(NT):
        if i + 2 < NT:
            own_T(i + 2)
        drain(gen_Q(i), gen_G(i + 1) if i + 1 < NT else None)
        if i >= 1:
            qk_reload(i - 1)
    qk_reload(NT - 1)
    gate_math(NT, gbo[:, 0:8], gbo[:, 8:16], 8)

    if DBG == "A1":
        dbg = xt[0]; t_dbg = t_xt[0]

        def dump(r0, c0, n, src_ap, rd, eng="dve"):
            p.op(eng, lambda e: e.tensor_copy(out=dbg[:, 0:n], in_=src_ap), reads=rd + [t_dbg], writes=[t_dbg])
            o = p.op("sp", lambda e: e.dma_start(out=out_d[r0:r0 + 128, c0:c0 + n], in_=dbg[:, 0:n]), reads=[t_dbg], dma=t_dbg)
            p.final.append(("sp", o.sem, o.val))
        for (row, ti) in ((0, 0), (5, 15)):
            base = row * 128
            dump(base, 0, 1024, qk_t[ti % 2].rearrange("p a b -> p (a b)"), [t_qk[ti % 2]]) if ti == 15 else None
        dump(128, 0, 512, ktok[:, 0, :], [t_ktok[0]])
        dump(128, 512, 512, ktok[:, 15, :], [t_ktok[15]])
        dump(256, 0, 516, vext[:, 0, :, :].rearrange("p a b -> p (a b)"), [t_vext[0]])
        dump(384, 0, 256, G[:].rearrange("p a b -> p (a b)"), [t_G])
        dump(512, 0, 512, GW[:, 0, :, :].rearrange("p a b -> p (a b)")[:, 0:128], [t_GW]) if False else None
        for pl in range(4):
            dump(512, pl * 128, 128, GW[:, pl, :, :].rearrange("p a b -> p (a b)"), [t_GW])
        dump(768, 0, 512, ya_t[1], [t_ya[1]])
        dump(768, 512, 512, og_t[1], [t_og[1]])
        return nc, p

    H = ARENA32[:, 0:NT * 512].rearrange("p (a b) -> p a b", a=NT); t_H = [T("H%d" % i) for i in range(NT)]
    w_out = ARENA[:, 16384:16384 + 8 * D].rearrange("p (k n) -> p k n", k=8)
    t_qkT = T("qkT_unused")
    for k in range(8):
        p.op("pool", lambda e, k=k: e.dma_start(out=w_out[:, k, :], in_=w_out_d[k * 128:(k + 1) * 128, :]),
             writes=[t_wout, t_win], dma=t_wout)
    swb = [ARENA[:, XB + 4096 + i * 512:XB + 4096 + (i + 1) * 512].rearrange("p (a b) -> p a b", a=4) for i in range(2)]
    t_sw = [T("sw%d" % i) for i in range(2)]
    dn = [sb("dn%d" % i, [128, 4]) for i in range(2)]; t_dn = [T("dn%d" % i) for i in range(2)]
    r4 = [sb("r4%d" % i, [128, 4]) for i in range(2)]; t_r4 = [T("r4%d" % i) for i in range(2)]
    PP = [PBALL[:, (2 + 2 * d) * 512:(4 + 2 * d) * 512] for d in range(2)]
    PPT = [T("pp0"), T("pp1")]
    h_first = [True] * NT

    vsd = [sb("vsd%d" % i, [128, 4, 129], BF16) for i in range(2)]; t_vsd = [T("vsd%d" % i) for i in range(2)]

    def scan_pair(j):
        last = (j == NT - 1)
        steps = [(j, 0), (NT - 1 - j, 1)]
        css = {c: slice(1 + c * 128, 1 + (c + 1) * 128) for c, _ in steps}
        if not last:
            for c, d in steps:
                for h in range(4):
                    p.op("act", lambda e, c=c, d=d, h=h: e.activation(out=vsd[d][:, h, :], in_=vext[:, c, h, :], func=AF.Copy,
                                                                      scale=GW[:, 2, c, d * 4 + h:d * 4 + h + 1]),
                         reads=[t_vext[c], t_GW], writes=[t_vsd[d]])
        for c, d in steps:
            for h in range(4):
                p.op("pe", lambda e, c=c, d=d, h=h: e.matmul(PB[d][:, h * 128:(h + 1) * 128], lhsT=hT[:, 4 + h, css[c]], rhs=hT[:, h, css[c]], start=True, stop=True),
                     reads=[t_hTt[c]], writes=[PT[d]])
        for c, d in steps:
            mask = maskU if d == 0 else maskL
            for h in range(4):
                p.op("dve", lambda e, c=c, d=d, h=h, mask=mask: e.scalar_tensor_tensor(out=swb[d][:, h, :], in0=PB[d][:, h * 128:(h + 1) * 128],
                                                                                     scalar=GW[:, 0, c, d * 4 + h:d * 4 + h + 1], in1=mask, op0=ALU.mult, op1=ALU.mult),
                     reads=[PT[d], t_GW, t_cstb], writes=[t_sw[d]])
        for c, d in steps:
            for h in range(4):
                p.op("pe", lambda e, c=c, d=d, h=h: e.matmul(PP[d][:, h * 256:h * 256 + 129], lhsT=swb[d][:, h, :], rhs=vext[:, c, h, :], start=True, stop=False),
                     reads=[t_sw[d], t_vext[c]], writes=[PPT[d]])
                p.op("pe", lambda e, c=c, d=d, h=h: e.matmul(PP[d][:, h * 256:h * 256 + 129], lhsT=hT[:, h, css[c]], rhs=Sb[:, d * 4 + h, :], start=False, stop=True),
                     reads=[t_hTt[c], t_Sb[d * 4 + h]], writes=[PPT[d]])
        if not last:
            for c, d in steps:
                for h in range(4):
                    pbk = 6 + (h // 2)
                    p.op("pe", lambda e, c=c, d=d, h=h, pbk=pbk: e.matmul(PB[pbk][:, (h % 2) * 256:(h % 2) * 256 + 129], lhsT=ktok[:, c, h * 128:(h + 1) * 128],
                                                                         rhs=vsd[d][:, h, :], start=True, stop=True), reads=[t_ktok[c], t_vsd[d]], writes=[PT[pbk]])
                    sidx = d * 4 + h
                    p.op("dve", lambda e, c=c, h=h, pbk=pbk, sidx=sidx: e.scalar_tensor_tensor(out=S[:, sidx, :], in0=S[:, sidx, :], scalar=GW[:, 3, c, sidx:sidx + 1],
                                                                                           in1=PB[pbk][:, (h % 2) * 256:(h % 2) * 256 + 129], op0=ALU.mult, op1=ALU.add),
                         reads=[t_S[sidx], t_GW, PT[pbk]], writes=[t_S[sidx]])
        for c, d in steps:
            den = PP[d].rearrange("p (a b) -> p a b", b=256)[:, :, 128]
            ebs = GW[:, 1, c, d * 4:(d + 1) * 4]
            p.op("dve", lambda e, d=d, den=den, ebs=ebs: e.tensor_tensor(out=dn[d][:], in0=den, in1=ebs, op=ALU.mult), reads=[PPT[d], t_GW], writes=[t_dn[d]])
            p.op("dve", lambda e, d=d: e.scalar_tensor_tensor(out=r4[d][:], in0=dn[d][:], scalar=-1.0, in1=dn[d][:], op0=ALU.mult, op1=ALU.max),
                 reads=[t_dn[d]], writes=[t_r4[d]])
            p.op("dve", lambda e, d=d: e.tensor_scalar_max(out=r4[d][:], in0=r4[d][:], scalar1=1.0), reads=[t_r4[d]], writes=[t_r4[d]])
            p.op("dve", lambda e, d=d: e.reciprocal(out=r4[d][:], in_=r4[d][:]), reads=[t_r4[d]], writes=[t_r4[d]])
            p.op("dve", lambda e, d=d, ebs=ebs: e.tensor_tensor(out=r4[d][:], in0=r4[d][:], in1=ebs, op=ALU.mult), reads=[t_r4[d], t_GW], writes=[t_r4[d]])
        if not last:
            for c, d in steps:
                for h in range(4):
                    sidx = d * 4 + h
                    p.op("act", lambda e, sidx=sidx: e.activation(out=Sb[:, sidx, :], in_=S[:, sidx, :], func=AF.Copy), reads=[t_S[sidx]], writes=[t_Sb[sidx]])
        for c, d in steps:
            first = h_first[c]
            h_first[c] = False
            for h in range(4):
                hs = slice(h * 128, (h + 1) * 128)
                if first:
                    p.op("act", lambda e, c=c, d=d, h=h, hs=hs: e.activation(out=H[:, c, hs], in_=PP[d][:, h * 256:h * 256 + 128], func=AF.Copy, scale=r4[d][:, h:h + 1]),
                         reads=[PPT[d], t_r4[d]], writes=[t_H[c], t_win])
                else:
                    p.op("dve", lambda e, c=c, d=d, h=h, hs=hs: e.scalar_tensor_tensor(out=H[:, c, hs], in0=PP[d][:, h * 256:h * 256 + 128], scalar=r4[d][:, h:h + 1],
                                                                                     in1=H[:, c, hs], op0=ALU.mult, op1=ALU.add),
                         reads=[PPT[d], t_r4[d], t_H[c]], writes=[t_H[c]])

    for j in range(NT):
        scan_pair(j)

    A_WDN = 8 * 2 * DFF
    A_SP = A_WDN + NFC * D
    w_up = ARENA[:, 0:A_WDN].rearrange("p (k n) -> p k n", k=8)
    w_dn = ARENA[:, A_WDN:A_SP].rearrange("p (k n) -> p k n", k=NFC)
    t_wdn = T("w_dn")
    fence1 = sb("fence1", [128, 1])
    p.op("dve", lambda e: e.memset(fence1[:], 0.0), writes=t_hT_all + t_ktok + t_vext + [t_wup, t_wdn])
    for k in (5, 6, 7):
        p.op("pool", lambda e, k=k: e.dma_start(out=w_up[:, k, :], in_=w_up_d[k * 128:(k + 1) * 128, :]), writes=[t_wup], dma=t_wup)
    for k in range(14):
        p.op("pool", lambda e, k=k: e.dma_start(out=w_dn[:, k, :], in_=w_dn_d[k * 128:(k + 1) * 128, :]), writes=[t_wdn], dma=t_wdn)

    if DBG == "H":
        for c in range(NT):
            o = p.op("sp", lambda e, c=c: e.dma_start(out=out_d[c * 128:(c + 1) * 128, 0:512], in_=H[:, c, :]), reads=[t_H[c]], dma=t_H[c])
            p.final.append(("sp", o.sem, o.val))
        return nc, p

    x_old = [t_gu, t_gv, t_t1, t_vn, t_Wk, t_cwbc, t_sgk] + t_ya + t_og + t_cv + t_sgq + t_qk + t_sw
    mng = xf(0, 1024); ln1g = xf(1024, 1024); ln1b = xf(2048, 1024); g1bc = xf(3072, 1024)
    yab = xf(4096, 1024); zt = xf(5120, 1024); ogt = xf(6144, 512)
    diag = yab
    YB = A_X + 2 * 6656
    ybf_q = [ARENA[:, YB:YB + 1024], ARENA[:, A_X + 2 * 6144:A_X + 2 * 6144 + 1024]]
    yT = ARENA[:, YB + 1024:YB + 2048].rearrange("p (a b) -> p a b", a=8)
    t_mng = T("mng"); t_ln1g = T("ln1g"); t_ln1b = T("ln1b"); t_g1bc = T("g1bc"); t_yab = T("yab"); t_zt = T("zt")
    t_ogt = T("ogt"); t_diag = t_yab; t_ybf_q = [T("ybf0"), T("ybf1")]; t_yT = T("yT")
    x_new = [t_mng, t_ln1g, t_ln1b, t_g1bc, t_yab, t_zt, t_ogt, t_yT] + t_ybf_q
    fence = sb("fence", [128, 1])
    p.op("dve", lambda e: e.memset(fence[:], 0.0), writes=x_old + x_new)
    p.op("sp", lambda e: e.dma_start(out=mng, in_=mng_d.partition_broadcast(128)), writes=[t_mng], dma=t_mng)
    p.op("sp", lambda e: e.dma_start(out=ln1g, in_=ln1g_d.partition_broadcast(128)), writes=[t_ln1g], dma=t_ln1g)
    p.op("sp", lambda e: e.dma_start(out=ln1b, in_=ln1b_d.partition_broadcast(128)), writes=[t_ln1b], dma=t_ln1b)

    def bcast_row(base, dst, tdst, tdiag, dg):
        dg3 = dg.rearrange("p (a b) -> p a b", a=8)
        for k in range(8):
            p.op("dve", lambda e, k=k: e.tensor_scalar_mul(out=dg3[:, k, :], in0=ident, scalar1=mod[:, base + k, 0:1]),
                 reads=[t_cst, t_mod], writes=[tdiag])
        for hlf in range(2):
            for k4 in range(4):
                k = hlf * 4 + k4
                p.op("pe", lambda e, k=k, k4=k4, hlf=hlf: e.matmul(PB[2 + hlf][:, k4 * 128:(k4 + 1) * 128], lhsT=ones, rhs=dg3[:, k, :], start=True, stop=True),
                     reads=[t_cst, tdiag], writes=[PT[2 + hlf]])
            p.op("act", lambda e, hlf=hlf: e.activation(out=dst[:, hlf * 512:(hlf + 1) * 512], in_=PB[2 + hlf][:], func=AF.Copy),
                 reads=[PT[2 + hlf]], writes=[tdst])

    bcast_row(16, g1bc, t_g1bc, t_diag, diag)
    PB6b = PB[6].bitcast(BF16)

    def layer_norm_store(z, tz, g_ap, tg, b_ap, tb, dst_d, extra_w=(), aff="pool", priv=None):
        st_, t_st_, mv_, t_mv_, rs_, t_rs_ = priv if priv is not None else (stats, t_stats, mv, t_mv, rstd, t_rstd)
        for c in range(2):
            p.op("dve", lambda e, c=c: e.bn_stats(out=st_[:, c, :], in_=z[:, c * 512:(c + 1) * 512]), reads=[tz], writes=[t_st_])
        p.op("dve", lambda e: e.bn_aggr(out=mv_[:, 0, :], in_=st_[:, 0:2, :]), reads=[t_st_], writes=[t_mv_])
        rstd_of(1, mv_, t_mv_, rs_, t_rs_)
        p.op("dve", lambda e: e.tensor_scalar(out=z, in0=z, scalar1=mv_[:, 0, 0:1], scalar2=rs_[:, 0:1], op0=ALU.subtract, op1=ALU.mult),
             reads=[tz, t_mv_, t_rs_], writes=[tz])
        p.op(aff, lambda e: e.tensor_tensor(out=z, in0=z, in1=g_ap, op=ALU.mult), reads=[tz, tg], writes=[tz])
        p.op(aff, lambda e: e.tensor_tensor(out=z, in0=z, in1=b_ap, op=ALU.add), reads=[tz, tb], writes=[tz])
        return p.op("sp", lambda e: e.dma_start(out=dst_d, in_=z), reads=[tz], writes=list(extra_w), dma=tz)

    t_x1 = [T("x1_%d" % i) for i in range(NT)]
    ya_in = [PA[:, 3088 + i * 512:3088 + (i + 1) * 512] for i in range(2)]; t_ya_in = [T("ya_in%d" % i) for i in range(2)]
    og_in = [PA[:, 4112 + i * 512:4112 + (i + 1) * 512] for i in range(2)]; t_og_in = [T("og_in%d" % i) for i in range(2)]

    def loads_yaog(i):
        s2 = i % 2
        p.op("sp", lambda e: e.dma_start(out=ya_in[s2][:], in_=ya_d[i]), reads=[t_yad[i]], writes=[t_ya_in[s2]], dma=t_ya_in[s2])
        p.op("sp", lambda e: e.dma_start(out=og_in[s2][:], in_=og_d[i]), reads=[t_ogd[i]], writes=[t_og_in[s2]], dma=t_og_in[s2])

    def loads_x(i):
        s2 = i % 2
        p.op("sp", lambda e: e.dma_start(out=xt[s2][:], in_=x_d[i * 128:(i + 1) * 128, :]), writes=[t_xt[s2]], dma=t_xt[s2])

    def headnorm(i):
        s2 = i % 2
        src = [(ya_in[s2][:, g * 128:(g + 1) * 128], t_ya_in[s2]) for g in range(4)] + [(H[:, i, g * 128:(g + 1) * 128], t_H[i]) for g in range(4)]
        for g in range(8):
            p.op("dve", lambda e, g=g: e.bn_stats(out=stats[:, g, :], in_=src[g][0]), reads=[src[g][1]], writes=[t_stats])
        for g in range(8):
            p.op("dve", lambda e, g=g: e.bn_aggr(out=mv[:, g, :], in_=stats[:, g, :]), reads=[t_stats], writes=[t_mv])
        rstd_of(8)
        for g in range(8):
            p.op("dve", lambda e, g=g: e.tensor_scalar(out=yab[:, g * 128:(g + 1) * 128], in0=src[g][0],
                                                       scalar1=mv[:, g, 0:1], scalar2=rstd[:, g:g + 1], op0=ALU.subtract, op1=ALU.mult),
                 reads=[src[g][1], t_mv, t_rstd], writes=[t_yab])
        p.op("dve", lambda e: e.tensor_tensor(out=ybf_q[s2][:, 0:512], in0=yab[:, 0:512], in1=mng[:, 0:512], op=ALU.mult), reads=[t_yab, t_mng], writes=[t_ybf_q[s2]])
        p.op("dve", lambda e: e.tensor_tensor(out=yab[:, 512:1024], in0=yab[:, 512:1024], in1=mng[:, 512:1024], op=ALU.mult), reads=[t_yab, t_mng], writes=[t_yab])
        p.op("dve", lambda e: e.tensor_tensor(out=ybf_q[s2][:, 512:1024], in0=yab[:, 512:1024], in1=og_in[s2][:], op=ALU.mult), reads=[t_yab, t_og_in[s2]], writes=[t_ybf_q[s2]])

    st1 = sb("st1", [128, 2, 6]); mv1 = sb("mv1", [128, 1, 2]); rs1 = sb("rs1", [128, 1])
    ln1priv = (st1, T("st1"), mv1, T("mv1"), rs1, T("rs1"))

    def out_mm(i):
        s2 = i % 2
        for k in range(8):
            p.op("pe", lambda e, k=k: e.transpose(PB6b[:, k * 128:(k + 1) * 128], ybf_q[s2][:, k * 128:(k + 1) * 128], identb), reads=[t_ybf_q[s2], t_cstb], writes=[PT[6]])
        p.op("act", lambda e: e.activation(out=yT.rearrange("p a b -> p (a b)"), in_=PB6b[:, 0:1024], func=AF.Copy), reads=[PT[6]], writes=[t_yT])
        for hlf in range(2):
            for k in range(8):
                p.op("pe", lambda e, k=k, hlf=hlf: e.matmul(PB[2 + 2 * s2 + hlf][:], lhsT=yT[:, k, :], rhs=w_out[:, k, hlf * 512:(hlf + 1) * 512],
                                                            start=(k == 0), stop=(k == 7)), reads=[t_yT, t_wout], writes=[PT[2 + 2 * s2 + hlf]])

    def out_ln(i):
        s2 = i % 2
        for hlf in range(2):
            p.op("dve", lambda e, hlf=hlf: e.tensor_tensor(out=zt[:, hlf * 512:(hlf + 1) * 512], in0=PB[2 + 2 * s2 + hlf][:], in1=g1bc[:, hlf * 512:(hlf + 1) * 512], op=ALU.mult),
                 reads=[PT[2 + 2 * s2 + hlf], t_g1bc], writes=[t_zt])
        p.op("dve", lambda e: e.scalar_tensor_tensor(out=zt, in0=xt[s2][:], scalar=ALPHA, in1=zt, op0=ALU.mult, op1=ALU.add),
             reads=[t_xt[s2], t_zt], writes=[t_zt])
        dst = out_d[i * 128:(i + 1) * 128, :] if DBG == "X1" else x1s_d[i * 128:(i + 1) * 128, :]
        return layer_norm_store(zt, t_zt, ln1g, t_ln1g, ln1b, t_ln1b, dst, extra_w=[t_x1[i]], aff="pool", priv=ln1priv)

    loads_yaog(0); loads_x(0)
    if NT > 1:
        loads_yaog(1); loads_x(1)
    headnorm(0)
    if NT > 2:
        loads_yaog(2)
    out_mm(0)
    def wup_into_dead_H(k, h_tiles):
        p.op("pool", lambda e: e.dma_start(out=w_up[:, k, :], in_=w_up_d[k * 128:(k + 1) * 128, :]), writes=[t_wup] + h_tiles, dma=t_wup)

    for i in range(NT):
        if i + 1 < NT:
            headnorm(i + 1)
            if i + 1 == 5:
                wup_into_dead_H(0, t_H[0:6])
            elif i + 1 == 10:
                wup_into_dead_H(1, t_H[5:11])
            elif i + 1 == 15:
                wup_into_dead_H(2, t_H[10:16])
            if i + 3 < NT:
                loads_yaog(i + 3)
            out_mm(i + 1)
        o = out_ln(i)
        if i + 2 < NT:
            loads_x(i + 2)
    if DBG == "X1":
        return nc, p

    everything = ([t_win, t_wout] + t_hT_all + [ t_Wk, t_cwbc, t_sgk] + t_ktok + t_vext + t_H + x_old + x_new)
    ln2g = ARENA32[:, A_SP // 2:A_SP // 2 + 1024]; ln2b = ARENA32[:, A_SP // 2 + 1024:A_SP // 2 + 2048]
    g2bc = ARENA32[:, A_SP // 2 + 2048:A_SP // 2 + 3072]
    gT = ARENA[:, A_SP + 6144:A_SP + 6144 + NFC * 128].rearrange("p (a b) -> p a b", a=NFC)
    t_ln2g = T("ln2g"); t_ln2b = T("ln2b"); t_g2bc = T("g2bc"); t_gT = T("gT")
    t_fenceB = T("fenceB")
    newB = [t_ln2g, t_ln2b, t_g2bc, t_gT, t_fenceB]
    p.op("dve", lambda e: e.memset(fence[:], 0.0), writes=everything + newB)
    for k in (3, 4):
        p.op("pool", lambda e, k=k: e.dma_start(out=w_up[:, k, :], in_=w_up_d[k * 128:(k + 1) * 128, :]), reads=[t_fenceB], writes=[t_wup], dma=t_wup)
    for k in range(14, NFC):
        p.op("pool", lambda e, k=k: e.dma_start(out=w_dn[:, k, :], in_=w_dn_d[k * 128:(k + 1) * 128, :]), reads=[t_fenceB], writes=[t_wdn], dma=t_wdn)
    p.op("sp", lambda e: e.dma_start(out=ln2g, in_=ln2g_d.partition_broadcast(128)), writes=[t_ln2g], dma=t_ln2g)
    p.op("sp", lambda e: e.dma_start(out=ln2b, in_=ln2b_d.partition_broadcast(128)), writes=[t_ln2b], dma=t_ln2b)
    bcast_row(40, g2bc, t_g2bc, t_xt[0], xt[0][:])
    h2T_q = [ARENA[:, A_SP + 6144:A_SP + 6144 + 2048].rearrange("p (a b) -> p a b", a=8), PAb[:, 6144:8192].rearrange("p (a b) -> p a b", a=8)]
    t_h2T_q = [T("h2T0"), T("h2T1")]
    gTf = [ARENA[:, A_SP + 8192 + i * 256:A_SP + 8192 + (i + 1) * 256] for i in range(2)]; t_gTf = [T("gTf%d" % i) for i in range(2)]
    tcv = [sb("tcv%d" % i, [128, 256]) for i in range(4)]; t_tcv = [T("tcv%d" % i) for i in range(4)]
    sgl = [sb("sgl%d" % i, [128, 256]) for i in range(2)]; t_sgl = [T("sgl%d" % i) for i in range(2)]
    zt2 = sb("zt2", [128, D])
    zt_tt = [zt2[:], PA[:, 2048:3072]]; t_zt_tt = [T("zt2a"), T("zt2b")]
    xt_q = [[xt[0][:], xt[1][:]], [PA[:, 0:1024], PA[:, 1024:2048]]]
    t_xt_q = [[t_xt[0], t_xt[1]], [T("xt2"), T("xt3")]]
    deadA = t_S + [t_ctxS, t_GW] + t_ya_in + t_og_in
    p.op("dve", lambda e: e.memset(fence[:], 0.0), writes=deadA + [t_gT] + t_h2T_q + t_gTf + t_zt_tt + t_xt_q[1])
    ACC = [[0, 1], [6, 7]]

    def ffn_loads(ip):
        q = ip % 2
        for tt, i in enumerate((2 * ip, 2 * ip + 1)):
            p.op("sp", lambda e, tt=tt, i=i: e.dma_start(out=xt_q[q][tt], in_=x1s_d[i * 128:(i + 1) * 128, :]), reads=[t_x1[i]], writes=[t_xt_q[q][tt]], dma=t_xt_q[q][tt])

    def ffn_transposes(ip):
        q = ip % 2
        for tt in range(2):
            for half in range(2):
                pbk = 2 + 2 * tt + half
                for k4 in range(4):
                    k = half * 4 + k4
                    p.op("pe", lambda e, k=k, k4=k4, pbk=pbk, tt=tt: e.transpose(PB[pbk][:, k4 * 128:(k4 + 1) * 128], xt_q[q][tt][:, k * 128:(k + 1) * 128], ident),
                         reads=[t_xt_q[q][tt], t_cst], writes=[PT[pbk]])
                for k4 in range(4):
                    k = half * 4 + k4
                    p.op("act", lambda e, k=k, k4=k4, pbk=pbk, tt=tt: e.activation(out=h2T_q[q][:, k, tt * 128:(tt + 1) * 128], in_=PB[pbk][:, k4 * 128:(k4 + 1) * 128],
                                                                                func=AF.Identity, bias=mod[:, 24 + k, 0:1], scale=mod[:, 32 + k, 0:1]),
                         reads=[PT[pbk], t_mod], writes=[t_h2T_q[q]])

    def ffn_fcloop(ip):
        q = ip % 2
        h2T = h2T_q[q]; t_h2T = t_h2T_q[q]

        def up(fc):
            par = fc % 2
            for which in range(2):
                ch = fc + which * NFC
                pbk = 2 + 2 * par + which
                for k in range(8):
                    p.op("pe", lambda e, k=k, ch=ch, pbk=pbk: e.matmul(PB[pbk][:, 0:256], lhsT=w_up[:, k, ch * 128:(ch + 1) * 128], rhs=h2T[:, k, :],
                                                                      start=(k == 0), stop=(k == 7)), reads=[t_wup, t_h2T], writes=[PT[pbk]])

        def ew(fc):
            par = fc % 2
            for which in range(2):
                ch = fc + which * NFC
                pbk = 2 + 2 * par + which
                tb = 2 * par + which
                Pv = PB[pbk][:, 0:256]
                P3 = Pv.rearrange("p (r w) -> p r w", w=64)
                t3 = tcv[tb][:].rearrange("p (r w) -> p r w", w=64)
                p.op("act", lambda e, ch=ch, Pv=Pv, tb=tb: e.activation(out=tcv[tb][:], in_=Pv, func=AF.Identity, bias=fcb[:, ch:ch + 1], scale=fcw[:, ch, 1:2]),
                     reads=[PT[pbk], t_fcw, t_fcb], writes=[t_tcv[tb]])
                p.op("dve", lambda e, ch=ch, P3=P3, t3=t3, pbk=pbk, tb=tb: e.scalar_tensor_tensor(out=t3[:, :, 1:64], in0=P3[:, :, 0:63], scalar=fcw[:, ch, 0:1], in1=t3[:, :, 1:64],
                                                                                  op0=ALU.mult, op1=ALU.add), reads=[PT[pbk], t_fcw, t_tcv[tb]], writes=[t_tcv[tb]])
                p.op("dve", lambda e, ch=ch, P3=P3, t3=t3, pbk=pbk, tb=tb: e.scalar_tensor_tensor(out=t3[:, :, 0:63], in0=P3[:, :, 1:64], scalar=fcw[:, ch, 2:3], in1=t3[:, :, 0:63],
                                                                                  op0=ALU.mult, op1=ALU.add), reads=[PT[pbk], t_fcw, t_tcv[tb]], writes=[t_tcv[tb]])
            tv, tg = 2 * par, 2 * par + 1
            p.op("act", lambda e: e.activation(out=sgl[par][:], in_=tcv[tg][:], func=AF.Silu), reads=[t_tcv[tg]], writes=[t_sgl[par]])
            p.op("dve", lambda e: e.tensor_tensor(out=gTf[par], in0=tcv[tv][:], in1=sgl[par][:], op=ALU.mult),
                 reads=[t_tcv[tv], t_sgl[par]], writes=[t_gTf[par]])

        def wd(fc):
            par = fc % 2
            for tt in range(2):
                for hlf in range(2):
                    bk = ACC[tt][hlf]
                    p.op("pe", lambda e, tt=tt, hlf=hlf, bk=bk: e.matmul(PB[bk][:], lhsT=gTf[par][:, tt * 128:(tt + 1) * 128],
                                                                         rhs=w_dn[:, fc, hlf * 512:(hlf + 1) * 512], start=(fc == 0), stop=(fc == NFC - 1)),
                         reads=[t_gTf[par], t_wdn], writes=[PT[bk]])

        up(0)
        for fc in range(NFC):
            if fc + 1 < NFC:
                up(fc + 1)
            ew(fc)
            wd(fc)
            if ip + 1 < NT // 2:
                if fc == 2:
                    ffn_loads(ip + 1)

    def ffn_epilogue(ip):
        q = ip % 2
        for tt, i in enumerate((2 * ip, 2 * ip + 1)):
            z = zt_tt[tt]; tz = t_zt_tt[tt]
            for hlf in range(2):
                bk = ACC[tt][hlf]
                p.op("dve", lambda e, hlf=hlf, bk=bk, z=z: e.tensor_tensor(out=z[:, hlf * 512:(hlf + 1) * 512], in0=PB[bk][:], in1=g2bc[:, hlf * 512:(hlf + 1) * 512], op=ALU.mult),
                     reads=[PT[bk], t_g2bc], writes=[tz])
            p.op("dve", lambda e, tt=tt, z=z: e.scalar_tensor_tensor(out=z, in0=xt_q[q][tt], scalar=ALPHA, in1=z, op0=ALU.mult, op1=ALU.add),
                 reads=[t_xt_q[q][tt], tz], writes=[tz])
        for tt, i in enumerate((2 * ip, 2 * ip + 1)):
            layer_norm_store(zt_tt[tt], t_zt_tt[tt], ln2g, t_ln2g, ln2b, t_ln2b, out_d[i * 128:(i + 1) * 128, :])

    ffn_loads(0)
    ffn_transposes(0)
    for ip in range(NT // 2):
        ffn_fcloop(ip)
        if ip + 1 < NT // 2:
            ffn_transposes(ip + 1)
        ffn_epilogue(ip)
    return nc, p


def _consts():
    i = np.arange(128)
    ident = np.eye(128, dtype=np.float32)
    triU = (i[:, None] <= i[None, :]).astype(np.float32)
    triL = (i[:, None] >= i[None, :]).astype(np.float32)
    ones = np.ones((128, 128), np.float32)
    return np.ascontiguousarray(np.stack([ident, triU, triL, ones], axis=1))


def _pcol(v, nch):
    return np.ascontiguousarray(np.asarray(v, np.float32).reshape(nch, 128).T)


def make_in_maps(x, c, ctx, c_ctx, w_ada, b_ada, w_in, gmlp_ln_g, gmlp_ws, gmlp_bs, qk_conv_w, qk_conv_b,
                 b_igate, b_fgate, mix_norm_g, w_out, ln1_g, ln1_b, w_up, ffn_conv_w, ffn_conv_b, w_down, ln2_g, ln2_b):
    f = lambda a: np.ascontiguousarray(np.asarray(a, np.float32))
    x = f(x); ctx = f(ctx); w_in0 = f(w_in)[0]
    zero = np.zeros((1, D), np.float32)
    GOFF = 3072
    shared = {
        "cst": _consts(),
        "w_ada": f(w_ada)[0], "b_adaT": _pcol(f(b_ada)[0], 48), "w_in": w_in0,
        "cbk": f(qk_conv_b)[0, 512:1024],
        "gmlp_ln_g": f(gmlp_ln_g)[0].reshape(512),
        "wsT": np.ascontiguousarray(f(gmlp_ws)[0].transpose(2, 0, 1)),
        "bsT": np.ascontiguousarray(f(gmlp_bs)[0].T),
        "qkcw": np.ascontiguousarray(f(qk_conv_w)[0].reshape(3, 8, 128).transpose(2, 1, 0)),
        "qkcb": _pcol(f(qk_conv_b)[0], 8),
        "gbo": np.concatenate([f(b_igate)[0].reshape(8), f(b_fgate)[0].reshape(8)]),
        "mix_norm_g": f(mix_norm_g)[0], "w_out": f(w_out)[0], "ln1_g": f(ln1_g)[0], "ln1_b": f(ln1_b)[0],
        "w_up": f(w_up)[0],
        "fcw": np.ascontiguousarray(f(ffn_conv_w)[0].reshape(3, 42, 128).transpose(2, 1, 0)),
        "fcb": _pcol(f(ffn_conv_b)[0], 42),
        "w_down": f(w_down)[0], "ln2_g": f(ln2_g)[0], "ln2_b": f(ln2_b)[0],
    }
    taps = f(qk_conv_w)[0][:, 512:1024]
    bi = f(b_igate)[0]; bf_ = f(b_fgate)[0]

    def dir_params(d):
        wg = np.concatenate([w_in0[:, GOFF + d * 4:GOFF + d * 4 + 4], w_in0[:, GOFF + 8 + d * 4:GOFF + 8 + d * 4 + 4]], axis=1)
        gb = np.concatenate([bi[d], bf_[d]])
        cw = taps if d == 0 else taps[::-1]
        return wg, gb, cw

    maps = []
    for core in range(NCORE):
        b, r = divmod(core, 4)
        T0 = r * TOK
        xb = x[b]
        m = dict(shared)
        m["x"] = np.ascontiguousarray(xb[T0:T0 + TOK])
        lo = xb[T0 - 1:T0] if T0 > 0 else zero
        hi = xb[T0 + TOK:T0 + TOK + 1] if T0 + TOK < SEQ else zero
        m["xh"] = np.ascontiguousarray(np.concatenate([lo, hi], 0))
        flags = np.zeros((128, 16), np.float32)
        flags[:, 0] = 1.0 if T0 > 0 else 0.0
        flags[:, 1] = 1.0 if T0 + TOK < SEQ else 0.0
        m["ctxf"] = np.ascontiguousarray(np.concatenate([zero, ctx[b], zero], 0))
        m["ctxr"] = np.ascontiguousarray(np.concatenate([zero, ctx[b][::-1], zero], 0))
        dirs = [0, 1]
        segs = [(j, 0) for j in range(r)] + [(j, 1) for j in range(3, r, -1)]
        for k, (j, d) in enumerate(segs):
            s0 = j * TOK
            plo = xb[s0 - 1:s0] if s0 > 0 else zero
            phi = xb[s0 + TOK:s0 + TOK + 1] if s0 + TOK < SEQ else zero
            flo = 1.0 if s0 > 0 else 0.0
            fhi = 1.0 if s0 + TOK < SEQ else 0.0
            if d == 0:
                rows = np.concatenate([plo, xb[s0:s0 + TOK], phi], 0)
                flags[:, 6 + 2 * k] = flo; flags[:, 7 + 2 * k] = fhi
            else:
                rows = np.concatenate([phi, xb[s0:s0 + TOK][::-1], plo], 0)
                flags[:, 6 + 2 * k] = fhi; flags[:, 7 + 2 * k] = flo
            m["xo%d" % k] = np.ascontiguousarray(rows)
            dirs.append(d)
        flags[:, 12 + r] = 1.0
        m["flags"] = flags
        ps = [dir_params(d) for d in dirs]
        m["wg"] = np.ascontiguousarray(np.stack([q[0] for q in ps]))
        m["gb"] = np.ascontiguousarray(np.stack([q[1] for q in ps]))
        m["cwk"] = np.ascontiguousarray(np.stack([q[2] for q in ps]))
        cT = np.stack([f(c)[b], f(c_ctx)], axis=1)
        m["cT"] = np.ascontiguousarray(cT.reshape(8, 128, 2).transpose(1, 0, 2))
        maps.append(m)
    return maps


_NC_CACHE = {}


def kernel(**inputs):
    if "nc" not in _NC_CACHE:
        nc, p = build_nc()
        p.emit()
        _NC_CACHE["nc"] = nc
    nc = _NC_CACHE["nc"]
    maps = make_in_maps(**inputs)
    res = run_bass_kernel_spmd(nc, maps, core_ids=list(range(NCORE)))
    out = np.zeros((2, SEQ, D), np.float32)
    for core in range(NCORE):
        b, r = divmod(core, 4)
        out[b, r * TOK:(r + 1) * TOK] = res.results[core]["out"]
    return out
```

```python
import numpy as np
import concourse.bass as bass
import concourse.mybir as mybir
from concourse.bass_utils import run_bass_kernel_spmd

F32 = mybir.dt.float32
BF16 = mybir.dt.bfloat16
AF = mybir.ActivationFunctionType
ALU = mybir.AluOpType

D = 1024
SEQ = 8192
NCORE = 8
TOK = 2048
NT = 16
CTX = 256
N_IN = 3088
DFF = 2688
NFC = 21
ALPHA = 2.0 ** 0.25
EPS = 1e-5
LN_QSCALE = float(np.log(128.0 ** -0.5))
GC = 1.5957691216057308

ENGS = ("pe", "act", "dve", "pool", "sp")


class T:
    __slots__ = ("name", "w", "rs", "dsem", "dcnt")

    def __init__(self, name):
        self.name = name
        self.w = None
        self.rs = []
        self.dsem = None
        self.dcnt = 0


class Op:
    __slots__ = ("eng", "fn", "deps", "is_dma", "sem", "val", "need_inc", "idx")

    def __init__(self, eng, fn, is_dma):
        self.eng = eng
        self.fn = fn
        self.deps = []
        self.is_dma = is_dma
        self.sem = None
        self.val = None
        self.need_inc = False


class Prog:
    def __init__(self, nc):
        self.nc = nc
        self.q = {e: [] for e in ENGS}
        self.esem = {}
        self.final = []
        self.nsem = 0
        self.dma_tiles = {}

    def _newsem(self, name):
        self.nsem += 1
        return self.nc.alloc_semaphore(name)

    def op(self, eng, fn, reads=(), writes=(), dma=None):
        o = Op(eng, fn, dma is not None)
        deps = []
        for t in reads:
            if t.w is not None:
                deps.append(t.w)
        for t in writes:
            if t.w is not None:
                deps.append(t.w)
            deps.extend(t.rs)
        best = {}
        for d in deps:
            if d.is_dma:
                key = ("dma", d.sem.num)
                if key not in best or d.val > best[key].val:
                    best[key] = d
            else:
                if eng == "pe" and d.eng == "pe":
                    continue
                key = ("eng", d.eng)
                if key not in best or d.idx > best[key].idx:
                    best[key] = d
        for d in best.values():
            o.deps.append(d)
            d.need_inc = True
        if dma is not None:
            if dma.dsem is None:
                dma.dsem = self._newsem("d_" + dma.name)
            dma.dcnt += 16
            self.dma_tiles[id(dma)] = dma
            o.sem = dma.dsem
            o.val = dma.dcnt
            o.need_inc = True
        for t in reads:
            t.rs.append(o)
        for t in writes:
            t.w = o
            t.rs = []
        o.idx = len(self.q[eng])
        self.q[eng].append(o)
        return o

    def emit(self):
        nc = self.nc
        for e in ENGS:
            cnt = 0
            for o in self.q[e]:
                if o.is_dma or not o.need_inc:
                    continue
                if e not in self.esem:
                    self.esem[e] = self._newsem("e_" + e)
                cnt += 1
                o.sem = self.esem[e]
                o.val = cnt
        me = self

        def run(e, eng):
            waited = {}
            for o in me.q[e]:
                for d in o.deps:
                    k = d.sem.num
                    if waited.get(k, 0) >= d.val:
                        continue
                    eng.wait_ge(d.sem, d.val)
                    waited[k] = d.val
                ins = o.fn(eng)
                if o.need_inc:
                    ins.then_inc(o.sem, 16 if o.is_dma else 1)
            if e == "sp":
                for t in me.dma_tiles.values():
                    eng.wait_ge(t.dsem, t.dcnt)

        with nc.Block() as block:
            @block.sync
            def _(eng):
                run("sp", eng)

            @block.tensor
            def _(eng):
                run("pe", eng)

            @block.scalar
            def _(eng):
                run("act", eng)

            @block.vector
            def _(eng):
                run("dve", eng)

            @block.gpsimd
            def _(eng):
                run("pool", eng)


class Ring:
    def __init__(self, items):
        self.items = items
        self.i = 0

    def next(self):
        it = self.items[self.i % len(self.items)]
        self.i += 1
        return it


def build_nc():
    nc = bass.Bass("TRN2", target_bir_lowering=False)
    p = Prog(nc)

    def din(name, shape):
        return nc.dram_tensor(name, list(shape), F32, kind="ExternalInput").ap()

    x_d = din("x", [TOK, D])
    xh_d = din("xh", [2, D])
    xs_d = [din("ctxf", [CTX + 2, D]), din("ctxr", [CTX + 2, D]),
            din("xo0", [TOK + 2, D]), din("xo1", [TOK + 2, D]), din("xo2", [TOK + 2, D])]
    SL_NT = [2, 2, 16, 16, 16]
    SL_VEC = [1, 1, 0, 0, 0]
    cT_d = din("cT", [128, 8, 2])
    cst_d = din("cst", [128, 4, 128])
    flags_d = din("flags", [128, 16])
    w_ada_d = din("w_ada", [D, 6 * D])
    b_ada_d = din("b_adaT", [128, 48])
    w_in_d = din("w_in", [D, N_IN])
    wg_d = din("wg", [5, D, 8])
    gb_d = din("gb", [5, 8])
    cw_d = din("cwk", [5, 3, 512])
    cbk_d = din("cbk", [512])
    lng_d = din("gmlp_ln_g", [512])
    wsT_d = din("wsT", [128, 4, 128])
    bsT_d = din("bsT", [128, 4])
    qkw_d = din("qkcw", [128, 8, 3])
    qkb_d = din("qkcb", [128, 8])
    gbo_d = din("gbo", [16])
    mng_d = din("mix_norm_g", [D])
    w_out_d = din("w_out", [D, D])
    ln1g_d = din("ln1_g", [D]); ln1b_d = din("ln1_b", [D])
    w_up_d = din("w_up", [D, 2 * DFF])
    fcw_d = din("fcw", [128, 42, 3])
    fcb_d = din("fcb", [128, 42])
    w_dn_d = din("w_down", [DFF, D])
    ln2g_d = din("ln2_g", [D]); ln2b_d = din("ln2_b", [D])
    out_d = nc.dram_tensor("out", [TOK, D], F32, kind="ExternalOutput").ap()
    x1s_d = nc.dram_tensor("x1s", [TOK, D], F32).ap()

    def sb(name, shape, dt=F32):
        return nc.alloc_sbuf_tensor("s_" + name, list(shape), dt)

    PBALL = nc.alloc_psum_tensor("pball", [128, 4096], F32)
    PB = [PBALL[:, i * 512:(i + 1) * 512] for i in range(8)]
    PT = [T("pb%d" % i) for i in range(8)]

    cst = sb("cst", [128, 4, 128]); t_cst = T("cst")
    ident = cst[:, 0, :]; triU = cst[:, 1, :]; triL = cst[:, 2, :]; ones = cst[:, 3, :]
    cstb = sb("cstb", [128, 4, 128], BF16); t_cstb = T("cstb")
    identb = cstb[:, 0, :]; maskU = cstb[:, 1, :]; maskL = cstb[:, 2, :]; onesb = cstb[:, 3, :]
    flags = sb("flags", [128, 16]); t_flags = T("flags")
    p.op("sp", lambda e: e.dma_start(out=cst[:], in_=cst_d), writes=[t_cst], dma=t_cst)
    p.op("sp", lambda e: e.dma_start(out=flags[:], in_=flags_d), writes=[t_flags], dma=t_flags)
    p.op("dve", lambda e: e.tensor_copy(out=cstb[:], in_=cst[:]), reads=[t_cst], writes=[t_cstb])

    def load_const(name, shape, src):
        tl = sb(name, shape); tt = T(name)
        p.op("sp", lambda e: e.dma_start(out=tl[:], in_=src), writes=[tt], dma=tt)
        return tl, tt

    b_adaT, t_bada = load_const("b_adaT", [128, 48], b_ada_d)
    cT, t_cT = load_const("cT", [128, 8, 2], cT_d)
    bsT, t_bsT = load_const("bsT", [128, 4], bsT_d)
    qkw, t_qkw = load_const("qkw", [128, 8, 3], qkw_d)
    qkb, t_qkb = load_const("qkb", [128, 8], qkb_d)
    fcw, t_fcw = load_const("fcw", [128, 42, 3], fcw_d)
    fcb, t_fcb = load_const("fcb", [128, 42], fcb_d)
    gbo, t_gbo = load_const("gbo", [128, 16], gbo_d.partition_broadcast(128))
    gbs, t_gbs = load_const("gbs", [128, 40], gb_d.rearrange("a b -> (a b)").partition_broadcast(128))
    lng, t_lng = load_const("lng", [128, 512], lng_d.partition_broadcast(128))
    cbk32, t_cbk32 = load_const("cbk32", [1, 512], cbk_d.rearrange("(a b) -> a b", a=1))
    cbkb = sb("cbkb", [1, 512], BF16); t_cbkb = T("cbkb")
    p.op("dve", lambda e: e.tensor_copy(out=cbkb[:], in_=cbk32[:]), reads=[t_cbk32], writes=[t_cbkb])

    A_WIN, A_HT, A_KTOK, A_VEXT, A_X, A_END = 0, 24704, 41104, 49296, 57552, 73936
    ARENA = sb("arena", [128, A_END], BF16)
    ARENA32 = ARENA.bitcast(F32)
    t_win = T("w_in"); t_wout = T("w_out"); t_wup = T("w_up")
    w_in = ARENA[:, A_WIN:A_WIN + 8 * N_IN].rearrange("p (k n) -> p k n", k=8)
    wsTb = sb("wsTb", [128, 4, 128], BF16); t_wsTb = T("wsTb")
    wgb = sb("wgb", [128, 5, 8, 8], BF16); t_wgb = T("wgb")
    for k in range(8):
        p.op("pool", lambda e, k=k: e.dma_start(out=w_in[:, k, :], in_=w_in_d[k * 128:(k + 1) * 128, :]),
             writes=[t_win], dma=t_win)
    p.op("pool", lambda e: e.dma_start(out=wsTb[:], in_=wsT_d), writes=[t_wsTb], dma=t_wsTb)
    for s_ in range(5):
        p.op("pool", lambda e, s_=s_: e.dma_start(out=wgb[:, s_, :, :], in_=wg_d[s_].rearrange("(k p) n -> p k n", p=128)), writes=[t_wgb], dma=t_wgb)

    siluc = sb("siluc", [128, 8, 2]); t_siluc = T("siluc")
    p.op("act", lambda e: e.activation(out=siluc[:], in_=cT[:], func=AF.Sigmoid), reads=[t_cT], writes=[t_siluc])
    p.op("dve", lambda e: e.tensor_tensor(out=siluc[:], in0=siluc[:], in1=cT[:], op=ALU.mult), reads=[t_siluc, t_cT], writes=[t_siluc])
    mod = sb("mod", [128, 48, 2]); t_mod = T("mod")
    hT = ARENA[:, A_HT:A_HT + 8 * (TOK + 2)].rearrange("p (k n) -> p k n", k=8)
    t_hTt = [T("hT%d" % i) for i in range(NT)]; t_hTh = T("hTh"); t_hT_all = t_hTt + [t_hTh]
    wa = [ARENA32[:, A_HT // 2 + i * 4096:A_HT // 2 + (i + 1) * 4096].rearrange("p (k n) -> p k n", k=8) for i in range(2)]
    t_wa = [T("wa0"), T("wa1")]
    modrow = sb("modrow", [2, 512]); t_modrow = T("modrow")
    for blk in range(12):
        bi = blk % 2
        p.op("sp", lambda e, blk=blk, bi=bi: e.dma_start(
            out=wa[bi], in_=w_ada_d[:, blk * 512:(blk + 1) * 512].rearrange("(k p) n -> p k n", p=128)),
            writes=[t_wa[bi]], dma=t_wa[bi])
        pbk = blk % 2
        for k in range(8):
            p.op("pe", lambda e, bi=bi, k=k, pbk=pbk: e.matmul(PB[pbk][0:2, :], lhsT=siluc[:, k, :], rhs=wa[bi][:, k, :],
                                                              start=(k == 0), stop=(k == 7)), reads=[t_wa[bi], t_siluc], writes=[PT[pbk]])
        p.op("act", lambda e, pbk=pbk: e.activation(out=modrow[:], in_=PB[pbk][0:2, :], func=AF.Copy), reads=[PT[pbk]], writes=[t_modrow])
        for j in range(4):
            p.op("pe", lambda e, j=j, pbk=pbk: e.transpose(PB[2 + pbk][:, j * 2:j * 2 + 2], modrow[0:2, j * 128:(j + 1) * 128], ident[0:2, 0:2]),
                 reads=[t_modrow, t_cst], writes=[PT[2 + pbk]])
        p.op("dve", lambda e, blk=blk, pbk=pbk: e.tensor_tensor(
            out=mod[:, blk * 4:(blk + 1) * 4, :], in0=PB[2 + pbk][:, 0:8].rearrange("p (j v) -> p j v", v=2),
            in1=b_adaT[:, blk * 4:(blk + 1) * 4].unsqueeze(2).to_broadcast([128, 4, 2]), op=ALU.add),
            reads=[PT[2 + pbk], t_bada], writes=[t_mod])
    p.op("dve", lambda e: e.tensor_scalar_add(out=mod[:, 8:16, :], in0=mod[:, 8:16, :], scalar1=1.0), reads=[t_mod], writes=[t_mod])
    p.op("dve", lambda e: e.tensor_scalar_add(out=mod[:, 32:40, :], in0=mod[:, 32:40, :], scalar1=1.0), reads=[t_mod], writes=[t_mod])
    fence0 = sb("fence0", [128, 1])
    p.op("dve", lambda e: e.memset(fence0[:], 0.0), writes=t_wa + t_hT_all)
    xt = [sb("xt%d" % i, [128, D]) for i in range(2)]
    t_xt = [T("xt%d" % i) for i in range(2)]
    xt_ring = Ring(list(range(2)))
    ktok = ARENA[:, A_KTOK:A_KTOK + NT * 512].rearrange("p (a b) -> p a b", a=NT); t_ktok = [T("ktok%d" % i) for i in range(NT)]
    vext = ARENA[:, A_VEXT:A_VEXT + NT * 4 * 129].rearrange("p (a b c) -> p a b c", a=NT, b=4); t_vext = [T("vext%d" % i) for i in range(NT)]
    p.op("pool", lambda e: e.memset(ARENA[:, A_VEXT:A_VEXT + NT * 4 * 129], 1.0), writes=t_vext)
    PA = sb("pa", [128, 5136]); PAb = PA.bitcast(BF16)
    G = sb("G", [128, NT, 16]); t_G = T("G")
    GW = PA[:, 2064:3088].rearrange("p (a b c) -> p a b c", a=8, b=NT); t_GW = T("GW")
    hmod = sb("hmod", [128, 2, 8]); t_hmod = T("hmod")
    vs = [sb("vs%d" % i, [128, 4, 129], BF16) for i in range(2)]
    t_vs = [T("vs%d" % i) for i in range(2)]
    vs_ring = Ring(list(range(2)))
    S = PA[:, 0:1032].rearrange("p (a b) -> p a b", a=8); t_S = [T("S%d" % i) for i in range(8)]
    Sb = sb("Sb", [128, 8, 129], BF16); t_Sb = [T("Sb%d" % i) for i in range(8)]
    ctxS = PA[:, 1032:2064].rearrange("p (a b) -> p a b", a=8); t_ctxS = T("ctxS")
    initF = sb("initF", [128, 4, 129]); t_initF = T("initF")
    tmpS = sb("tmpS", [128, 4, 129]); t_tmpS = T("tmpS")
    Wk = ARENA[:, A_X:A_X + 12288].rearrange("p (a b c) -> p a b c", a=3, b=8); t_Wk = T("Wk")
    cwbc = ARENA32[:, (A_X + 12288) // 2:(A_X + 12288) // 2 + 1536].rearrange("p (a b) -> p a b", a=3); t_cwbc = T("cwbc")
    sgk = ARENA32[:, (A_X + 15360) // 2:(A_X + 15360) // 2 + 512]; t_sgk = T("sgk")

    kcol = 1536
    vcol = 2048

    def transposes(src_d, row0, nrows, col0, vec, gate=None, ti=0):
        xi = xt_ring.next()
        hi = hx_ring.next() if gate is not None else None
        p.op("sp", lambda e: e.dma_start(out=xt[xi][0:nrows, :], in_=src_d[row0:row0 + nrows, :]), writes=[t_xt[xi]], dma=t_xt[xi])
        for half in range(2):
            pbk = half
            for k4 in range(4):
                k = half * 4 + k4
                p.op("pe", lambda e, k=k, k4=k4, pbk=pbk: e.transpose(PB[pbk][:, k4 * 128:k4 * 128 + nrows], xt[xi][0:nrows, k * 128:(k + 1) * 128],
                                                                      ident[0:nrows, 0:nrows]),
                     reads=[t_xt[xi], t_cst], writes=[PT[pbk]])
            for k4 in range(4):
                k = half * 4 + k4
                p.op("act", lambda e, k=k, k4=k4, pbk=pbk: e.activation(
                    out=hT[:, k, col0:col0 + nrows], in_=PB[pbk][:, k4 * 128:k4 * 128 + nrows], func=AF.Identity,
                    bias=mod[:, k, vec:vec + 1], scale=mod[:, 8 + k, vec:vec + 1]),
                    reads=[PT[pbk], t_mod], writes=[t_hTt[ti]])
                if gate is not None:
                    p.op("act", lambda e, k=k, k4=k4, pbk=pbk: e.activation(
                        out=hx32[hi][:, k, 0:nrows], in_=PB[pbk][:, k4 * 128:k4 * 128 + nrows], func=AF.Identity,
                        bias=mod[:, k, vec:vec + 1], scale=mod[:, 8 + k, vec:vec + 1]),
                        reads=[PT[pbk], t_mod], writes=[t_hx32[hi]])
        if gate is not None:
            w32, t_w32, ng, dsts = gate
            for k in range(8):
                p.op("pe", lambda e, k=k: e.matmul(PB[4][:, 0:ng], lhsT=hx32[hi][:, k, 0:nrows], rhs=w32[:, k, :],
                                                  start=(k == 0), stop=(k == 7)), reads=[t_hx32[hi], t_w32], writes=[PT[4]])
            for (dst, c0, c1) in dsts:
                p.op("dve", lambda e, dst=dst, c0=c0, c1=c1: e.tensor_copy(out=dst, in_=PB[4][:, c0:c1]), reads=[PT[4]], writes=[t_G])

    def halos(src_rows, vec, fcol, ncols):
        xi = xt_ring.next()
        for r in range(2):
            p.op("sp", lambda e, r=r: e.dma_start(out=xt[xi][r:r + 1, :], in_=src_rows[r]), writes=[t_xt[xi]], dma=t_xt[xi])
        for r in range(2):
            p.op("dve", lambda e, r=r: e.tensor_scalar_mul(out=hmod[:, 0, :], in0=mod[:, 0:8, vec], scalar1=flags[:, fcol + r:fcol + r + 1]),
                 reads=[t_mod, t_flags], writes=[t_hmod])
            p.op("dve", lambda e, r=r: e.tensor_scalar_mul(out=hmod[:, 1, :], in0=mod[:, 8:16, vec], scalar1=flags[:, fcol + r:fcol + r + 1]),
                 reads=[t_mod, t_flags], writes=[t_hmod])
            for k in range(8):
                p.op("pe", lambda e, k=k: e.transpose(PB[0][:, k * 2:k * 2 + 2], xt[xi][0:2, k * 128:(k + 1) * 128], ident[0:2, 0:2]),
                     reads=[t_xt[xi], t_cst], writes=[PT[0]])
            c = 0 if r == 0 else ncols + 1
            for k in range(8):
                p.op("act", lambda e, k=k, r=r, c=c: e.activation(
                    out=hT[:, k, c:c + 1], in_=PB[0][:, k * 2 + r:k * 2 + r + 1], func=AF.Identity,
                    bias=hmod[:, 0, k:k + 1], scale=hmod[:, 1, k:k + 1]),
                    reads=[PT[0], t_hmod], writes=[t_hTh])

    def gate_math(nt, bi_ap, bf_ap, ncol):
        li = GW[:, 4, 0:nt, 0:ncol]; gf = GW[:, 5, 0:nt, 0:ncol]; bb = GW[:, 6, 0:nt, 0:ncol]; bl = GW[:, 7, 0:nt, 0:ncol]
        bc = lambda ap: ap.unsqueeze(1).to_broadcast([128, nt, ncol])
        p.op("dve", lambda e: e.tensor_tensor(out=li, in0=G[:, 0:nt, 0:ncol], in1=bc(bi_ap), op=ALU.add), reads=[t_G, t_gbo, t_gbs], writes=[t_GW])
        p.op("dve", lambda e: e.tensor_tensor(out=gf, in0=G[:, 0:nt, 8:8 + ncol], in1=bc(bf_ap), op=ALU.add), reads=[t_G, t_gbo, t_gbs], writes=[t_GW])
        p.op("act", lambda e: e.activation(out=gf, in_=gf, func=AF.Exp, scale=-1.0), reads=[t_GW], writes=[t_GW])
        p.op("act", lambda e: e.activation(out=gf, in_=gf, func=AF.Ln, bias=1.0), reads=[t_GW], writes=[t_GW])
        for h in range(ncol // 4):
            tri = triU if h == 0 else triL
            p.op("pe", lambda e, h=h, tri=tri: e.matmul(PB[6][:, h * 64:h * 64 + nt * 4], lhsT=tri, rhs=GW[:, 5, 0:nt, h * 4:h * 4 + 4],
                                                         start=True, stop=True), reads=[t_GW, t_cst], writes=[PT[6]])
        p.op("pe", lambda e: e.matmul(PB[6][:, 256:256 + nt * ncol], lhsT=ones, rhs=gf, start=True, stop=True), reads=[t_GW, t_cst], writes=[PT[6]])
        for h in range(ncol // 4):
            p.op("dve", lambda e, h=h: e.tensor_scalar_mul(out=GW[:, 6, 0:nt, h * 4:h * 4 + 4],
                                                           in0=PB[6][:, h * 64:h * 64 + nt * 4].rearrange("p (a b) -> p a b", b=4), scalar1=-1.0),
                 reads=[PT[6]], writes=[t_GW])
        p.op("dve", lambda e: e.tensor_scalar_mul(out=bl, in0=PB[6][:, 256:256 + nt * ncol].rearrange("p (a b) -> p a b", b=ncol), scalar1=-1.0),
             reads=[PT[6]], writes=[t_GW])
        p.op("dve", lambda e: e.tensor_tensor(out=li, in0=li, in1=bb, op=ALU.subtract), reads=[t_GW], writes=[t_GW])
        p.op("act", lambda e: e.activation(out=GW[:, 0, 0:nt, 0:ncol], in_=li, func=AF.Exp), reads=[t_GW], writes=[t_GW])
        p.op("act", lambda e: e.activation(out=GW[:, 1, 0:nt, 0:ncol], in_=bb, func=AF.Exp, bias=LN_QSCALE), reads=[t_GW], writes=[t_GW])
        p.op("dve", lambda e: e.tensor_tensor(out=li, in0=li, in1=bl, op=ALU.add), reads=[t_GW], writes=[t_GW])
        p.op("act", lambda e: e.activation(out=GW[:, 2, 0:nt, 0:ncol], in_=li, func=AF.Exp), reads=[t_GW], writes=[t_GW])
        p.op("act", lambda e: e.activation(out=GW[:, 3, 0:nt, 0:ncol], in_=bl, func=AF.Exp), reads=[t_GW], writes=[t_GW])

    def state_update(i, h, gcol, sidx, pbk, first=False):
        vi = vs_ring.next()
        p.op("act", lambda e: e.activation(out=vs[vi][:, h, :], in_=vext[:, i, h, :], func=AF.Copy, scale=GW[:, 2, i, gcol:gcol + 1]),
             reads=[t_vext[i], t_GW], writes=[t_vs[vi]])
        p.op("pe", lambda e: e.matmul(PB[pbk][:, (h % 2) * 256:(h % 2) * 256 + 129], lhsT=ktok[:, i, h * 128:(h + 1) * 128], rhs=vs[vi][:, h, :],
                                      start=True, stop=True), reads=[t_ktok[i], t_vs[vi]], writes=[PT[pbk]])
        p.op("dve", lambda e: e.scalar_tensor_tensor(out=S[:, sidx, :], in0=S[:, sidx, :], scalar=GW[:, 3, i, gcol:gcol + 1],
                                                     in1=PB[pbk][:, (h % 2) * 256:(h % 2) * 256 + 129], op0=ALU.mult, op1=ALU.add),
             reads=[t_S[sidx], t_GW, PT[pbk]], writes=[t_S[sidx]])

    p.op("pool", lambda e: e.memset(PA[:, 0:1032], 0.0), writes=t_S)
    p.op("pool", lambda e: e.memset(initF[:].rearrange("p a b -> p (a b)"), 0.0), writes=[t_initF])

    def chain_switch(kk):
        swc = flags[:, 12 + kk:13 + kk]
        s4 = S[:, 0:4, :]
        rd = [t_flags, t_initF, t_ctxS, t_tmpS] + t_S[0:4]
        p.op("dve", lambda e: e.scalar_tensor_tensor(out=initF[:], in0=s4, scalar=swc, in1=initF[:], op0=ALU.mult, op1=ALU.add),
             reads=rd, writes=[t_initF])
        p.op("dve", lambda e: e.tensor_tensor(out=tmpS[:], in0=ctxS[:, 4:8, :], in1=s4, op=ALU.subtract), reads=rd, writes=[t_tmpS])
        p.op("dve", lambda e: e.scalar_tensor_tensor(out=s4, in0=tmpS[:], scalar=swc, in1=s4, op0=ALU.mult, op1=ALU.add),
             reads=rd, writes=t_S[0:4])

    for sl in range(5):
        nt = SL_NT[sl]; ncols = nt * 128; vec = SL_VEC[sl]; src = xs_d[sl]
        p.op("sp", lambda e, sl=sl: e.dma_start(out=ARENA32[:, (A_X + 12288) // 2:(A_X + 12288) // 2 + 1536],
                                                  in_=cw_d[sl].rearrange("a b -> (a b)").partition_broadcast(128)),
             writes=[t_cwbc], dma=t_cwbc)
        n_b = 0
        for j in ((0, 1, 2) if sl == 0 else (0, 2)):
            for k in range(8):
                eng = "pool" if (n_b % 4 == 3) else "dve"
                n_b += 1
                p.op(eng, lambda e, j=j, k=k: e.tensor_tensor(out=Wk[:, j, k, :], in0=w_in[:, k, kcol:kcol + 512], in1=cwbc[:, j, :], op=ALU.mult),
                     reads=[t_win, t_cwbc], writes=[t_Wk])
        halos([src[0:1, :], src[ncols + 1:ncols + 2, :]], vec, 2 + 2 * sl, ncols)

        def slot_T(i, sl=sl, src=src, vec=vec):
            transposes(src, 1 + i * 128, 128, 1 + i * 128, vec, ti=i)

        def win(i, nt=nt):
            return [t_hTt[i], t_hTt[i - 1] if i > 0 else t_hTh, t_hTt[i + 1] if i < nt - 1 else t_hTh]
        if sl == 2:
            chain_switch(0)
        slot_T(0)
        if nt > 1:
            slot_T(1)
        for i in range(nt):
            if i + 2 < nt:
                slot_T(i + 2)
            n = 0
            for j in range(3):
                for k in range(8):
                    p.op("pe", lambda e, i=i, j=j, k=k, n=n: e.matmul(PB[2][:], lhsT=hT[:, k, i * 128 + j:i * 128 + j + 128], rhs=Wk[:, j, k, :],
                                                                       start=(n == 0), stop=False), reads=win(i) + [t_Wk], writes=[PT[2]])
                    n += 1
            p.op("pe", lambda e: e.matmul(PB[2][:], lhsT=onesb[0:1, :], rhs=cbkb[:], start=False, stop=True), reads=[t_cstb, t_cbkb], writes=[PT[2]])
            p.op("act", lambda e: e.activation(out=sgk, in_=PB[2][:], func=AF.Sigmoid), reads=[PT[2]], writes=[t_sgk])
            p.op("dve", lambda e, i=i: e.tensor_tensor(out=ktok[:, i, :], in0=PB[2][:], in1=sgk, op=ALU.mult), reads=[PT[2], t_sgk], writes=[t_ktok[i]])
            for k in range(8):
                p.op("pe", lambda e, i=i, k=k: e.matmul(PB[3][:], lhsT=hT[:, k, 1 + i * 128:1 + (i + 1) * 128], rhs=w_in[:, k, vcol:vcol + 512],
                                                         start=(k == 0), stop=(k == 7)), reads=[t_hTt[i], t_win], writes=[PT[3]])
            p.op("act", lambda e, i=i: e.activation(out=vext[:, i, :, 0:128], in_=PB[3][:].rearrange("p (a b) -> p a b", b=128), func=AF.Copy),
                 reads=[PT[3]], writes=[t_vext[i]])
            for k in range(8):
                p.op("pe", lambda e, i=i, k=k, sl=sl: e.matmul(PB[4][:, 0:8], lhsT=hT[:, k, 1 + i * 128:1 + (i + 1) * 128], rhs=wgb[:, sl, k, :],
                                                                start=(k == 0), stop=(k == 7)), reads=[t_hTt[i], t_wgb], writes=[PT[4]])
            p.op("dve", lambda e, i=i: e.tensor_copy(out=G[:, i, 0:4], in_=PB[4][:, 0:4]), reads=[PT[4]], writes=[t_G])
            p.op("dve", lambda e, i=i: e.tensor_copy(out=G[:, i, 8:12], in_=PB[4][:, 4:8]), reads=[PT[4]], writes=[t_G])
        gate_math(nt, gbs[:, sl * 8:sl * 8 + 4], gbs[:, sl * 8 + 4:sl * 8 + 8], 4)
        for i in range(nt):
            p.op("pool", lambda e, i=i: e.tensor_tensor(out=vext[:, i, :, 0:128], in0=vext[:, i, :, 0:128],
                                                        in1=GW[:, 2, i, 0:4].unsqueeze(2).to_broadcast([128, 4, 128]), op=ALU.mult),
                 reads=[t_vext[i], t_GW], writes=[t_vext[i]])
            p.op("pool", lambda e, i=i: e.tensor_copy(out=vext[:, i, :, 128], in_=GW[:, 2, i, 0:4]), reads=[t_vext[i], t_GW], writes=[t_vext[i]])
        for i in range(nt):
            for h in range(4):
                pbk = 5 + (h // 2)
                p.op("pe", lambda e, i=i, h=h, pbk=pbk: e.matmul(PB[pbk][:, (h % 2) * 256:(h % 2) * 256 + 129], lhsT=ktok[:, i, h * 128:(h + 1) * 128], rhs=vext[:, i, h, :],
                                                                 start=True, stop=True), reads=[t_ktok[i], t_vext[i]], writes=[PT[pbk]])
                p.op("dve", lambda e, i=i, h=h, pbk=pbk: e.scalar_tensor_tensor(out=S[:, h, :], in0=S[:, h, :], scalar=GW[:, 3, i, h:h + 1],
                                                                                in1=PB[pbk][:, (h % 2) * 256:(h % 2) * 256 + 129], op0=ALU.mult, op1=ALU.add),
                     reads=[t_S[h], t_GW, PT[pbk]], writes=[t_S[h]])
        if sl == 0:
            p.op("dve", lambda e: e.tensor_copy(out=ctxS[:, 0:4, :], in_=S[:, 0:4, :]), reads=t_S[0:4], writes=[t_ctxS])
            p.op("pool", lambda e: e.memset(S[:, 0:4, :], 0.0), reads=[t_ctxS], writes=t_S[0:4])
        elif sl == 1:
            p.op("dve", lambda e: e.tensor_copy(out=ctxS[:, 4:8, :], in_=S[:, 0:4, :]), reads=t_S[0:4], writes=[t_ctxS])
            p.op("dve", lambda e: e.tensor_copy(out=S[:, 0:4, :], in_=ctxS[:, 0:4, :]), reads=[t_ctxS], writes=t_S[0:4])
        else:
            chain_switch(sl - 1)
    p.op("pool", lambda e: e.memset(vext[:, :, :, 128], 1.0), writes=t_vext)
    p.op("dve", lambda e: e.tensor_copy(out=S[:, 4:8, :], in_=S[:, 0:4, :]), reads=t_S[0:4], writes=t_S[4:8])
    p.op("dve", lambda e: e.tensor_copy(out=S[:, 0:4, :], in_=initF[:]), reads=[t_initF] + t_S[4:8], writes=t_S[0:4])
    p.op("act", lambda e: e.activation(out=Sb[:], in_=S[:], func=AF.Copy), reads=t_S, writes=t_Sb)
    DBG = globals().get("DBG_STAGE", "NONE")
    ya_d = nc.dram_tensor("ya_s", [NT, 128, 512], F32).ap()
    og_d = nc.dram_tensor("og_s", [NT, 128, 512], F32).ap()
    qk_d = nc.dram_tensor("qk_s", [NT, 128, 8 * 128], BF16).ap()
    t_yad = [T("yad%d" % i) for i in range(NT)]; t_ogd = [T("ogd%d" % i) for i in range(NT)]; t_qkd = [T("qkd%d" % i) for i in range(NT)]
    X32 = A_X // 2

    def xf(off, n):
        return ARENA32[:, X32 + off:X32 + off + n]

    gu = xf(0, 512); t_gu = T("gu")
    gv = xf(512, 512); t_gv = T("gv")
    t1 = xf(1024, 512); t_t1 = T("t1")
    ya_t = [xf(1536 + i * 512, 512) for i in range(2)]; t_ya = [T("ya%d" % i) for i in range(2)]
    og_t = [xf(2560 + i * 512, 512) for i in range(2)]; t_og = [T("og%d" % i) for i in range(2)]
    cv = [xf(3584 + i * 128, 128) for i in range(2)]; t_cv = [T("cv%d" % i) for i in range(2)]
    sgq = [xf(3840 + i * 128, 128) for i in range(2)]; t_sgq = [T("sgq%d" % i) for i in range(2)]
    XB = A_X + 2 * 4096
    vn = ARENA[:, XB:XB + 512]; t_vn = T("vn")
    qk_t = [ARENA[:, XB + 512 + i * 1024:XB + 512 + (i + 1) * 1024].rearrange("p (a b) -> p a b", a=8) for i in range(2)]
    t_qk = [T("qk%d" % i) for i in range(2)]
    stats = sb("stats", [128, 8, 6]); t_stats = T("stats")
    mv = sb("mv", [128, 8, 2]); t_mv = T("mv")
    rstd = sb("rstd", [128, 8]); t_rstd = T("rstd")
    PB7b = PB[7].bitcast(BF16)

    def rstd_of(n, mv_=None, t_mv_=None, rstd_=None, t_rstd_=None):
        mv_ = mv if mv_ is None else mv_; t_mv_ = t_mv if t_mv_ is None else t_mv_
        rstd_ = rstd if rstd_ is None else rstd_; t_rstd_ = t_rstd if t_rstd_ is None else t_rstd_
        p.op("act", lambda e: e.activation(out=rstd_[:, 0:n], in_=mv_[:, 0:n, 1], func=AF.Ln, bias=EPS), reads=[t_mv_], writes=[t_rstd_])
        p.op("act", lambda e: e.activation(out=rstd_[:, 0:n], in_=rstd_[:, 0:n], func=AF.Exp, scale=-0.5), reads=[t_rstd_], writes=[t_rstd_])

    def gelu_from_psum(pb, tpb, dst, tdst):
        p.op("act", lambda e: e.activation(out=t1, in_=pb[:], func=AF.Square), reads=[tpb], writes=[t_t1])
        p.op("dve", lambda e: e.tensor_scalar(out=t1, in0=t1, scalar1=0.044715, scalar2=1.0, op0=ALU.mult, op1=ALU.add), reads=[t_t1], writes=[t_t1])
        p.op("dve", lambda e: e.tensor_tensor(out=t1, in0=t1, in1=pb[:], op=ALU.mult), reads=[t_t1, tpb], writes=[t_t1])
        p.op("act", lambda e: e.activation(out=t1, in_=t1, func=AF.Sigmoid, scale=GC), reads=[t_t1], writes=[t_t1])
        p.op("dve", lambda e: e.tensor_tensor(out=dst, in0=t1, in1=pb[:], op=ALU.mult), reads=[t_t1, tpb], writes=[tdst])

    halos([xh_d[0:1, :], xh_d[1:2, :]], 0, 0, TOK)

    def own_T(i):
        transposes(x_d, i * 128, 128, 1 + i * 128, 0, ti=i)

    def owin(i):
        return [t_hTt[i], t_hTt[i - 1] if i > 0 else t_hTh, t_hTt[i + 1] if i < NT - 1 else t_hTh]
    own_T(0)
    own_T(1)

    def proj_tok(i, pbk, c0, n):
        for k in range(8):
            p.op("pe", lambda e, k=k: e.matmul(PB[pbk][:, 0:n], lhsT=hT[:, k, 1 + i * 128:1 + (i + 1) * 128], rhs=w_in[:, k, c0:c0 + n],
                                              start=(k == 0), stop=(k == 7)), reads=[t_hTt[i], t_win], writes=[PT[pbk]])

    def gen_G(i):
        sl2 = i % 2
        proj_tok(i, 2, 0, 512)
        gelu_from_psum(PB[2], PT[2], gu, t_gu)
        yield
        proj_tok(i, 3, 512, 512)
        gelu_from_psum(PB[3], PT[3], gv, t_gv)
        yield
        for g in range(4):
            p.op("dve", lambda e, g=g: e.bn_stats(out=stats[:, g, :], in_=gv[:, g * 128:(g + 1) * 128]), reads=[t_gv], writes=[t_stats])
        for g in range(4):
            p.op("dve", lambda e, g=g: e.bn_aggr(out=mv[:, g, :], in_=stats[:, g, :]), reads=[t_stats], writes=[t_mv])
        rstd_of(4)
        yield
        for g in range(4):
            p.op("dve", lambda e, g=g: e.tensor_scalar(out=gv[:, g * 128:(g + 1) * 128], in0=gv[:, g * 128:(g + 1) * 128],
                                                       scalar1=mv[:, g, 0:1], scalar2=rstd[:, g:g + 1], op0=ALU.subtract, op1=ALU.mult),
                 reads=[t_gv, t_mv, t_rstd], writes=[t_gv])
        p.op("pool", lambda e: e.tensor_tensor(out=vn, in0=gv, in1=lng[:], op=ALU.mult), reads=[t_gv, t_lng], writes=[t_vn])
        yield
        for g in range(4):
            p.op("pe", lambda e, g=g: e.matmul(PB[4][:, g * 128:(g + 1) * 128], lhsT=wsTb[:, g, :], rhs=vn[:, g * 128:(g + 1) * 128],
                                              start=True, stop=True), reads=[t_wsTb, t_vn], writes=[PT[4]])
        for g in range(4):
            p.op("dve", lambda e, g=g: e.scalar_tensor_tensor(out=ya_t[sl2][:, g * 128:(g + 1) * 128], in0=PB[4][:, g * 128:(g + 1) * 128],
                                                              scalar=bsT[:, g:g + 1], in1=gu[:, g * 128:(g + 1) * 128], op0=ALU.add, op1=ALU.mult),
                 reads=[PT[4], t_bsT, t_gu], writes=[t_ya[sl2]])
        p.op("sp", lambda e, i=i: e.dma_start(out=ya_d[i], in_=ya_t[sl2]), reads=[t_ya[sl2]], writes=[t_yad[i]], dma=t_ya[sl2])
        yield

    def gen_Q(i):
        sl2 = i % 2
        proj_tok(i, 6, vcol, 512)
        p.op("act", lambda e, i=i: e.activation(out=vext[:, i, :, 0:128], in_=PB[6][:].rearrange("p (a b) -> p a b", b=128), func=AF.Copy),
             reads=[PT[6]], writes=[t_vext[i]])
        proj_tok(i, 7, 3072, 16)
        p.op("dve", lambda e, i=i: e.tensor_copy(out=G[:, i, :], in_=PB[7][:, 0:16]), reads=[PT[7]], writes=[t_G])
        yield
        proj_tok(i, 5, 2560, 512)
        p.op("act", lambda e: e.activation(out=og_t[sl2], in_=PB[5][:], func=AF.Sigmoid), reads=[PT[5]], writes=[t_og[sl2]])
        p.op("sp", lambda e, i=i: e.dma_start(out=og_d[i], in_=og_t[sl2]), reads=[t_og[sl2]], writes=[t_ogd[i]], dma=t_og[sl2])
        yield
        def qk_mm(hc):
            pbk = 5 + hc % 2
            c2 = hc % 2
            for k in range(8):
                p.op("pe", lambda e, k=k: e.matmul(PB[pbk][:, 0:130], lhsT=w_in[:, k, 1024 + hc * 128:1024 + (hc + 1) * 128],
                                                  rhs=hT[:, k, i * 128:i * 128 + 130], start=(k == 0), stop=(k == 7)),
                     reads=owin(i) + [t_win], writes=[PT[pbk]])
            p.op("act", lambda e: e.activation(out=cv[c2], in_=PB[pbk][:, 0:128], func=AF.Identity,
                                               bias=qkb[:, hc:hc + 1], scale=qkw[:, hc, 0:1]), reads=[PT[pbk], t_qkw, t_qkb], writes=[t_cv[c2]])

        def qk_ew(hc):
            pbk = 5 + hc % 2
            c2 = hc % 2
            for j in (1, 2):
                p.op("dve", lambda e, j=j: e.scalar_tensor_tensor(out=cv[c2], in0=PB[pbk][:, j:j + 128], scalar=qkw[:, hc, j:j + 1], in1=cv[c2],
                                                                  op0=ALU.mult, op1=ALU.add), reads=[PT[pbk], t_qkw, t_cv[c2]], writes=[t_cv[c2]])
            p.op("act", lambda e: e.activation(out=sgq[c2], in_=cv[c2], func=AF.Sigmoid), reads=[t_cv[c2]], writes=[t_sgq[c2]])
            p.op("dve", lambda e: e.tensor_tensor(out=qk_t[sl2][:, hc, :], in0=cv[c2], in1=sgq[c2], op=ALU.mult),
                 reads=[t_cv[c2], t_sgq[c2]], writes=[t_qk[sl2]])

        qk_mm(0)
        for hc in range(8):
            if hc + 1 < 8:
                qk_mm(hc + 1)
            qk_ew(hc)
            yield
        for h in range(4):
            p.op("pe", lambda e, h=h: e.transpose(PB7b[:, 512 + h * 128:512 + (h + 1) * 128], qk_t[sl2][:, 4 + h, :], identb),
                 reads=[t_qk[sl2], t_cstb], writes=[PT[7]])
        p.op("act", lambda e, i=i: e.activation(out=ktok[:, i, :], in_=PB7b[:, 512:1024], func=AF.Copy), reads=[PT[7]], writes=[t_ktok[i]])
        p.op("sp", lambda e, i=i: e.dma_start(out=qk_d[i], in_=qk_t[sl2].rearrange("p a b -> p (a b)")), reads=[t_qk[sl2]], writes=[t_qkd[i]], dma=t_qk[sl2])
        yield

    def drain(*gens):
        gens = [g for g in gens if g is not None]
        while gens:
            for g in list(gens):
                try:
                    next(g)
                except StopIteration:
                    gens.remove(g)

    def qk_reload(i):
        p.op("sp", lambda e: e.dma_start(out=hT[:, :, 1 + i * 128:1 + (i + 1) * 128], in_=qk_d[i].rearrange("p (a b) -> p a b", a=8)),
             reads=[t_qkd[i]], writes=[t_hTt[i]], dma=t_hTt[i])

    drain(gen_G(0))
    for i in range(NT):
        if i + 2 < NT:
            own_T(i + 2)
        drain(gen_Q(i), gen_G(i + 1) if i + 1 < NT else None)
        if i >= 1:
            qk_reload(i - 1)
    qk_reload(NT - 1)
    gate_math(NT, gbo[:, 0:8], gbo[:, 8:16], 8)

    if DBG == "A1":
        dbg = xt[0]; t_dbg = t_xt[0]

        def dump(r0, c0, n, src_ap, rd, eng="dve"):
            p.op(eng, lambda e: e.tensor_copy(out=dbg[:, 0:n], in_=src_ap), reads=rd + [t_dbg], writes=[t_dbg])
            o = p.op("sp", lambda e: e.dma_start(out=out_d[r0:r0 + 128, c0:c0 + n], in_=dbg[:, 0:n]), reads=[t_dbg], dma=t_dbg)
            p.final.append(("sp", o.sem, o.val))
        for (row, ti) in ((0, 0), (5, 15)):
            base = row * 128
            dump(base, 0, 1024, qk_t[ti % 2].rearrange("p a b -> p (a b)"), [t_qk[ti % 2]]) if ti == 15 else None
        dump(128, 0, 512, ktok[:, 0, :], [t_ktok[0]])
        dump(128, 512, 512, ktok[:, 15, :], [t_ktok[15]])
        dump(256, 0, 516, vext[:, 0, :, :].rearrange("p a b -> p (a b)"), [t_vext[0]])
        dump(384, 0, 256, G[:].rearrange("p a b -> p (a b)"), [t_G])
        dump(512, 0, 512, GW[:, 0, :, :].rearrange("p a b -> p (a b)")[:, 0:128], [t_GW]) if False else None
        for pl in range(4):
            dump(512, pl * 128, 128, GW[:, pl, :, :].rearrange("p a b -> p (a b)"), [t_GW])
        dump(768, 0, 512, ya_t[1], [t_ya[1]])
        dump(768, 512, 512, og_t[1], [t_og[1]])
        return nc, p

    H = ARENA32[:, 0:NT * 512].rearrange("p (a b) -> p a b", a=NT); t_H = [T("H%d" % i) for i in range(NT)]
    w_out = ARENA[:, 16384:16384 + 8 * D].rearrange("p (k n) -> p k n", k=8)
    t_qkT = T("qkT_unused")
    for k in range(8):
        p.op("pool", lambda e, k=k: e.dma_start(out=w_out[:, k, :], in_=w_out_d[k * 128:(k + 1) * 128, :]),
             writes=[t_wout, t_win], dma=t_wout)
    swb = [ARENA[:, XB + 4096 + i * 512:XB + 4096 + (i + 1) * 512].rearrange("p (a b) -> p a b", a=4) for i in range(2)]
    t_sw = [T("sw%d" % i) for i in range(2)]
    dn = [sb("dn%d" % i, [128, 4]) for i in range(2)]; t_dn = [T("dn%d" % i) for i in range(2)]
    r4 = [sb("r4%d" % i, [128, 4]) for i in range(2)]; t_r4 = [T("r4%d" % i) for i in range(2)]
    PP = [PBALL[:, (2 + 2 * d) * 512:(4 + 2 * d) * 512] for d in range(2)]
    PPT = [T("pp0"), T("pp1")]
    h_first = [True] * NT

    vsd = [sb("vsd%d" % i, [128, 4, 129], BF16) for i in range(2)]; t_vsd = [T("vsd%d" % i) for i in range(2)]

    def scan_pair(j):
        last = (j == NT - 1)
        steps = [(j, 0), (NT - 1 - j, 1)]
        css = {c: slice(1 + c * 128, 1 + (c + 1) * 128) for c, _ in steps}
        if not last:
            for c, d in steps:
                for h in range(4):
                    p.op("act", lambda e, c=c, d=d, h=h: e.activation(out=vsd[d][:, h, :], in_=vext[:, c, h, :], func=AF.Copy,
                                                                      scale=GW[:, 2, c, d * 4 + h:d * 4 + h + 1]),
                         reads=[t_vext[c], t_GW], writes=[t_vsd[d]])
        for c, d in steps:
            for h in range(4):
                p.op("pe", lambda e, c=c, d=d, h=h: e.matmul(PB[d][:, h * 128:(h + 1) * 128], lhsT=hT[:, 4 + h, css[c]], rhs=hT[:, h, css[c]], start=True, stop=True),
                     reads=[t_hTt[c]], writes=[PT[d]])
        for c, d in steps:
            mask = maskU if d == 0 else maskL
            for h in range(4):
                p.op("dve", lambda e, c=c, d=d, h=h, mask=mask: e.scalar_tensor_tensor(out=swb[d][:, h, :], in0=PB[d][:, h * 128:(h + 1) * 128],
                                                                                     scalar=GW[:, 0, c, d * 4 + h:d * 4 + h + 1], in1=mask, op0=ALU.mult, op1=ALU.mult),
                     reads=[PT[d], t_GW, t_cstb], writes=[t_sw[d]])
        for c, d in steps:
            for h in range(4):
                p.op("pe", lambda e, c=c, d=d, h=h: e.matmul(PP[d][:, h * 256:h * 256 + 129], lhsT=swb[d][:, h, :], rhs=vext[:, c, h, :], start=True, stop=False),
                     reads=[t_sw[d], t_vext[c]], writes=[PPT[d]])
                p.op("pe", lambda e, c=c, d=d, h=h: e.matmul(PP[d][:, h * 256:h * 256 + 129], lhsT=hT[:, h, css[c]], rhs=Sb[:, d * 4 + h, :], start=False, stop=True),
                     reads=[t_hTt[c], t_Sb[d * 4 + h]], writes=[PPT[d]])
        if not last:
            for c, d in steps:
                for h in range(4):
                    pbk = 6 + (h // 2)
                    p.op("pe", lambda e, c=c, d=d, h=h, pbk=pbk: e.matmul(PB[pbk][:, (h % 2) * 256:(h % 2) * 256 + 129], lhsT=ktok[:, c, h * 128:(h + 1) * 128],
                                                                         rhs=vsd[d][:, h, :], start=True, stop=True), reads=[t_ktok[c], t_vsd[d]], writes=[PT[pbk]])
                    sidx = d * 4 + h
                    p.op("dve", lambda e, c=c, h=h, pbk=pbk, sidx=sidx: e.scalar_tensor_tensor(out=S[:, sidx, :], in0=S[:, sidx, :], scalar=GW[:, 3, c, sidx:sidx + 1],
                                                                                           in1=PB[pbk][:, (h % 2) * 256:(h % 2) * 256 + 129], op0=ALU.mult, op1=ALU.add),
                         reads=[t_S[sidx], t_GW, PT[pbk]], writes=[t_S[sidx]])
        for c, d in steps:
            den = PP[d].rearrange("p (a b) -> p a b", b=256)[:, :, 128]
            ebs = GW[:, 1, c, d * 4:(d + 1) * 4]
            p.op("dve", lambda e, d=d, den=den, ebs=ebs: e.tensor_tensor(out=dn[d][:], in0=den, in1=ebs, op=ALU.mult), reads=[PPT[d], t_GW], writes=[t_dn[d]])
            p.op("dve", lambda e, d=d: e.scalar_tensor_tensor(out=r4[d][:], in0=dn[d][:], scalar=-1.0, in1=dn[d][:], op0=ALU.mult, op1=ALU.max),
                 reads=[t_dn[d]], writes=[t_r4[d]])
            p.op("dve", lambda e, d=d: e.tensor_scalar_max(out=r4[d][:], in0=r4[d][:], scalar1=1.0), reads=[t_r4[d]], writes=[t_r4[d]])
            p.op("dve", lambda e, d=d: e.reciprocal(out=r4[d][:], in_=r4[d][:]), reads=[t_r4[d]], writes=[t_r4[d]])
            p.op("dve", lambda e, d=d, ebs=ebs: e.tensor_tensor(out=r4[d][:], in0=r4[d][:], in1=ebs, op=ALU.mult), reads=[t_r4[d], t_GW], writes=[t_r4[d]])
        if not last:
            for c, d in steps:
                for h in range(4):
                    sidx = d * 4 + h
                    p.op("act", lambda e, sidx=sidx: e.activation(out=Sb[:, sidx, :], in_=S[:, sidx, :], func=AF.Copy), reads=[t_S[sidx]], writes=[t_Sb[sidx]])
        for c, d in steps:
            first = h_first[c]
            h_first[c] = False
            for h in range(4):
                hs = slice(h * 128, (h + 1) * 128)
                if first:
                    p.op("act", lambda e, c=c, d=d, h=h, hs=hs: e.activation(out=H[:, c, hs], in_=PP[d][:, h * 256:h * 256 + 128], func=AF.Copy, scale=r4[d][:, h:h + 1]),
                         reads=[PPT[d], t_r4[d]], writes=[t_H[c], t_win])
                else:
                    p.op("dve", lambda e, c=c, d=d, h=h, hs=hs: e.scalar_tensor_tensor(out=H[:, c, hs], in0=PP[d][:, h * 256:h * 256 + 128], scalar=r4[d][:, h:h + 1],
                                                                                     in1=H[:, c, hs], op0=ALU.mult, op1=ALU.add),
                         reads=[PPT[d], t_r4[d], t_H[c]], writes=[t_H[c]])

    for j in range(NT):
        scan_pair(j)

    A_WDN = 8 * 2 * DFF
    A_SP = A_WDN + NFC * D
    w_up = ARENA[:, 0:A_WDN].rearrange("p (k n) -> p k n", k=8)
    w_dn = ARENA[:, A_WDN:A_SP].rearrange("p (k n) -> p k n", k=NFC)
    t_wdn = T("w_dn")
    fence1 = sb("fence1", [128, 1])
    p.op("dve", lambda e: e.memset(fence1[:], 0.0), writes=t_hT_all + t_ktok + t_vext + [t_wup, t_wdn])
    for k in (5, 6, 7):
        p.op("pool", lambda e, k=k: e.dma_start(out=w_up[:, k, :], in_=w_up_d[k * 128:(k + 1) * 128, :]), writes=[t_wup], dma=t_wup)
    for k in range(14):
        p.op("pool", lambda e, k=k: e.dma_start(out=w_dn[:, k, :], in_=w_dn_d[k * 128:(k + 1) * 128, :]), writes=[t_wdn], dma=t_wdn)

    if DBG == "H":
        for c in range(NT):
            o = p.op("sp", lambda e, c=c: e.dma_start(out=out_d[c * 128:(c + 1) * 128, 0:512], in_=H[:, c, :]), reads=[t_H[c]], dma=t_H[c])
            p.final.append(("sp", o.sem, o.val))
        return nc, p

    x_old = [t_gu, t_gv, t_t1, t_vn, t_Wk, t_cwbc, t_sgk] + t_ya + t_og + t_cv + t_sgq + t_qk + t_sw
    mng = xf(0, 1024); ln1g = xf(1024, 1024); ln1b = xf(2048, 1024); g1bc = xf(3072, 1024)
    yab = xf(4096, 1024); zt = xf(5120, 1024); ogt = xf(6144, 512)
    diag = yab
    YB = A_X + 2 * 6656
    ybf_q = [ARENA[:, YB:YB + 1024], ARENA[:, A_X + 2 * 6144:A_X + 2 * 6144 + 1024]]
    yT = ARENA[:, YB + 1024:YB + 2048].rearrange("p (a b) -> p a b", a=8)
    t_mng = T("mng"); t_ln1g = T("ln1g"); t_ln1b = T("ln1b"); t_g1bc = T("g1bc"); t_yab = T("yab"); t_zt = T("zt")
    t_ogt = T("ogt"); t_diag = t_yab; t_ybf_q = [T("ybf0"), T("ybf1")]; t_yT = T("yT")
    x_new = [t_mng, t_ln1g, t_ln1b, t_g1bc, t_yab, t_zt, t_ogt, t_yT] + t_ybf_q
    fence = sb("fence", [128, 1])
    p.op("dve", lambda e: e.memset(fence[:], 0.0), writes=x_old + x_new)
    p.op("sp", lambda e: e.dma_start(out=mng, in_=mng_d.partition_broadcast(128)), writes=[t_mng], dma=t_mng)
    p.op("sp", lambda e: e.dma_start(out=ln1g, in_=ln1g_d.partition_broadcast(128)), writes=[t_ln1g], dma=t_ln1g)
    p.op("sp", lambda e: e.dma_start(out=ln1b, in_=ln1b_d.partition_broadcast(128)), writes=[t_ln1b], dma=t_ln1b)

    def bcast_row(base, dst, tdst, tdiag, dg):
        dg3 = dg.rearrange("p (a b) -> p a b", a=8)
        for k in range(8):
            p.op("dve", lambda e, k=k: e.tensor_scalar_mul(out=dg3[:, k, :], in0=ident, scalar1=mod[:, base + k, 0:1]),
                 reads=[t_cst, t_mod], writes=[tdiag])
        for hlf in range(2):
            for k4 in range(4):
                k = hlf * 4 + k4
                p.op("pe", lambda e, k=k, k4=k4, hlf=hlf: e.matmul(PB[2 + hlf][:, k4 * 128:(k4 + 1) * 128], lhsT=ones, rhs=dg3[:, k, :], start=True, stop=True),
                     reads=[t_cst, tdiag], writes=[PT[2 + hlf]])
            p.op("act", lambda e, hlf=hlf: e.activation(out=dst[:, hlf * 512:(hlf + 1) * 512], in_=PB[2 + hlf][:], func=AF.Copy),
                 reads=[PT[2 + hlf]], writes=[tdst])

    bcast_row(16, g1bc, t_g1bc, t_diag, diag)
    PB6b = PB[6].bitcast(BF16)

    def layer_norm_store(z, tz, g_ap, tg, b_ap, tb, dst_d, extra_w=(), aff="pool", priv=None):
        st_, t_st_, mv_, t_mv_, rs_, t_rs_ = priv if priv is not None else (stats, t_stats, mv, t_mv, rstd, t_rstd)
        for c in range(2):
            p.op("dve", lambda e, c=c: e.bn_stats(out=st_[:, c, :], in_=z[:, c * 512:(c + 1) * 512]), reads=[tz], writes=[t_st_])
        p.op("dve", lambda e: e.bn_aggr(out=mv_[:, 0, :], in_=st_[:, 0:2, :]), reads=[t_st_], writes=[t_mv_])
        rstd_of(1, mv_, t_mv_, rs_, t_rs_)
        p.op("dve", lambda e: e.tensor_scalar(out=z, in0=z, scalar1=mv_[:, 0, 0:1], scalar2=rs_[:, 0:1], op0=ALU.subtract, op1=ALU.mult),
             reads=[tz, t_mv_, t_rs_], writes=[tz])
        p.op(aff, lambda e: e.tensor_tensor(out=z, in0=z, in1=g_ap, op=ALU.mult), reads=[tz, tg], writes=[tz])
        p.op(aff, lambda e: e.tensor_tensor(out=z, in0=z, in1=b_ap, op=ALU.add), reads=[tz, tb], writes=[tz])
        return p.op("sp", lambda e: e.dma_start(out=dst_d, in_=z), reads=[tz], writes=list(extra_w), dma=tz)

    t_x1 = [T("x1_%d" % i) for i in range(NT)]
    ya_in = [PA[:, 3088 + i * 512:3088 + (i + 1) * 512] for i in range(2)]; t_ya_in = [T("ya_in%d" % i) for i in range(2)]
    og_in = [PA[:, 4112 + i * 512:4112 + (i + 1) * 512] for i in range(2)]; t_og_in = [T("og_in%d" % i) for i in range(2)]

    def loads_yaog(i):
        s2 = i % 2
        p.op("sp", lambda e: e.dma_start(out=ya_in[s2][:], in_=ya_d[i]), reads=[t_yad[i]], writes=[t_ya_in[s2]], dma=t_ya_in[s2])
        p.op("sp", lambda e: e.dma_start(out=og_in[s2][:], in_=og_d[i]), reads=[t_ogd[i]], writes=[t_og_in[s2]], dma=t_og_in[s2])

    def loads_x(i):
        s2 = i % 2
        p.op("sp", lambda e: e.dma_start(out=xt[s2][:], in_=x_d[i * 128:(i + 1) * 128, :]), writes=[t_xt[s2]], dma=t_xt[s2])

    def headnorm(i):
        s2 = i % 2
        src = [(ya_in[s2][:, g * 128:(g + 1) * 128], t_ya_in[s2]) for g in range(4)] + [(H[:, i, g * 128:(g + 1) * 128], t_H[i]) for g in range(4)]
        for g in range(8):
            p.op("dve", lambda e, g=g: e.bn_stats(out=stats[:, g, :], in_=src[g][0]), reads=[src[g][1]], writes=[t_stats])
        for g in range(8):
            p.op("dve", lambda e, g=g: e.bn_aggr(out=mv[:, g, :], in_=stats[:, g, :]), reads=[t_stats], writes=[t_mv])
        rstd_of(8)
        for g in range(8):
            p.op("dve", lambda e, g=g: e.tensor_scalar(out=yab[:, g * 128:(g + 1) * 128], in0=src[g][0],
                                                       scalar1=mv[:, g, 0:1], scalar2=rstd[:, g:g + 1], op0=ALU.subtract, op1=ALU.mult),
                 reads=[src[g][1], t_mv, t_rstd], writes=[t_yab])
        p.op("dve", lambda e: e.tensor_tensor(out=ybf_q[s2][:, 0:512], in0=yab[:, 0:512], in1=mng[:, 0:512], op=ALU.mult), reads=[t_yab, t_mng], writes=[t_ybf_q[s2]])
        p.op("dve", lambda e: e.tensor_tensor(out=yab[:, 512:1024], in0=yab[:, 512:1024], in1=mng[:, 512:1024], op=ALU.mult), reads=[t_yab, t_mng], writes=[t_yab])
        p.op("dve", lambda e: e.tensor_tensor(out=ybf_q[s2][:, 512:1024], in0=yab[:, 512:1024], in1=og_in[s2][:], op=ALU.mult), reads=[t_yab, t_og_in[s2]], writes=[t_ybf_q[s2]])

    st1 = sb("st1", [128, 2, 6]); mv1 = sb("mv1", [128, 1, 2]); rs1 = sb("rs1", [128, 1])
    ln1priv = (st1, T("st1"), mv1, T("mv1"), rs1, T("rs1"))

    def out_mm(i):
        s2 = i % 2
        for k in range(8):
            p.op("pe", lambda e, k=k: e.transpose(PB6b[:, k * 128:(k + 1) * 128], ybf_q[s2][:, k * 128:(k + 1) * 128], identb), reads=[t_ybf_q[s2], t_cstb], writes=[PT[6]])
        p.op("act", lambda e: e.activation(out=yT.rearrange("p a b -> p (a b)"), in_=PB6b[:, 0:1024], func=AF.Copy), reads=[PT[6]], writes=[t_yT])
        for hlf in range(2):
            for k in range(8):
                p.op("pe", lambda e, k=k, hlf=hlf: e.matmul(PB[2 + 2 * s2 + hlf][:], lhsT=yT[:, k, :], rhs=w_out[:, k, hlf * 512:(hlf + 1) * 512],
                                                            start=(k == 0), stop=(k == 7)), reads=[t_yT, t_wout], writes=[PT[2 + 2 * s2 + hlf]])

    def out_ln(i):
        s2 = i % 2
        for hlf in range(2):
            p.op("dve", lambda e, hlf=hlf: e.tensor_tensor(out=zt[:, hlf * 512:(hlf + 1) * 512], in0=PB[2 + 2 * s2 + hlf][:], in1=g1bc[:, hlf * 512:(hlf + 1) * 512], op=ALU.mult),
                 reads=[PT[2 + 2 * s2 + hlf], t_g1bc], writes=[t_zt])
        p.op("dve", lambda e: e.scalar_tensor_tensor(out=zt, in0=xt[s2][:], scalar=ALPHA, in1=zt, op0=ALU.mult, op1=ALU.add),
             reads=[t_xt[s2], t_zt], writes=[t_zt])
        dst = out_d[i * 128:(i + 1) * 128, :] if DBG == "X1" else x1s_d[i * 128:(i + 1) * 128, :]
        return layer_norm_store(zt, t_zt, ln1g, t_ln1g, ln1b, t_ln1b, dst, extra_w=[t_x1[i]], aff="pool", priv=ln1priv)

    loads_yaog(0); loads_x(0)
    if NT > 1:
        loads_yaog(1); loads_x(1)
    headnorm(0)
    if NT > 2:
        loads_yaog(2)
    out_mm(0)
    def wup_into_dead_H(k, h_tiles):
        p.op("pool", lambda e: e.dma_start(out=w_up[:, k, :], in_=w_up_d[k * 128:(k + 1) * 128, :]), writes=[t_wup] + h_tiles, dma=t_wup)

    for i in range(NT):
        if i + 1 < NT:
            headnorm(i + 1)
            if i + 1 == 5:
                wup_into_dead_H(0, t_H[0:6])
            elif i + 1 == 10:
                wup_into_dead_H(1, t_H[5:11])
            elif i + 1 == 15:
                wup_into_dead_H(2, t_H[10:16])
            if i + 3 < NT:
                loads_yaog(i + 3)
            out_mm(i + 1)
        o = out_ln(i)
        if i + 2 < NT:
            loads_x(i + 2)
    if DBG == "X1":
        return nc, p

    everything = ([t_win, t_wout] + t_hT_all + [ t_Wk, t_cwbc, t_sgk] + t_ktok + t_vext + t_H + x_old + x_new)
    ln2g = ARENA32[:, A_SP // 2:A_SP // 2 + 1024]; ln2b = ARENA32[:, A_SP // 2 + 1024:A_SP // 2 + 2048]
    g2bc = ARENA32[:, A_SP // 2 + 2048:A_SP // 2 + 3072]
    gT = ARENA[:, A_SP + 6144:A_SP + 6144 + NFC * 128].rearrange("p (a b) -> p a b", a=NFC)
    t_ln2g = T("ln2g"); t_ln2b = T("ln2b"); t_g2bc = T("g2bc"); t_gT = T("gT")
    t_fenceB = T("fenceB")
    newB = [t_ln2g, t_ln2b, t_g2bc, t_gT, t_fenceB]
    p.op("dve", lambda e: e.memset(fence[:], 0.0), writes=everything + newB)
    for k in (3, 4):
        p.op("pool", lambda e, k=k: e.dma_start(out=w_up[:, k, :], in_=w_up_d[k * 128:(k + 1) * 128, :]), reads=[t_fenceB], writes=[t_wup], dma=t_wup)
    t_wdn_late = T("w_dn_late")
    for k in range(14, NFC):
        p.op("pool", lambda e, k=k: e.dma_start(out=w_dn[:, k, :], in_=w_dn_d[k * 128:(k + 1) * 128, :]), reads=[t_fenceB], writes=[t_wdn_late], dma=t_wdn_late)
    p.op("sp", lambda e: e.dma_start(out=ln2g, in_=ln2g_d.partition_broadcast(128)), writes=[t_ln2g], dma=t_ln2g)
    p.op("sp", lambda e: e.dma_start(out=ln2b, in_=ln2b_d.partition_broadcast(128)), writes=[t_ln2b], dma=t_ln2b)
    bcast_row(40, g2bc, t_g2bc, t_xt[0], xt[0][:])
    h2T_q = [ARENA[:, A_SP + 6144:A_SP + 6144 + 2048].rearrange("p (a b) -> p a b", a=8), PAb[:, 6144:8192].rearrange("p (a b) -> p a b", a=8)]
    t_h2T_q = [T("h2T0"), T("h2T1")]
    gTf = [ARENA[:, A_SP + 8192 + i * 256:A_SP + 8192 + (i + 1) * 256] for i in range(2)]; t_gTf = [T("gTf%d" % i) for i in range(2)]
    tcv = [sb("tcv%d" % i, [128, 256]) for i in range(4)]; t_tcv = [T("tcv%d" % i) for i in range(4)]
    sgl = [sb("sgl%d" % i, [128, 256]) for i in range(2)]; t_sgl = [T("sgl%d" % i) for i in range(2)]
    zt2 = sb("zt2", [128, D])
    zt_tt = [zt2[:], PA[:, 2048:3072]]; t_zt_tt = [T("zt2a"), T("zt2b")]
    xt_q = [[xt[0][:], xt[1][:]], [PA[:, 0:1024], PA[:, 1024:2048]]]
    t_xt_q = [[t_xt[0], t_xt[1]], [T("xt2"), T("xt3")]]
    deadA = t_S + [t_ctxS, t_GW] + t_ya_in + t_og_in
    p.op("dve", lambda e: e.memset(fence[:], 0.0), writes=deadA + [t_gT] + t_h2T_q + t_gTf + t_zt_tt + t_xt_q[1])
    ACC = [[0, 1], [6, 7]]

    def ffn_loads(ip):
        q = ip % 2
        for tt, i in enumerate((2 * ip, 2 * ip + 1)):
            p.op("sp", lambda e, tt=tt, i=i: e.dma_start(out=xt_q[q][tt], in_=x1s_d[i * 128:(i + 1) * 128, :]), reads=[t_x1[i]], writes=[t_xt_q[q][tt]], dma=t_xt_q[q][tt])

    def ffn_transposes(ip):
        q = ip % 2
        for tt in range(2):
            for half in range(2):
                pbk = 2 + 2 * tt + half
                for k4 in range(4):
                    k = half * 4 + k4
                    p.op("pe", lambda e, k=k, k4=k4, pbk=pbk, tt=tt: e.transpose(PB[pbk][:, k4 * 128:(k4 + 1) * 128], xt_q[q][tt][:, k * 128:(k + 1) * 128], ident),
                         reads=[t_xt_q[q][tt], t_cst], writes=[PT[pbk]])
                for k4 in range(4):
                    k = half * 4 + k4
                    p.op("act", lambda e, k=k, k4=k4, pbk=pbk, tt=tt: e.activation(out=h2T_q[q][:, k, tt * 128:(tt + 1) * 128], in_=PB[pbk][:, k4 * 128:(k4 + 1) * 128],
                                                                                func=AF.Identity, bias=mod[:, 24 + k, 0:1], scale=mod[:, 32 + k, 0:1]),
                         reads=[PT[pbk], t_mod], writes=[t_h2T_q[q]])

    def ffn_fcloop(ip):
        q = ip % 2
        h2T = h2T_q[q]; t_h2T = t_h2T_q[q]

        def up(fc):
            par = fc % 2
            for which in range(2):
                ch = fc + which * NFC
                pbk = 2 + 2 * par + which
                for k in range(8):
                    p.op("pe", lambda e, k=k, ch=ch, pbk=pbk: e.matmul(PB[pbk][:, 0:256], lhsT=w_up[:, k, ch * 128:(ch + 1) * 128], rhs=h2T[:, k, :],
                                                                      start=(k == 0), stop=(k == 7)), reads=[t_wup, t_h2T], writes=[PT[pbk]])

        def ew(fc):
            par = fc % 2
            for which in range(2):
                ch = fc + which * NFC
                pbk = 2 + 2 * par + which
                tb = 2 * par + which
                Pv = PB[pbk][:, 0:256]
                P3 = Pv.rearrange("p (r w) -> p r w", w=64)
                t3 = tcv[tb][:].rearrange("p (r w) -> p r w", w=64)
                p.op("act", lambda e, ch=ch, Pv=Pv, tb=tb: e.activation(out=tcv[tb][:], in_=Pv, func=AF.Identity, bias=fcb[:, ch:ch + 1], scale=fcw[:, ch, 1:2]),
                     reads=[PT[pbk], t_fcw, t_fcb], writes=[t_tcv[tb]])
                p.op("dve", lambda e, ch=ch, P3=P3, t3=t3, pbk=pbk, tb=tb: e.scalar_tensor_tensor(out=t3[:, :, 1:64], in0=P3[:, :, 0:63], scalar=fcw[:, ch, 0:1], in1=t3[:, :, 1:64],
                                                                                  op0=ALU.mult, op1=ALU.add), reads=[PT[pbk], t_fcw, t_tcv[tb]], writes=[t_tcv[tb]])
                p.op("dve", lambda e, ch=ch, P3=P3, t3=t3, pbk=pbk, tb=tb: e.scalar_tensor_tensor(out=t3[:, :, 0:63], in0=P3[:, :, 1:64], scalar=fcw[:, ch, 2:3], in1=t3[:, :, 0:63],
                                                                                  op0=ALU.mult, op1=ALU.add), reads=[PT[pbk], t_fcw, t_tcv[tb]], writes=[t_tcv[tb]])
            tv, tg = 2 * par, 2 * par + 1
            p.op("act", lambda e: e.activation(out=sgl[par][:], in_=tcv[tg][:], func=AF.Silu), reads=[t_tcv[tg]], writes=[t_sgl[par]])
            p.op("dve", lambda e: e.tensor_tensor(out=gTf[par], in0=tcv[tv][:], in1=sgl[par][:], op=ALU.mult),
                 reads=[t_tcv[tv], t_sgl[par]], writes=[t_gTf[par]])

        def wd(fc):
            par = fc % 2
            for tt in range(2):
                for hlf in range(2):
                    bk = ACC[tt][hlf]
                    p.op("pe", lambda e, tt=tt, hlf=hlf, bk=bk: e.matmul(PB[bk][:], lhsT=gTf[par][:, tt * 128:(tt + 1) * 128],
                                                                         rhs=w_dn[:, fc, hlf * 512:(hlf + 1) * 512], start=(fc == 0), stop=(fc == NFC - 1)),
                         reads=[t_gTf[par], t_wdn if fc < 14 else t_wdn_late], writes=[PT[bk]])

        up(0)
        for fc in range(NFC):
            if fc + 1 < NFC:
                up(fc + 1)
            ew(fc)
            wd(fc)
            if ip + 1 < NT // 2:
                if fc == 2:
                    ffn_loads(ip + 1)

    def ffn_epilogue(ip):
        q = ip % 2
        for tt, i in enumerate((2 * ip, 2 * ip + 1)):
            z = zt_tt[tt]; tz = t_zt_tt[tt]
            for hlf in range(2):
                bk = ACC[tt][hlf]
                p.op("dve", lambda e, hlf=hlf, bk=bk, z=z: e.tensor_tensor(out=z[:, hlf * 512:(hlf + 1) * 512], in0=PB[bk][:], in1=g2bc[:, hlf * 512:(hlf + 1) * 512], op=ALU.mult),
                     reads=[PT[bk], t_g2bc], writes=[tz])
            p.op("dve", lambda e, tt=tt, z=z: e.scalar_tensor_tensor(out=z, in0=xt_q[q][tt], scalar=ALPHA, in1=z, op0=ALU.mult, op1=ALU.add),
                 reads=[t_xt_q[q][tt], tz], writes=[tz])
        for tt, i in enumerate((2 * ip, 2 * ip + 1)):
            layer_norm_store(zt_tt[tt], t_zt_tt[tt], ln2g, t_ln2g, ln2b, t_ln2b, out_d[i * 128:(i + 1) * 128, :])

    ffn_loads(0)
    ffn_transposes(0)
    for ip in range(NT // 2):
        ffn_fcloop(ip)
        if ip + 1 < NT // 2:
            ffn_transposes(ip + 1)
        ffn_epilogue(ip)
    return nc, p


def _consts():
    i = np.arange(128)
    ident = np.eye(128, dtype=np.float32)
    triU = (i[:, None] <= i[None, :]).astype(np.float32)
    triL = (i[:, None] >= i[None, :]).astype(np.float32)
    ones = np.ones((128, 128), np.float32)
    return np.ascontiguousarray(np.stack([ident, triU, triL, ones], axis=1))


def _pcol(v, nch):
    return np.ascontiguousarray(np.asarray(v, np.float32).reshape(nch, 128).T)


def make_in_maps(x, c, ctx, c_ctx, w_ada, b_ada, w_in, gmlp_ln_g, gmlp_ws, gmlp_bs, qk_conv_w, qk_conv_b,
                 b_igate, b_fgate, mix_norm_g, w_out, ln1_g, ln1_b, w_up, ffn_conv_w, ffn_conv_b, w_down, ln2_g, ln2_b):
    f = lambda a: np.ascontiguousarray(np.asarray(a, np.float32))
    x = f(x); ctx = f(ctx); w_in0 = f(w_in)[0]
    zero = np.zeros((1, D), np.float32)
    GOFF = 3072
    shared = {
        "cst": _consts(),
        "w_ada": f(w_ada)[0], "b_adaT": _pcol(f(b_ada)[0], 48), "w_in": w_in0,
        "cbk": f(qk_conv_b)[0, 512:1024],
        "gmlp_ln_g": f(gmlp_ln_g)[0].reshape(512),
        "wsT": np.ascontiguousarray(f(gmlp_ws)[0].transpose(2, 0, 1)),
        "bsT": np.ascontiguousarray(f(gmlp_bs)[0].T),
        "qkcw": np.ascontiguousarray(f(qk_conv_w)[0].reshape(3, 8, 128).transpose(2, 1, 0)),
        "qkcb": _pcol(f(qk_conv_b)[0], 8),
        "gbo": np.concatenate([f(b_igate)[0].reshape(8), f(b_fgate)[0].reshape(8)]),
        "mix_norm_g": f(mix_norm_g)[0], "w_out": f(w_out)[0], "ln1_g": f(ln1_g)[0], "ln1_b": f(ln1_b)[0],
        "w_up": f(w_up)[0],
        "fcw": np.ascontiguousarray(f(ffn_conv_w)[0].reshape(3, 42, 128).transpose(2, 1, 0)),
        "fcb": _pcol(f(ffn_conv_b)[0], 42),
        "w_down": f(w_down)[0], "ln2_g": f(ln2_g)[0], "ln2_b": f(ln2_b)[0],
    }
    taps = f(qk_conv_w)[0][:, 512:1024]
    bi = f(b_igate)[0]; bf_ = f(b_fgate)[0]

    def dir_params(d):
        wg = np.concatenate([w_in0[:, GOFF + d * 4:GOFF + d * 4 + 4], w_in0[:, GOFF + 8 + d * 4:GOFF + 8 + d * 4 + 4]], axis=1)
        gb = np.concatenate([bi[d], bf_[d]])
        cw = taps if d == 0 else taps[::-1]
        return wg, gb, cw

    maps = []
    for core in range(NCORE):
        b, r = divmod(core, 4)
        T0 = r * TOK
        xb = x[b]
        m = dict(shared)
        m["x"] = np.ascontiguousarray(xb[T0:T0 + TOK])
        lo = xb[T0 - 1:T0] if T0 > 0 else zero
        hi = xb[T0 + TOK:T0 + TOK + 1] if T0 + TOK < SEQ else zero
        m["xh"] = np.ascontiguousarray(np.concatenate([lo, hi], 0))
        flags = np.zeros((128, 16), np.float32)
        flags[:, 0] = 1.0 if T0 > 0 else 0.0
        flags[:, 1] = 1.0 if T0 + TOK < SEQ else 0.0
        m["ctxf"] = np.ascontiguousarray(np.concatenate([zero, ctx[b], zero], 0))
        m["ctxr"] = np.ascontiguousarray(np.concatenate([zero, ctx[b][::-1], zero], 0))
        dirs = [0, 1]
        segs = [(j, 0) for j in range(r)] + [(j, 1) for j in range(3, r, -1)]
        for k, (j, d) in enumerate(segs):
            s0 = j * TOK
            plo = xb[s0 - 1:s0] if s0 > 0 else zero
            phi = xb[s0 + TOK:s0 + TOK + 1] if s0 + TOK < SEQ else zero
            flo = 1.0 if s0 > 0 else 0.0
            fhi = 1.0 if s0 + TOK < SEQ else 0.0
            if d == 0:
                rows = np.concatenate([plo, xb[s0:s0 + TOK], phi], 0)
                flags[:, 6 + 2 * k] = flo; flags[:, 7 + 2 * k] = fhi
            else:
                rows = np.concatenate([phi, xb[s0:s0 + TOK][::-1], plo], 0)
                flags[:, 6 + 2 * k] = fhi; flags[:, 7 + 2 * k] = flo
            m["xo%d" % k] = np.ascontiguousarray(rows)
            dirs.append(d)
        flags[:, 12 + r] = 1.0
        m["flags"] = flags
        ps = [dir_params(d) for d in dirs]
        m["wg"] = np.ascontiguousarray(np.stack([q[0] for q in ps]))
        m["gb"] = np.ascontiguousarray(np.stack([q[1] for q in ps]))
        m["cwk"] = np.ascontiguousarray(np.stack([q[2] for q in ps]))
        cT = np.stack([f(c)[b], f(c_ctx)], axis=1)
        m["cT"] = np.ascontiguousarray(cT.reshape(8, 128, 2).transpose(1, 0, 2))
        maps.append(m)
    return maps


_NC_CACHE = {}


def kernel(**inputs):
    if "nc" not in _NC_CACHE:
        nc, p = build_nc()
        p.emit()
        _NC_CACHE["nc"] = nc
    nc = _NC_CACHE["nc"]
    maps = make_in_maps(**inputs)
    res = run_bass_kernel_spmd(nc, maps, core_ids=list(range(NCORE)))
    out = np.zeros((2, SEQ, D), np.float32)
    for core in range(NCORE):
        b, r = divmod(core, 4)
        out[b, r * TOK:(r + 1) * TOK] = res.results[core]["out"]
    return out
```
